# Optimizing a Trainium2 kernel written in Bass

```python
import math
import jax, jax.numpy as jnp
from jax import lax
import numpy as np

D_MODEL = 4096
BATCH = 4
SEQ = 4096
DEPTH = 4

N_MEM = 256
REC_WIDTH = 3 * D_MODEL // 4
XA_HEADS = 4
XA_HEAD_DIM = D_MODEL // (4 * XA_HEADS)
XA_WIDTH = XA_HEADS * XA_HEAD_DIM
IN_WIDTH = 2 * REC_WIDTH + 2 * XA_WIDTH
MIX_WIDTH = REC_WIDTH + XA_WIDTH
S5_GROUP = 16
S5_GROUPS = REC_WIDTH // S5_GROUP
S5_STATE = 64
S5_DT_MIN = 1e-3
S5_DT_MAX = 1e-1
LRU_BLOCK = 256
LRU_BLOCKS = REC_WIDTH // LRU_BLOCK
LRU_C = 8.0
CONV_W = 4
N_MIXERS = 2
N_A = (DEPTH + 1) // 2
N_B = DEPTH // 2
EPS = 1e-6

kernel_name = 'hybrid_s5_rglru_memxattn_sandwich'


def rmsnorm(x, g):
    xf = x.astype(jnp.float32)
    y = xf * lax.rsqrt(jnp.mean(xf * xf, axis=-1, keepdims=True) + EPS) * g.astype(jnp.float32)
    return y.astype(x.dtype)


def s5_mixer(u, lam_re, lam_im, log_step, b_re, b_im, c_re, c_im, d_skip, w_glu, b_glu):
    bsz, L, _ = u.shape
    f32 = jnp.float32
    uf = u.astype(f32).reshape(bsz, L, S5_GROUPS, S5_GROUP)
    lr = jnp.minimum(lam_re.astype(f32), -1e-4)
    li = lam_im.astype(f32)
    dt = jnp.exp(log_step.astype(f32))[:, None]
    mag = jnp.exp(lr * dt)
    ab_re = mag * jnp.cos(li * dt)
    ab_im = mag * jnp.sin(li * dt)
    den = lr * lr + li * li
    nr = ab_re - 1.0
    ni = ab_im
    coef_re = (nr * lr + ni * li) / den
    coef_im = (ni * lr - nr * li) / den
    br = b_re.astype(f32)
    bi = b_im.astype(f32)
    bb_re = coef_re[..., None] * br - coef_im[..., None] * bi
    bb_im = coef_re[..., None] * bi + coef_im[..., None] * br
    x_re = jnp.einsum('blgc,gpc->lbgp', uf, bb_re)
    x_im = jnp.einsum('blgc,gpc->lbgp', uf, bb_im)
    a_re = jnp.broadcast_to(ab_re, (L,) + ab_re.shape)
    a_im = jnp.broadcast_to(ab_im, (L,) + ab_im.shape)

    def combine(e1, e2):
        a1r, a1i, b1r, b1i = e1
        a2r, a2i, b2r, b2i = e2
        ar = a1r * a2r - a1i * a2i
        ai = a1r * a2i + a1i * a2r
        a2r_b = a2r[:, None]
        a2i_b = a2i[:, None]
        new_br = a2r_b * b1r - a2i_b * b1i + b2r
        new_bi = a2r_b * b1i + a2i_b * b1r + b2i
        return ar, ai, new_br, new_bi

    _, _, s_re, s_im = lax.associative_scan(combine, (a_re, a_im, x_re, x_im), axis=0)
    y = (jnp.einsum('lbgp,gcp->blgc', s_re, c_re.astype(f32))
         - jnp.einsum('lbgp,gcp->blgc', s_im, c_im.astype(f32)))
    y = y.reshape(bsz, L, REC_WIDTH) + d_skip.astype(f32) * uf.reshape(bsz, L, REC_WIDTH)
    g = jax.nn.gelu(y)
    out = g * jax.nn.sigmoid(g @ w_glu.astype(f32) + b_glu.astype(f32))
    return out.astype(u.dtype)


def rglru_mixer(u, conv_w, conv_b, w_a, b_a, w_x, b_x, lam):
    bsz, L, E = u.shape
    f32 = jnp.float32
    xc = lax.conv_general_dilated(
        u, conv_w.reshape(CONV_W, 1, E).astype(u.dtype),
        window_strides=(1,), padding=[(CONV_W - 1, 0)],
        dimension_numbers=('NWC', 'WIO', 'NWC'), feature_group_count=E) + conv_b
    xb = xc.reshape(bsz, L, LRU_BLOCKS, LRU_BLOCK)
    r = jax.nn.sigmoid((jnp.einsum('blhi,hij->blhj', xb, w_a) + b_a).astype(f32)).reshape(bsz, L, E)
    ig = jax.nn.sigmoid((jnp.einsum('blhi,hij->blhj', xb, w_x) + b_x).astype(f32)).reshape(bsz, L, E)
    log_a = -LRU_C * r * jax.nn.softplus(-lam.astype(f32))
    a = jnp.exp(log_a)
    mult = jnp.sqrt(-jnp.expm1(2.0 * log_a))
    mult = jnp.where((jnp.arange(L) == 0)[None, :, None], 1.0, mult)
    bt = mult * (ig * xc.astype(f32))

    def step(h, ab):
        a_t, b_t = ab
        h = a_t * h + b_t
        return h, h

    h0 = jnp.zeros((bsz, E), f32)
    _, hs = lax.scan(step, h0, (jnp.swapaxes(a, 0, 1), jnp.swapaxes(bt, 0, 1)))
    return jnp.swapaxes(hs, 0, 1).astype(u.dtype)


def memory_attention(q, mem_n, w_kv):
    bsz, L, _ = q.shape
    kv = mem_n @ w_kv
    k, v = jnp.split(kv, 2, axis=-1)
    qh = q.reshape(bsz, L, XA_HEADS, XA_HEAD_DIM)
    kh = k.reshape(bsz, N_MEM, XA_HEADS, XA_HEAD_DIM)
    vh = v.reshape(bsz, N_MEM, XA_HEADS, XA_HEAD_DIM)
    s = jnp.einsum('blhd,bnhd->bhln', qh, kh).astype(jnp.float32) * (XA_HEAD_DIM ** -0.5)
    p = jax.nn.softmax(s, axis=-1).astype(vh.dtype)
    o = jnp.einsum('bhln,bnhd->blhd', p, vh)
    return o.reshape(bsz, L, XA_WIDTH)


def setup_inputs(seed: int = 0) -> dict:
    key = jax.random.key(seed)
    ks = iter(jax.random.split(key, 40))
    f32 = jnp.float32

    def nrm(shape, scale):
        return scale * jax.random.normal(next(ks), shape, f32)

    x = nrm((BATCH, SEQ, D_MODEL), 1.0)
    mem = nrm((BATCH, N_MEM, D_MODEL), 1.0)
    w_in = nrm((DEPTH, D_MODEL, IN_WIDTH), D_MODEL ** -0.5)
    w_kv = nrm((DEPTH, D_MODEL, 2 * XA_WIDTH), D_MODEL ** -0.5)
    w_out = nrm((DEPTH, MIX_WIDTH, D_MODEL), MIX_WIDTH ** -0.5)
    pre_norm = 1.0 + nrm((DEPTH, D_MODEL), 0.02)
    post_norm = 1.0 + nrm((DEPTH, D_MODEL), 0.02)
    mem_norm = 1.0 + nrm((DEPTH, D_MODEL), 0.02)
    n_idx = jnp.arange(S5_STATE, dtype=f32)
    s5_lam_re = -0.5 + nrm((N_A, S5_GROUPS, S5_STATE), 0.01)
    s5_lam_im = math.pi * n_idx + nrm((N_A, S5_GROUPS, S5_STATE), 0.01)
    s5_log_step = jax.random.uniform(next(ks), (N_A, S5_GROUPS), f32,
                                     minval=math.log(S5_DT_MIN), maxval=math.log(S5_DT_MAX))
    s5_b_re = nrm((N_A, S5_GROUPS, S5_STATE, S5_GROUP), (2 * S5_GROUP) ** -0.5)
    s5_b_im = nrm((N_A, S5_GROUPS, S5_STATE, S5_GROUP), (2 * S5_GROUP) ** -0.5)
    s5_c_re = nrm((N_A, S5_GROUPS, S5_GROUP, S5_STATE), (2 * S5_STATE) ** -0.5)
    s5_c_im = nrm((N_A, S5_GROUPS, S5_GROUP, S5_STATE), (2 * S5_STATE) ** -0.5)
    s5_d = nrm((N_A, REC_WIDTH), 1.0)
    s5_w_glu = nrm((N_A, REC_WIDTH, REC_WIDTH), REC_WIDTH ** -0.5)
    s5_b_glu = nrm((N_A, REC_WIDTH), 0.01)
    lru_conv_w = nrm((N_B, CONV_W, REC_WIDTH), CONV_W ** -0.5)
    lru_conv_b = nrm((N_B, REC_WIDTH), 0.01)
    lru_w_a = nrm((N_B, LRU_BLOCKS, LRU_BLOCK, LRU_BLOCK), LRU_BLOCK ** -0.5)
    lru_b_a = nrm((N_B, LRU_BLOCKS, LRU_BLOCK), 0.01)
    lru_w_x = nrm((N_B, LRU_BLOCKS, LRU_BLOCK, LRU_BLOCK), LRU_BLOCK ** -0.5)
    lru_b_x = nrm((N_B, LRU_BLOCKS, LRU_BLOCK), 0.01)
    a_pow = jax.random.uniform(next(ks), (N_B, REC_WIDTH), f32, minval=0.9, maxval=0.999)
    a0 = a_pow ** (1.0 / LRU_C)
    lru_lam = jnp.log(a0) - jnp.log1p(-a0)
    return {'x': x, 'mem': mem, 'w_in': w_in, 'w_kv': w_kv, 'w_out': w_out,
            'pre_norm': pre_norm, 'post_norm': post_norm, 'mem_norm': mem_norm,
            's5_lam_re': s5_lam_re, 's5_lam_im': s5_lam_im, 's5_log_step': s5_log_step,
            's5_b_re': s5_b_re, 's5_b_im': s5_b_im, 's5_c_re': s5_c_re, 's5_c_im': s5_c_im,
            's5_d': s5_d, 's5_w_glu': s5_w_glu, 's5_b_glu': s5_b_glu,
            'lru_conv_w': lru_conv_w, 'lru_conv_b': lru_conv_b, 'lru_w_a': lru_w_a, 'lru_b_a': lru_b_a,
            'lru_w_x': lru_w_x, 'lru_b_x': lru_b_x, 'lru_lam': lru_lam}


def reference(x, mem, w_in, w_kv, w_out, pre_norm, post_norm, mem_norm,
              s5_lam_re, s5_lam_im, s5_log_step, s5_b_re, s5_b_im, s5_c_re, s5_c_im,
              s5_d, s5_w_glu, s5_b_glu,
              lru_conv_w, lru_conv_b, lru_w_a, lru_b_a, lru_w_x, lru_b_x, lru_lam):
    h = x
    for i in range(DEPTH):
        hn = rmsnorm(h, pre_norm[i])
        proj = hn @ w_in[i]
        u, gate, q, q_gate = jnp.split(
            proj, [REC_WIDTH, 2 * REC_WIDTH, 2 * REC_WIDTH + XA_WIDTH], axis=-1)
        if i % N_MIXERS == 0:
            j = i // N_MIXERS
            y = s5_mixer(u, s5_lam_re[j], s5_lam_im[j], s5_log_step[j], s5_b_re[j], s5_b_im[j],
                         s5_c_re[j], s5_c_im[j], s5_d[j], s5_w_glu[j], s5_b_glu[j])
        else:
            j = i // N_MIXERS
            y = rglru_mixer(u, lru_conv_w[j], lru_conv_b[j], lru_w_a[j], lru_b_a[j],
                            lru_w_x[j], lru_b_x[j], lru_lam[j])
        m = memory_attention(q, rmsnorm(mem, mem_norm[i]), w_kv[i])
        mixed = jnp.concatenate([y * jax.nn.silu(gate), m * jax.nn.silu(q_gate)], axis=-1)
        out = mixed @ w_out[i]
        h = h + rmsnorm(out, post_norm[i])
    return h
```

```python
import math
import numpy as np
import concourse.bass as bass
import concourse.mybir as mybir
from concourse.bass_utils import run_bass_kernel_spmd

F32 = mybir.dt.float32
BF16 = mybir.dt.bfloat16
AF = mybir.ActivationFunctionType
ALU = mybir.AluOpType

D = 4096
NMEM = 256
REC = 3072
TB = 256
T8 = 8
EPS = 1e-6
PI = math.pi


class Res:
    __slots__ = ("w", "rs")

    def __init__(self):
        self.w = None
        self.rs = []


class Op:
    __slots__ = ("eng", "fn", "deps", "is_dma", "chan", "seq", "signal")

    def __init__(self, eng, fn, is_dma=False, chan=None):
        self.eng = eng
        self.fn = fn
        self.deps = []
        self.is_dma = is_dma
        self.chan = chan
        self.seq = None
        self.signal = False


class KB:
    ENGS = ("pe", "act", "dve", "pool", "sp")

    def __init__(self, nc):
        self.nc = nc
        self.streams = {e: [] for e in self.ENGS}
        self.esem = {e: nc.alloc_semaphore(name="s_" + e) for e in self.ENGS}
        self.chans = {}
        self.rmap = {}
        self.fence = []

    def R(self, *key):
        r = self.rmap.get(key)
        if r is None:
            r = self.rmap[key] = Res()
        return r

    def chan(self, name):
        if name not in self.chans:
            self.chans[name] = [self.nc.alloc_semaphore(name="c_" + name), 0]
        return name

    def _track(self, op, reads, writes):
        seen = set()
        for r in reads:
            if r.w is not None and id(r.w) not in seen:
                seen.add(id(r.w))
                op.deps.append(r.w)
        for w in writes:
            if w.w is not None and id(w.w) not in seen:
                seen.add(id(w.w))
                op.deps.append(w.w)
            for d in w.rs:
                if id(d) not in seen and d is not op:
                    seen.add(id(d))
                    op.deps.append(d)
        for r in reads:
            r.rs.append(op)
        for w in writes:
            w.w = op
            w.rs = []

    def barrier(self):
        f = []
        for e in self.ENGS:
            for o in reversed(self.streams[e]):
                if not o.is_dma:
                    f.append(o)
                    break
        lastd = {}
        for e in self.ENGS:
            for o in self.streams[e]:
                if o.is_dma:
                    lastd[o.chan] = o
        f.extend(lastd.values())
        self.fence = f

    def op(self, eng, fn, reads=(), writes=()):
        o = Op(eng, fn)
        self._track(o, reads, writes)
        o.deps.extend(self.fence)
        self.streams[eng].append(o)
        return o

    def dma(self, queue, out, in_, reads=(), writes=(), chan=None):
        chan = self.chan(chan or ("q_" + queue))
        o = Op(queue, lambda e: e.dma_start(out=out, in_=in_), is_dma=True, chan=chan)
        self._track(o, reads, writes)
        o.deps.extend(self.fence)
        self.streams[queue].append(o)
        c = self.chans[chan]
        c[1] += 1
        o.seq = c[1]
        return o

    def emit(self, final_waits=()):
        nc = self.nc
        for e in self.ENGS:
            for o in self.streams[e]:
                for d in o.deps:
                    if not d.is_dma:
                        d.signal = True
        for e in self.ENGS:
            k = 0
            for o in self.streams[e]:
                if not o.is_dma and o.signal:
                    k += 1
                    o.seq = k
        kb = self

        def body(e, eng):
            waited = {}

            def wait_for(d):
                if d.is_dma:
                    sem, val, key = kb.chans[d.chan][0], 16 * d.seq, ("c", d.chan)
                else:
                    sem, val, key = kb.esem[d.eng], d.seq, ("e", d.eng)
                if waited.get(key, 0) >= val:
                    return
                waited[key] = val
                eng.wait_ge(sem, val)

            for o in kb.streams[e]:
                for d in o.deps:
                    wait_for(d)
                ins = o.fn(eng)
                if o.is_dma:
                    ins.then_inc(kb.chans[o.chan][0], 16)
                elif o.signal:
                    ins.then_inc(kb.esem[e], 1)
            if e == "sp":
                for d in final_waits:
                    wait_for(d)

        with nc.Block() as block:
            @block.tensor
            def _(eng):
                body("pe", eng)

            @block.scalar
            def _(eng):
                body("act", eng)

            @block.vector
            def _(eng):
                body("dve", eng)

            @block.gpsimd
            def _(eng):
                body("pool", eng)

            @block.sync
            def _(eng):
                body("sp", eng)


def build(nc, NT, layers=(0, 1, 2, 3), dbg=None):
    NB = NT // TB
    NCH = TB // T8
    kb = KB(nc)
    R = kb.R

    def din(name, shape, dt=F32):
        return nc.dram_tensor(name, list(shape), dt, kind="ExternalInput").ap()

    xT = din("xT", [D, NT])
    memT = din("memT", [D, NMEM])
    w_in = din("w_in", [4, 64, 128, 32 * 128])
    w_out = din("w_out", [4, 32, 128, 32 * 128])
    w_kv = din("w_kv", [4, 16, 128, 32 * 128])
    w_glu = din("w_glu", [2, 24, 128, 24 * 128])
    gains_d = din("gains", [128, 3 * 4 * 32])
    s5_l2 = din("s5_l2", [2, 128, 3 * 96])
    s5_bl2 = din("s5_bl2", [2, 128, 2 * 96 * 32])
    s5_cl2 = din("s5_cl2", [2, 128, 2 * 96 * 32])
    s5_l1 = din("s5_l1", [2, 128, 3 * 24 * 128])
    s5_bl1 = din("s5_bl1", [2, 128, 2 * 24 * 128])
    s5_dg = din("s5_dg", [2, 128, 2 * 24])
    lru_v = din("lru_v", [2, 128, 24 * 8])
    lru_w = din("lru_w", [2, 12, 128, 2 * 2 * 256])
    outT = nc.dram_tensor("outT", [D, NT], F32, kind="ExternalOutput").ap()
    o_scr = nc.dram_tensor("o_scr", [D, TB], F32, kind="Internal").ap()
    zmat_d = nc.dram_tensor("zmat", [24, 128, 16 * 128], BF16, kind="Internal").ap()
    camat_d = nc.dram_tensor("camat", [128, 2 * 8 * 96 * 32], BF16, kind="Internal").ap()
    kmat_d = nc.dram_tensor("kmat", [24, 128, 8 * 128], BF16, kind="Internal").ap()

    def sb(name, shape, dt=F32):
        return nc.alloc_sbuf_tensor("sb_" + name, list(shape), dt)

    ones_bf = sb("ones_bf", [128, 128], BF16)
    gains = sb("gains", [128, 3, 4, 32])
    hnT = sb("hnT", [128, 32, TB], BF16)
    mixed = sb("mixed", [128, 32, TB], BF16)
    u = sb("u", [128, 24, 3 + TB], BF16)
    gsil = sb("gsil", [128, 24, TB], BF16)
    qT = sb("qT", [128, 8, TB], BF16)
    gqsil = sb("gqsil", [128, 8, TB], BF16)
    NWS = 3
    wb = [sb("wb%d" % i, [128, 32, 128], BF16) for i in range(NWS)]
    NHS = 4
    hs = [sb("hs%d" % i, [128, TB]) for i in range(NHS)]
    sqb = [sb("sqb%d" % i, [128, TB], BF16) for i in range(2)]
    rstd = sb("rstd", [128, TB])
    rstd_m = sb("rstd_m", [128, NMEM])
    KT = sb("KT", [128, 8, NMEM], BF16)
    Vt = sb("Vt", [128, 2, 1024], BF16)
    expT = sb("expT", [128, 2, TB], BF16)
    rden = sb("rden", [128, TB])
    NTMP = 5
    tmpf = [sb("tmpf%d" % i, [128, TB]) for i in range(NTMP)]
    osb = [sb("osb%d" % i, [128, TB]) for i in range(2)]
    S = sb("S5S", [128, 192])
    AAB = sb("S5AAB", [128, 2, 192])
    sct = sb("s5sct", [128, 3, 192])
    dg = sb("s5dg", [128, 2, 24])
    hbias = sb("hbias", [128, 24])
    lv = sb("lruv", [128, 24, 8])
    negc = sb("negc", [128, 24])
    hst = sb("hst", [128, 24])
    AFN = 9216
    ABN = 22528
    arF = sb("arenaF", [128, AFN])
    arB = sb("arenaB", [128, ABN], BF16)

    def carve(ar, off, shape):
        n = 1
        for d_ in shape:
            n *= d_
        v = ar[:, off:off + n]
        if len(shape) == 2:
            return v.rearrange("p (a b) -> p a b", a=shape[0])
        if len(shape) == 3:
            return v.rearrange("p (a b c) -> p a b c", a=shape[0], b=shape[1])
        if len(shape) == 4:
            return v.rearrange("p (a b c d) -> p a b c d", a=shape[0], b=shape[1], c=shape[2])
        return v

    Zsb = carve(arF, 0, [NCH, 192])
    Shist = carve(arB, 0, [NCH, 192])
    gT = carve(arB, 6144, [24, TB])
    zmat = [carve(arB, 12288 + i * 2048, [16, 128]) for i in range(2)]
    camat = [carve(arB, 16384 + i * 2048, [2, 8, 4, 32]) for i in range(2)]
    kmat = [carve(arB, 20480 + i * 1024, [8, 128]) for i in range(2)]
    xc = carve(arF, 0, [24, TB])
    a4 = carve(arF, 6144, [4, TB])
    g4 = carve(arF, 7168, [4, TB])
    v4 = carve(arF, 8192, [4, TB])
    xcb = carve(arB, 0, [24, TB])
    gw = [carve(arB, 6144 + i * 1024, [2, 2, 256]) for i in range(2)]

    ps = [nc.alloc_psum_tensor("ps%d" % i, [128, 512], F32) for i in range(8)]
    PS_SS = 0

    state = {"mm": 0, "w": 0, "tmp": 0, "z": 0, "y": 0, "hs": 0}

    def mm_bank():
        state["mm"] = (state["mm"] + 1) % 3
        return 1 + state["mm"]

    def z_bank():
        state["z"] = (state["z"] + 1) % 2
        return 4 + state["z"]

    def y_bank():
        state["y"] = (state["y"] + 1) % 2
        return 6 + state["y"]

    def tmp():
        state["tmp"] = (state["tmp"] + 1) % NTMP
        i = state["tmp"]
        return tmpf[i], R("tmpf", i)

    def hs_slot():
        state["hs"] = (state["hs"] + 1) % NHS
        i = state["hs"]
        return i, hs[i], R("hs", i)

    def V_tt(out, in0, in1, op, r, w):
        kb.op("dve", lambda e: e.tensor_tensor(out=out, in0=in0, in1=in1, op=op), r, w)

    def V_ts(out, in0, s1, s2, op0, op1, r, w):
        if op1 is None:
            kb.op("dve", lambda e: e.tensor_scalar(out=out, in0=in0, scalar1=s1, scalar2=None, op0=op0), r, w)
        else:
            kb.op("dve", lambda e: e.tensor_scalar(out=out, in0=in0, scalar1=s1, scalar2=s2, op0=op0, op1=op1), r, w)

    def V_stt(out, in0, scalar, in1, op0, op1, r, w):
        kb.op("dve", lambda e: e.scalar_tensor_tensor(out=out, in0=in0, scalar=scalar, in1=in1, op0=op0, op1=op1), r, w)

    def V_copy(out, in_, r, w):
        kb.op("dve", lambda e: e.tensor_copy(out=out, in_=in_), r, w)

    def V_recip(out, in_, r, w):
        kb.op("dve", lambda e: e.reciprocal(out=out, in_=in_), r, w)

    def V_memset(out, val, r, w):
        kb.op("dve", lambda e: e.memset(out, val), r, w)

    def A_act(out, in_, func, r, w, bias=None, scale=None):
        kw = {}
        if bias is not None:
            kw["bias"] = bias
        if scale is not None:
            kw["scale"] = scale
        kb.op("act", lambda e: e.activation(out=out, in_=in_, func=func, **kw), r, w)

    def MM(out, lhsT, rhs, start, stop, r, w, tp=None):
        if tp is None:
            kb.op("pe", lambda e: e.matmul(out, lhsT=lhsT, rhs=rhs, start=start, stop=stop), r, w)
        else:
            kb.op("pe", lambda e: e.matmul(out, lhsT=lhsT, rhs=rhs, start=start, stop=stop, tile_position=tp), r, w)

    def load_w(src):
        i = state["w"] = (state["w"] + 1) % NWS
        n = src.shape[-1] // 128
        kb.dma("pool", wb[i][:, 0:n, :], src.rearrange("p (k c) -> p k c", c=128), writes=[R("wb", i)], chan="wb%d" % i)
        return wb[i], R("wb", i)

    def rsqrt_mean(dst, rdst, psap, rps):
        V_ts(dst, psap, 1.0 / D, EPS, ALU.mult, ALU.add, [rps], [rdst])
        A_act(dst, dst, AF.Sqrt, [rdst], [rdst])
        V_recip(dst, dst, [rdst], [rdst])

    V_memset(ones_bf[:], 1.0, (), [R("ones")])
    kb.dma("sp", gains[:], gains_d.rearrange("p (a l k) -> p a l k", a=3, l=4), writes=[R("gains")], chan="misc")

    def silu_from_psum(out_bf, psap, rps, wres):
        t, rt = tmp()
        A_act(t[:], psap, AF.Tanh, [rps], [rt], scale=0.5)
        V_ts(t[:], t[:], 0.5, 0.5, ALU.mult, ALU.add, [rt], [rt])
        V_tt(out_bf, psap, t[:], ALU.mult, [rps, rt], [wres])

    mem_state = {"rstd": False}

    def phase_kv(l):
        if not mem_state["rstd"]:
            for kt in range(32):
                si, ht, rh = hs_slot()
                kb.dma("sp", ht[:], memT[kt * 128:(kt + 1) * 128, :], writes=[rh], chan="hs%d" % si)
                i = kt % 2
                A_act(sqb[i][:], ht[:], AF.Square, [rh], [R("sqb", i)])
                MM(ps[PS_SS][:, 0:NMEM], ones_bf[:], sqb[i][:], kt == 0, kt == 31, [R("ones"), R("sqb", i)], [R("ps", PS_SS)])
            rsqrt_mean(rstd_m[:], R("rstd_m"), ps[PS_SS][:, 0:NMEM], R("ps", PS_SS))
            mem_state["rstd"] = True
        for kt in range(32):
            si, ht, rh = hs_slot()
            kb.dma("sp", ht[:], memT[kt * 128:(kt + 1) * 128, :], writes=[rh], chan="hs%d" % si)
            V_stt(hnT[:, kt, :], ht[:], gains[:, 2, l, kt:kt + 1], rstd_m[:], ALU.mult, ALU.mult,
                  [rh, R("gains"), R("rstd_m")], [R("hnT", kt)])
        for c in range(16):
            wt, rw = load_w(w_kv[l, c])
            b = mm_bank()
            if c < 8:
                for kt in range(32):
                    MM(ps[b][:, 0:NMEM], wt[:, kt, :], hnT[:, kt, :], kt == 0, kt == 31, [rw, R("hnT", kt)], [R("ps", b)])
                A_act(KT[:, c, :], ps[b][:, 0:NMEM], AF.Copy, [R("ps", b)], [R("KT")])
            else:
                cv = c - 8
                for j in range(2):
                    for kt in range(32):
                        MM(ps[b][:, j * 128:(j + 1) * 128], hnT[:, kt, j * 128:(j + 1) * 128], wt[:, kt, :], kt == 0, kt == 31,
                           [rw, R("hnT", kt)], [R("ps", b)])
                A_act(Vt[:, :, cv * 128:(cv + 1) * 128], ps[b][:, 0:256].rearrange("p (j c) -> p j c", j=2), AF.Copy,
                      [R("ps", b)], [R("Vt")])

    PN = 384
    p2 = [carve(arF, i * PN, [1, PN])[:, 0, :] for i in range(8)]
    pin = arF[:, 8 * PN:11 * PN]
    PSCR = 11 * PN

    def cplx_disc(n, lre, lim, ls):
        ar, ai, cre, cim, t0, t1, t2, t3 = [p2[i][:, 0:n] for i in range(8)]
        rr = [R("prep", i) for i in range(8)]
        rin = [R("prepin")]
        A_act(t0, ls, AF.Exp, rin, [rr[4]])
        V_ts(t1, lre, -1e-4, None, ALU.min, None, rin, [rr[5]])
        V_tt(t2, t1, t0, ALU.mult, [rr[4], rr[5]], [rr[6]])
        A_act(t2, t2, AF.Exp, [rr[6]], [rr[6]])
        V_tt(t3, lim, t0, ALU.mult, rin + [rr[4]], [rr[7]])
        A_act(ai, t3, AF.Sin, [rr[7]], [rr[1]], scale=0.125)
        A_act(ar, t3, AF.Sin, [rr[7]], [rr[0]], scale=-0.125, bias=PI / 2)
        for _ in range(3):
            V_tt(t3, ar, ai, ALU.mult, [rr[0], rr[1]], [rr[7]])
            V_tt(ar, ar, ar, ALU.mult, [rr[0]], [rr[0]])
            V_tt(ai, ai, ai, ALU.mult, [rr[1]], [rr[1]])
            V_tt(ar, ar, ai, ALU.subtract, [rr[0], rr[1]], [rr[0]])
            V_ts(ai, t3, 2.0, None, ALU.mult, None, [rr[7]], [rr[1]])
        V_tt(ar, ar, t2, ALU.mult, [rr[0], rr[6]], [rr[0]])
        V_tt(ai, ai, t2, ALU.mult, [rr[1], rr[6]], [rr[1]])
        V_tt(t0, t1, t1, ALU.mult, [rr[5]], [rr[4]])
        V_tt(t2, lim, lim, ALU.mult, rin, [rr[6]])
        V_tt(t0, t0, t2, ALU.add, [rr[4], rr[6]], [rr[4]])
        V_recip(t0, t0, [rr[4]], [rr[4]])
        V_ts(t2, ar, -1.0, None, ALU.add, None, [rr[0]], [rr[6]])
        V_tt(cre, t2, t1, ALU.mult, [rr[6], rr[5]], [rr[2]])
        V_tt(t3, ai, lim, ALU.mult, [rr[1]] + rin, [rr[7]])
        V_tt(cre, cre, t3, ALU.add, [rr[2], rr[7]], [rr[2]])
        V_tt(cre, cre, t0, ALU.mult, [rr[2], rr[4]], [rr[2]])
        V_tt(cim, ai, t1, ALU.mult, [rr[1], rr[5]], [rr[3]])
        V_tt(t3, t2, lim, ALU.mult, [rr[6]] + rin, [rr[7]])
        V_tt(cim, cim, t3, ALU.subtract, [rr[3], rr[7]], [rr[3]])
        V_tt(cim, cim, t0, ALU.mult, [rr[3], rr[4]], [rr[3]])
        return ar, ai, cre, cim

    def cmul(ore, oim, are, aim, bre, bim, t0, t1, r, w):
        V_tt(t0, are, bre, ALU.mult, r + w, w)
        V_tt(t1, aim, bim, ALU.mult, r + w, w)
        V_tt(t0, t0, t1, ALU.subtract, w, w)
        V_tt(t1, are, bim, ALU.mult, r + w, w)
        V_tt(oim, aim, bre, ALU.mult, r + w, w)
        V_tt(oim, oim, t1, ALU.add, w, w)
        V_copy(ore, t0, w, w)

    def s5_prep(j):
        RX = R("prepbig")
        RB = R("prepbf")
        RK = R("prepK")
        r03 = [R("prep", 0), R("prep", 1), R("prep", 2), R("prep", 3)]
        wp = [R("prep", 4), R("prep", 5), R("prep", 6), R("prep", 7)]
        n2 = 96
        kb.dma("sp", pin[:, 0:3 * n2], s5_l2[j], writes=[R("prepin")], chan="misc")
        ar, ai, cre, cim = cplx_disc(n2, pin[:, 0:n2], pin[:, n2:2 * n2], pin[:, 2 * n2:3 * n2])
        p8r, p8i, t6, t7 = p2[4][:, 0:n2], p2[5][:, 0:n2], p2[6][:, 0:n2], p2[7][:, 0:n2]
        V_copy(p8r, ar, r03, wp)
        V_copy(p8i, ai, r03, wp)
        for _ in range(3):
            cmul(p8r, p8i, p8r, p8i, p8r, p8i, t6, t7, [], wp)
        V_copy(AAB[:, 0, 0:96], p8r, wp, [R("AAB")])
        V_copy(AAB[:, 0, 96:192], p8r, wp, [R("AAB")])
        V_ts(AAB[:, 1, 0:96], p8i, -1.0, None, ALU.mult, None, wp, [R("AAB")])
        V_copy(AAB[:, 1, 96:192], p8i, wp, [R("AAB")])
        NQ = 24
        Bf = carve(arF, PSCR, [2, NQ, 32])
        Cf = carve(arF, PSCR + 1536, [2, NQ, 32])
        T0 = carve(arF, PSCR + 3072, [NQ, 32])
        T1 = carve(arF, PSCR + 3840, [NQ, 32])
        Bb = carve(arB, 0, [2, NQ, 32])
        CAb = carve(arB, 1536, [2, NQ, 32])
        Kst = carve(arB, 3072, [6, 128])
        bl2 = s5_bl2[j].rearrange("p (a b c) -> p a b c", a=2, b=96)
        cl2 = s5_cl2[j].rearrange("p (a b c) -> p a b c", a=2, b=96)
        km_v = kmat_d.rearrange("t p (l c) -> p t l c", l=8)
        for qt in range(4):
            kb.dma("sp", Bf, bl2[:, :, qt * NQ:(qt + 1) * NQ, :], writes=[RX], chan="misc")
            kb.dma("sp", Cf, cl2[:, :, qt * NQ:(qt + 1) * NQ, :], writes=[RX], chan="misc")

            def bc(x):
                return x[:, qt * NQ:(qt + 1) * NQ].unsqueeze(2).to_broadcast([128, NQ, 32])

            cmul(Bf[:, 0], Bf[:, 1], bc(cre), bc(cim), Bf[:, 0], Bf[:, 1], T0, T1, r03, [RX])
            V_copy(Bb[:, 0], Bf[:, 0], [RX], [RB])
            V_copy(Bb[:, 1], Bf[:, 1], [RX], [RB])
            for jj in range(9):
                V_copy(CAb[:, 0], Cf[:, 0], [RX], [RB])
                V_ts(CAb[:, 1], Cf[:, 1], -1.0, None, ALU.mult, None, [RX], [RB])
                if jj >= 1:
                    for ri in range(2):
                        o0 = ((ri * 8 + jj - 1) * 96 + qt * NQ) * 32
                        kb.dma("sp", camat_d[:, o0:o0 + NQ * 32], CAb[:, ri].rearrange("p a c -> p (a c)"), reads=[RB],
                               writes=[R("camat_d")], chan="prepst")
                if jj <= 7:
                    V_memset(Kst, 0.0, [RK], [RK])
                    for tg in range(2):
                        b = z_bank()
                        for tl in range(3):
                            for q in range(4):
                                pr = (tg * 3 + tl) * 4 + q
                                for ri in range(2):
                                    MM(ps[b][32 * q:32 * q + 32, tl * 128 + 32 * q: tl * 128 + 32 * q + 32],
                                       Bb[:, ri, pr, :], CAb[:, ri, pr, :], ri == 0, ri == 1, [RB], [R("ps", b)], tp=(0, 32 * q))
                        for q in range(4):
                            src = ps[b][32 * q:32 * q + 32, 0:384].rearrange("p (t c) -> p t c", t=3)[:, :, 32 * q:32 * q + 32]
                            dst = Kst[32 * q:32 * q + 32, tg * 3:tg * 3 + 3, 32 * q:32 * q + 32]
                            A_act(dst, src, AF.Copy, [R("ps", b)], [RK])
                    kb.dma("sp", km_v[:, qt * 6:(qt + 1) * 6, jj, :], Kst, reads=[RK], writes=[R("kmat_d")], chan="prepst")
                    cmul(Cf[:, 0], Cf[:, 1], bc(ar), bc(ai), Cf[:, 0], Cf[:, 1], T0, T1, r03, [RX])
        n1 = 3 * 128
        l1v = s5_l1[j].rearrange("p (a n) -> p a n", a=3)
        b1v = s5_bl1[j].rearrange("p (a n) -> p a n", a=2)
        zm_v = zmat_d.rearrange("t p (k c) -> p t k c", k=16)
        Wr = arF[:, PSCR:PSCR + n1]
        Wi = arF[:, PSCR + n1:PSCR + 2 * n1]
        U0 = arF[:, PSCR + 2 * n1:PSCR + 3 * n1]
        U1 = arF[:, PSCR + 3 * n1:PSCR + 4 * n1]
        Zst = carve(arB, 4096, [3, 16, 128])
        for ch in range(8):
            kb.dma("sp", pin[:, 0:3 * n1].rearrange("p (a n) -> p a n", a=3), l1v[:, :, ch * n1:(ch + 1) * n1],
                   writes=[R("prepin")], chan="misc")
            ar, ai, cre, cim = cplx_disc(n1, pin[:, 0:n1], pin[:, n1:2 * n1], pin[:, 2 * n1:3 * n1])
            kb.dma("sp", arF[:, PSCR:PSCR + 2 * n1].rearrange("p (a n) -> p a n", a=2), b1v[:, :, ch * n1:(ch + 1) * n1],
                   writes=[RX], chan="misc")
            cmul(Wr, Wi, cre, cim, Wr, Wi, U0, U1, r03, [RX])
            for k in range(7, -1, -1):
                V_copy(Zst[:, :, 2 * k, :], Wr.rearrange("p (t c) -> p t c", t=3), [RX], [RB])
                V_copy(Zst[:, :, 2 * k + 1, :], Wi.rearrange("p (t c) -> p t c", t=3), [RX], [RB])
                if k > 0:
                    cmul(Wr, Wi, ar, ai, Wr, Wi, U0, U1, r03, [RX])
            kb.dma("sp", zm_v[:, ch * 3:(ch + 1) * 3], Zst, reads=[RB], writes=[R("zmat_d")], chan="prepst")
        kb.dma("sp", dg[:], s5_dg[j].rearrange("p (a t) -> p a t", a=2), writes=[R("dg")], chan="misc")
        V_ts(hbias[:], dg[:, 1, :], 0.5, None, ALU.mult, None, [R("dg")], [R("hb")])
        V_memset(S[:], 0.0, [R("S")], [R("S")])

    def s5_block(j, blk):
        RZ = R("Zsb")
        RS = R("S")
        RH = R("Shist")
        RT = R("sct")
        for t in range(24):
            i = t % 2
            kb.dma("sp", zmat[i], zmat_d[t].rearrange("p (k c) -> p k c", k=16), reads=[R("zmat_d")], writes=[R("zmat", i)],
                   chan="zm%d" % i)
            b = z_bank()
            uv = u[:, t, 3:3 + TB].rearrange("p (n k) -> p k n", k=T8)
            for q in range(4):
                for ri in range(2):
                    for k in range(T8):
                        MM(ps[b][:, (ri * 4 + q) * NCH:(ri * 4 + q + 1) * NCH], zmat[i][32 * q:32 * q + 32, 2 * k + ri, :],
                           uv[32 * q:32 * q + 32, k, :], k == 0, k == T8 - 1, [R("zmat", i), R("u", t)], [R("ps", b)], tp=(32 * q, 0))
            dst = Zsb.rearrange("p n (r g) -> p n r g", r=2)[:, :, :, 4 * t:4 * t + 4]
            src = ps[b][:, 0:8 * NCH].rearrange("p (r q n) -> p n r q", r=2, q=4)
            A_act(dst, src, AF.Copy, [R("ps", b)], [RZ])
        for n in range(NCH):
            A_act(Shist[:, n, :], S[:], AF.Copy, [RS], [RH])
            V_tt(sct[:, 0, :], AAB[:, 0, :], S[:], ALU.mult, [RS, R("AAB")], [RT])
            V_tt(sct[:, 1, 0:96], AAB[:, 1, 0:96], S[:, 96:192], ALU.mult, [RS, R("AAB")], [RT])
            V_tt(sct[:, 1, 96:192], AAB[:, 1, 96:192], S[:, 0:96], ALU.mult, [RS, R("AAB")], [RT])
            V_tt(sct[:, 0, :], sct[:, 0, :], sct[:, 1, :], ALU.add, [RT], [RT])
            V_tt(S[:], sct[:, 0, :], Zsb[:, n, :], ALU.add, [RT, RZ], [RS])
        for t in range(24):
            i = t % 2
            kb.dma("sp", camat[i].rearrange("p r k q c -> p (r k) (q c)"),
                   camat_d.rearrange("p (rk pr c) -> p rk pr c", rk=16, pr=96)[:, :, 4 * t:4 * t + 4, :].rearrange("p a q c -> p a (q c)"),
                   reads=[R("camat_d")], writes=[R("camat", i)], chan="cm%d" % i)
            kb.dma("sp", kmat[i], kmat_d[t].rearrange("p (l c) -> p l c", l=8), reads=[R("kmat_d")], writes=[R("kmat", i)],
                   chan="km%d" % i)
            b = y_bank()
            uv = u[:, t, 3:3 + TB].rearrange("p (n k) -> p k n", k=T8)
            for k in range(T8):
                yo = ps[b][:, k * NCH:(k + 1) * NCH]
                for l in range(k + 1):
                    MM(yo, kmat[i][:, l, :], uv[:, k - l, :], l == 0, False, [R("kmat", i), R("u", t)], [R("ps", b)])
                for q in range(4):
                    for ri in range(2):
                        MM(ps[b][32 * q:32 * q + 32, k * NCH:(k + 1) * NCH], camat[i][:, ri, k, q, :],
                           Shist[:, :, ri * 96 + 4 * t + q], False, (ri == 1), [R("camat", i), RH], [R("ps", b)], tp=(0, 32 * q))
            yt, ry = tmp()
            V_stt(yt[:].rearrange("p (n k) -> p n k", k=T8), u[:, t, 3:3 + TB].rearrange("p (n k) -> p n k", k=T8), dg[:, 0, t:t + 1],
                  ps[b][:, 0:TB].rearrange("p (k n) -> p n k", k=T8), ALU.mult, ALU.add, [R("u", t), R("dg"), R("ps", b)], [ry])
            t2, r2 = tmp()
            V_tt(t2[:], yt[:], yt[:], ALU.mult, [ry], [r2])
            V_ts(t2[:], t2[:], 0.044715, 1.0, ALU.mult, ALU.add, [r2], [r2])
            V_tt(t2[:], t2[:], yt[:], ALU.mult, [r2, ry], [r2])
            A_act(t2[:], t2[:], AF.Tanh, [r2], [r2], scale=math.sqrt(2.0 / PI))
            V_ts(t2[:], t2[:], 0.5, 0.5, ALU.mult, ALU.add, [r2], [r2])
            V_tt(gT[:, t, :], t2[:], yt[:], ALU.mult, [r2, ry], [R("gT", t)])
        for c in range(24):
            wt, rw = load_w(w_glu[j, c])
            b = mm_bank()
            for kt in range(24):
                MM(ps[b][:, 0:TB], wt[:, kt, :], gT[:, kt, :], kt == 0, kt == 23, [rw, R("gT", kt)], [R("ps", b)])
            t2, r2 = tmp()
            A_act(t2[:], ps[b][:, 0:TB], AF.Tanh, [R("ps", b), R("hb")], [r2], scale=0.5, bias=hbias[:, c:c + 1])
            V_ts(t2[:], t2[:], 0.5, 0.5, ALU.mult, ALU.add, [r2], [r2])
            V_tt(t2[:], t2[:], gT[:, c, :], ALU.mult, [r2, R("gT", c)], [r2])
            V_tt(mixed[:, c, :], t2[:], gsil[:, c, :], ALU.mult, [r2, R("gsil", c)], [R("mixed", c)])

    def lru_prep(j):
        kb.dma("sp", lv[:], lru_v[j].rearrange("p (t k) -> p t k", k=8), writes=[R("lv")], chan="misc")
        A_act(negc[:], lv[:, :, 7], AF.Exp, [R("lv")], [R("negc")], scale=-1.0)
        A_act(negc[:], negc[:], AF.Ln, [R("negc")], [R("negc")], bias=1.0)
        V_ts(negc[:], negc[:], -8.0, None, ALU.mult, None, [R("negc")], [R("negc")])
        V_ts(lv[:, :, 5:7], lv[:, :, 5:7], 0.5, None, ALU.mult, None, [R("lv")], [R("lv")])
        V_memset(hst[:], 0.0, [R("hst")], [R("hst")])
        V_memset(u[:, :, 0:3], 0.0, [R("utail")], [R("utail")])

    def lru_block(j, blk):
        for t in range(24):
            ru = [R("u", t), R("utail"), R("lv")]
            V_ts(xc[:, t, :], u[:, t, 3:3 + TB], lv[:, t, 3:4], lv[:, t, 4:5], ALU.mult, ALU.add, ru, [R("xc", t)])
            for jj in range(3):
                V_stt(xc[:, t, :], u[:, t, jj:jj + TB], lv[:, t, jj:jj + 1], xc[:, t, :], ALU.mult, ALU.add, ru + [R("xc", t)], [R("xc", t)])
            A_act(xcb[:, t, :], xc[:, t, :], AF.Copy, [R("xc", t)], [R("xcb", t)])
        V_copy(u[:, :, 0:3], u[:, :, TB:TB + 3], [R("u", t) for t in range(24)] + [R("xc", t) for t in range(24)], [R("utail")])
        for h2 in range(6):
            R4 = R("lru4")
            for hh in range(2):
                h = 2 * h2 + hh
                i = h % 2
                kb.dma("pool", gw[i], lru_w[j, h].rearrange("p (a i j) -> p a i j", a=2, i=2), writes=[R("gw", i)], chan="gw%d" % i)
                for j2 in range(2):
                    t = 2 * h + j2
                    k4 = 2 * hh + j2
                    b = z_bank()
                    for ax in range(2):
                        for i2 in range(2):
                            MM(ps[b][:, ax * TB:(ax + 1) * TB], gw[i][:, ax, i2, j2 * 128:(j2 + 1) * 128], xcb[:, 2 * h + i2, :],
                               i2 == 0, i2 == 1, [R("gw", i), R("xcb", 2 * h + i2)], [R("ps", b)])
                    r_, rr_ = tmp()
                    A_act(r_[:], ps[b][:, 0:TB], AF.Tanh, [R("ps", b), R("lv")], [rr_], scale=0.5, bias=lv[:, t, 5:6])
                    A_act(g4[:, k4, :], ps[b][:, TB:2 * TB], AF.Tanh, [R("ps", b), R("lv")], [R4], scale=0.5, bias=lv[:, t, 6:7])
                    V_ts(r_[:], r_[:], 0.5, 0.5, ALU.mult, ALU.add, [rr_], [rr_])
                    V_ts(g4[:, k4, :], g4[:, k4, :], 0.5, 0.5, ALU.mult, ALU.add, [R4], [R4])
                    A_act(a4[:, k4, :], r_[:], AF.Exp, [rr_, R("negc")], [R4], scale=negc[:, t:t + 1])
                    V_tt(v4[:, k4, :], a4[:, k4, :], a4[:, k4, :], ALU.mult, [R4], [R4])
                    V_ts(v4[:, k4, :], v4[:, k4, :], -1.0, 1.0, ALU.mult, ALU.add, [R4], [R4])
                    V_tt(g4[:, k4, :], g4[:, k4, :], xc[:, t, :], ALU.mult, [R4, R("xc", t)], [R4])
            A_act(v4, v4, AF.Sqrt, [R4], [R4])
            if blk == 0:
                V_memset(v4[:, :, 0:1], 1.0, [R4], [R4])
            V_tt(g4, g4, v4, ALU.mult, [R4], [R4])
            for k4 in range(4):
                t = 4 * h2 + k4
                kb.op("dve", lambda e, k4=k4, t=t: e.tensor_tensor_scan(out=v4[:, k4, :], data0=a4[:, k4, :], data1=g4[:, k4, :],
                                                                         initial=hst[:, t:t + 1], op0=ALU.mult, op1=ALU.add),
                      [R4, R("hst")], [R4])
                V_copy(hst[:, t:t + 1], v4[:, k4, TB - 1:TB], [R4], [R("hst")])
                V_tt(mixed[:, t, :], v4[:, k4, :], gsil[:, t, :], ALU.mult, [R4, R("gsil", t)], [R("mixed", t)])

    last_stores = []
    V_memset(u[:, :, 0:3], 0.0, (), [R("utail")])
    for li, l in enumerate(layers):
        is_s5 = (l % 2 == 0)
        j = l // 2
        src = xT if li == 0 else outT
        kb.barrier()
        phase_kv(l)
        if is_s5:
            s5_prep(j)
        else:
            lru_prep(j)
        kb.barrier()
        for blk in range(NB):
            tok = slice(blk * TB, (blk + 1) * TB)
            for kt in range(32):
                rsrc = [] if li == 0 else [R("outT", blk, kt)]
                si, ht, rh = hs_slot()
                kb.dma("sp", ht[:], src[kt * 128:(kt + 1) * 128, tok], reads=rsrc, writes=[rh], chan="hs%d" % si)
                i = kt % 2
                A_act(sqb[i][:], ht[:], AF.Square, [rh], [R("sqb", i)])
                MM(ps[PS_SS][:, 0:TB], ones_bf[:], sqb[i][:], kt == 0, kt == 31, [R("ones"), R("sqb", i)], [R("ps", PS_SS)])
            rsqrt_mean(rstd[:], R("rstd"), ps[PS_SS][:, 0:TB], R("ps", PS_SS))
            for kt in range(32):
                rsrc = [] if li == 0 else [R("outT", blk, kt)]
                si, ht, rh = hs_slot()
                kb.dma("sp", ht[:], src[kt * 128:(kt + 1) * 128, tok], reads=rsrc, writes=[rh], chan="hs%d" % si)
                V_stt(hnT[:, kt, :], ht[:], gains[:, 0, l, kt:kt + 1], rstd[:], ALU.mult, ALU.mult,
                      [rh, R("gains"), R("rstd")], [R("hnT", kt)])
            for c in range(64):
                wt, rw = load_w(w_in[l, c])
                b = mm_bank()
                for kt in range(32):
                    MM(ps[b][:, 0:TB], wt[:, kt, :], hnT[:, kt, :], kt == 0, kt == 31, [rw, R("hnT", kt)], [R("ps", b)])
                if c < 24:
                    A_act(u[:, c, 3:3 + TB], ps[b][:, 0:TB], AF.Copy, [R("ps", b)], [R("u", c)])
                elif c < 48:
                    silu_from_psum(gsil[:, c - 24, :], ps[b][:, 0:TB], R("ps", b), R("gsil", c - 24))
                elif c < 56:
                    A_act(qT[:, c - 48, :], ps[b][:, 0:TB], AF.Copy, [R("ps", b)], [R("qT", c - 48)])
                else:
                    silu_from_psum(gqsil[:, c - 56, :], ps[b][:, 0:TB], R("ps", b), R("gqsil", c - 56))
            if is_s5:
                s5_block(j, blk)
            else:
                lru_block(j, blk)
            for hd in range(4):
                for jn in range(2):
                    b = z_bank()
                    for dt_ in range(2):
                        MM(ps[b][:, 0:TB], KT[:, 2 * hd + dt_, jn * 128:(jn + 1) * 128], qT[:, 2 * hd + dt_, :], dt_ == 0, dt_ == 1,
                           [R("KT"), R("qT", 2 * hd + dt_)], [R("ps", b)])
                    A_act(expT[:, jn, :], ps[b][:, 0:TB], AF.Exp, [R("ps", b)], [R("expT", jn)], scale=1.0 / 16.0)
                for jn in range(2):
                    MM(ps[PS_SS][:, 0:TB], ones_bf[:], expT[:, jn, :], jn == 0, jn == 1, [R("ones"), R("expT", jn)], [R("ps", PS_SS)])
                V_recip(rden[:], ps[PS_SS][:, 0:TB], [R("ps", PS_SS)], [R("rden")])
                for dt_ in range(2):
                    b = y_bank()
                    c = 2 * hd + dt_
                    for jn in range(2):
                        MM(ps[b][:, 0:TB], Vt[:, jn, c * 128:(c + 1) * 128], expT[:, jn, :], jn == 0, jn == 1,
                           [R("Vt"), R("expT", jn)], [R("ps", b)])
                    t2, r2 = tmp()
                    V_tt(t2[:], ps[b][:, 0:TB], rden[:], ALU.mult, [R("ps", b), R("rden")], [r2])
                    V_tt(mixed[:, 24 + c, :], t2[:], gqsil[:, c, :], ALU.mult, [r2, R("gqsil", c)], [R("mixed", 24 + c)])
            for c in range(32):
                wt, rw = load_w(w_out[l, c])
                b = mm_bank()
                for kt in range(32):
                    MM(ps[b][:, 0:TB], wt[:, kt, :], mixed[:, kt, :], kt == 0, kt == 31, [rw, R("mixed", kt)], [R("ps", b)])
                i = c % 2
                A_act(osb[i][:], ps[b][:, 0:TB], AF.Copy, [R("ps", b)], [R("osb", i)])
                A_act(sqb[i][:], ps[b][:, 0:TB], AF.Square, [R("ps", b)], [R("sqb", i)])
                MM(ps[PS_SS][:, 0:TB], ones_bf[:], sqb[i][:], c == 0, c == 31, [R("ones"), R("sqb", i)], [R("ps", PS_SS)])
                kb.dma("sp", o_scr[c * 128:(c + 1) * 128, :], osb[i][:], reads=[R("osb", i)], writes=[R("o_scr", c)], chan="ost%d" % i)
            rsqrt_mean(rstd[:], R("rstd"), ps[PS_SS][:, 0:TB], R("ps", PS_SS))
            for c in range(32):
                i = c % 2
                rsrc = [] if li == 0 else [R("outT", blk, c)]
                si, ht, rh = hs_slot()
                kb.dma("sp", ht[:], src[c * 128:(c + 1) * 128, tok], reads=rsrc, writes=[rh], chan="hs%d" % si)
                kb.dma("sp", osb[i][:], o_scr[c * 128:(c + 1) * 128, :], reads=[R("o_scr", c)], writes=[R("osb", i)], chan="old%d" % i)
                V_stt(osb[i][:], osb[i][:], gains[:, 1, l, c:c + 1], rstd[:], ALU.mult, ALU.mult, [R("osb", i), R("gains"), R("rstd")],
                      [R("osb", i)])
                V_tt(ht[:], ht[:], osb[i][:], ALU.add, [rh, R("osb", i)], [rh])
                st = kb.dma("sp", outT[c * 128:(c + 1) * 128, tok], ht[:], reads=[rh], writes=[R("outT", blk, c)], chan="hst")
                if li == len(layers) - 1:
                    last_stores.append(st)
    kb.emit(final_waits=last_stores)
    return nc


def _panel(w, ktn):
    K, C = w.shape
    a = w.reshape(ktn, 128, C // 128, 128)
    return np.ascontiguousarray(a.transpose(2, 1, 0, 3)).reshape(C // 128, 128, ktn * 128)


def _chan_major(v):
    n = v.shape[-1] // 128
    a = v.reshape(v.shape[:-1] + (n, 128))
    return np.ascontiguousarray(np.moveaxis(a, -1, 0))


def prep_shared(inp):
    f = np.float32
    sh = {}
    sh["w_in"] = np.stack([_panel(np.asarray(inp["w_in"][l], f), 32) for l in range(4)])
    sh["w_out"] = np.stack([_panel(np.asarray(inp["w_out"][l], f), 32) for l in range(4)])
    sh["w_kv"] = np.stack([_panel(np.asarray(inp["w_kv"][l], f), 32) for l in range(4)])
    sh["w_glu"] = np.stack([_panel(np.asarray(inp["s5_w_glu"][j], f), 24) for j in range(2)])
    g = np.stack([_chan_major(np.asarray(inp[k], f)) for k in ("pre_norm", "post_norm", "mem_norm")], axis=1)
    sh["gains"] = np.ascontiguousarray(g).reshape(128, 3 * 4 * 32)
    lre = np.asarray(inp["s5_lam_re"], f)
    lim = np.asarray(inp["s5_lam_im"], f)
    ls = np.asarray(inp["s5_log_step"], f)
    bre = np.asarray(inp["s5_b_re"], f)
    bim = np.asarray(inp["s5_b_im"], f)
    cre = np.asarray(inp["s5_c_re"], f)
    cim = np.asarray(inp["s5_c_im"], f)

    def l2(v):
        return v.reshape(96, 2, 64).transpose(1, 2, 0).reshape(128, 96)

    def l1(v):
        a = v.reshape(24, 4, 2, 64)
        a = np.broadcast_to(a[:, :, None, :, :], (24, 4, 32, 2, 64))
        return np.ascontiguousarray(a.transpose(1, 2, 0, 3, 4)).reshape(128, 24 * 128)

    s5_l2, s5_bl2, s5_cl2, s5_l1, s5_bl1, s5_dg = [], [], [], [], [], []
    for j in range(2):
        lsb = np.broadcast_to(ls[j][:, None], (192, 64))
        s5_l2.append(np.stack([l2(lre[j]), l2(lim[j]), l2(lsb)], axis=1).reshape(128, 3 * 96))
        s5_l1.append(np.stack([l1(lre[j]), l1(lim[j]), l1(lsb)], axis=1).reshape(128, 3 * 24 * 128))

        def bl2(b):
            o = np.zeros((2, 64, 96, 2, 16), f)
            bb = b.reshape(96, 2, 64, 16)
            for g2 in range(2):
                o[g2, :, :, g2, :] = bb[:, g2].transpose(1, 0, 2)
            return o.reshape(128, 96 * 32)

        def cl2(c):
            return bl2(np.ascontiguousarray(c.transpose(0, 2, 1)))

        def bl1(b):
            o = np.zeros((4, 2, 16, 24, 2, 64), f)
            bb = b.reshape(24, 4, 2, 64, 16)
            for g2 in range(2):
                o[:, g2, :, :, g2, :] = bb[:, :, g2].transpose(1, 3, 0, 2)
            return o.reshape(128, 24 * 128)

        s5_bl2.append(np.stack([bl2(bre[j]), bl2(bim[j])], axis=1).reshape(128, -1))
        s5_cl2.append(np.stack([cl2(cre[j]), cl2(cim[j])], axis=1).reshape(128, -1))
        s5_bl1.append(np.stack([bl1(bre[j]), bl1(bim[j])], axis=1).reshape(128, -1))
        s5_dg.append(np.stack([_chan_major(np.asarray(inp["s5_d"][j], f)), _chan_major(np.asarray(inp["s5_b_glu"][j], f))],
                              axis=1).reshape(128, 48))
    sh["s5_l2"] = np.stack(s5_l2)
    sh["s5_bl2"] = np.stack(s5_bl2)
    sh["s5_cl2"] = np.stack(s5_cl2)
    sh["s5_l1"] = np.stack(s5_l1)
    sh["s5_bl1"] = np.stack(s5_bl1)
    sh["s5_dg"] = np.stack(s5_dg)
    lv, lw = [], []
    for j in range(2):
        cw = np.asarray(inp["lru_conv_w"][j], f)
        cols = [cw[0], cw[1], cw[2], cw[3], np.asarray(inp["lru_conv_b"][j], f),
                np.asarray(inp["lru_b_a"][j], f).reshape(-1), np.asarray(inp["lru_b_x"][j], f).reshape(-1),
                np.asarray(inp["lru_lam"][j], f)]
        v = np.stack([_chan_major(c) for c in cols], axis=2)
        lv.append(v.reshape(128, 24 * 8))
        wa = np.asarray(inp["lru_w_a"][j], f).reshape(12, 2, 128, 256)
        wx = np.asarray(inp["lru_w_x"][j], f).reshape(12, 2, 128, 256)
        w = np.stack([wa, wx], axis=1)
        lw.append(np.ascontiguousarray(w.transpose(0, 3, 1, 2, 4)).reshape(12, 128, 2 * 2 * 256))
    sh["lru_v"] = np.stack(lv)
    sh["lru_w"] = np.stack(lw)
    return {k: np.ascontiguousarray(v, dtype=f) for k, v in sh.items()}


def kernel(**inputs):
    x = np.asarray(inputs["x"], np.float32)
    mem = np.asarray(inputs["mem"], np.float32)
    B, L, _ = x.shape
    sh = prep_shared(inputs)
    nc = bass.Bass("TRN2", target_bir_lowering=False)
    build(nc, L)
    in_maps = []
    for b in range(B):
        m = dict(sh)
        m["xT"] = np.ascontiguousarray(x[b].T)
        m["memT"] = np.ascontiguousarray(mem[b].T)
        in_maps.append(m)
    res = run_bass_kernel_spmd(nc, in_maps, core_ids=list(range(B)))
    out = np.stack([np.ascontiguousarray(res.results[b]["outT"].T) for b in range(B)])
    return out.astype(np.float32)
```

```python
import math
import os
import numpy as np
import concourse.bass as bass
import concourse.mybir as mybir
from concourse.bass_utils import run_bass_kernel_spmd

F32 = mybir.dt.float32
BF16 = mybir.dt.bfloat16
AF = mybir.ActivationFunctionType
ALU = mybir.AluOpType

D = 4096
NMEM = 256
REC = 3072
TB = 256
T8 = 8
EPS = 1e-6
PI = math.pi
PESKIP = os.environ.get('K_PESKIP', '1') == '1'
WCACHE = os.environ.get('K_WCACHE', '1') == '1'
ZSYNC = int(os.environ.get('K_ZSYNC', '1'))
YSYNC = int(os.environ.get('K_YSYNC', '2'))


class Res:
    __slots__ = ("w", "rs")

    def __init__(self):
        self.w = None
        self.rs = {}


class Op:
    __slots__ = ("eng", "fn", "deps", "is_dma", "chan", "seq", "signal", "g", "pesync")
    _n = 0

    def __init__(self, eng, fn, is_dma=False, chan=None):
        Op._n += 1
        self.g = Op._n
        self.pesync = False
        self.eng = eng
        self.fn = fn
        self.deps = []
        self.is_dma = is_dma
        self.chan = chan
        self.seq = None
        self.signal = False


class KB:
    ENGS = ("pe", "act", "dve", "pool", "sp")

    def __init__(self, nc):
        self.nc = nc
        self.streams = {e: [] for e in self.ENGS}
        self.esem = {e: nc.alloc_semaphore(name="s_" + e) for e in self.ENGS}
        self.chans = {}
        self.rmap = {}
        self.fence = []

    def R(self, *key):
        r = self.rmap.get(key)
        if r is None:
            r = self.rmap[key] = Res()
        return r

    def chan(self, name):
        if name not in self.chans:
            self.chans[name] = [self.nc.alloc_semaphore(name="c_" + name), 0]
        return name

    def _track(self, op, reads, writes):
        seen = set()
        for r in reads:
            if r.w is not None and id(r.w) not in seen:
                seen.add(id(r.w))
                op.deps.append(r.w)
        for w in writes:
            if w.w is not None and id(w.w) not in seen:
                seen.add(id(w.w))
                op.deps.append(w.w)
            for d in w.rs.values():
                if id(d) not in seen and d is not op:
                    seen.add(id(d))
                    op.deps.append(d)
        for r in reads:
            r.rs[(op.eng, op.chan)] = op
        for w in writes:
            w.w = op
            w.rs = {}

    def barrier(self):
        f = []
        for e in self.ENGS:
            for o in reversed(self.streams[e]):
                if not o.is_dma:
                    f.append(o)
                    break
        lastd = {}
        for e in self.ENGS:
            for o in self.streams[e]:
                if o.is_dma:
                    lastd[o.chan] = o
        f.extend(lastd.values())
        self.fence = f

    def op(self, eng, fn, reads=(), writes=()):
        o = Op(eng, fn)
        self._track(o, reads, writes)
        o.deps.extend(self.fence)
        self.streams[eng].append(o)
        return o

    def dma(self, queue, out, in_, reads=(), writes=(), chan=None):
        chan = self.chan(chan or ("q_" + queue))
        o = Op(queue, lambda e: e.dma_start(out=out, in_=in_), is_dma=True, chan=chan)
        self._track(o, reads, writes)
        o.deps.extend(self.fence)
        self.streams[queue].append(o)
        c = self.chans[chan]
        c[1] += 1
        o.seq = c[1]
        return o

    def emit(self, final_waits=()):
        nc = self.nc
        for e in self.ENGS:
            for o in self.streams[e]:
                for d in o.deps:
                    if not d.is_dma and not (PESKIP and e == "pe" and d.eng == "pe" and not o.pesync):
                        d.signal = True
        for e in self.ENGS:
            k = 0
            for o in self.streams[e]:
                if not o.is_dma and o.signal:
                    k += 1
                    o.seq = k
        kb = self

        def body(e, eng):
            waited = {}

            def dkey(d):
                if d.is_dma:
                    return ("c", d.chan), 16 * d.seq
                return ("e", d.eng), d.seq

            def wait_for(d):
                if d.is_dma:
                    sem, val, key = kb.chans[d.chan][0], 16 * d.seq, ("c", d.chan)
                else:
                    sem, val, key = kb.esem[d.eng], d.seq, ("e", d.eng)
                if waited.get(key, 0) >= val:
                    return
                waited[key] = val
                eng.wait_ge(sem, val)

            ops = kb.streams[e]
            LOOK = int(os.environ.get('K_LOOK', '80'))
            for idx, o in enumerate(ops):
                need = {}
                for d in o.deps:
                    if PESKIP and e == "pe" and d.eng == "pe" and not d.is_dma and not o.pesync:
                        continue
                    key, val = dkey(d)
                    if waited.get(key, 0) < val and need.get(key, (0, None))[0] < val:
                        need[key] = (val, d)
                if need:
                    for o2 in ops[idx + 1: idx + 1 + LOOK]:
                        for d in o2.deps:
                            if PESKIP and e == "pe" and d.eng == "pe" and not d.is_dma and not o2.pesync:
                                continue
                            key, val = dkey(d)
                            if key in need and d.g < o.g and val > need[key][0]:
                                need[key] = (val, d)
                    for key, (val, d) in need.items():
                        wait_for(d)
                ins = o.fn(eng)
                if o.is_dma:
                    ins.then_inc(kb.chans[o.chan][0], 16)
                elif o.signal:
                    ins.then_inc(kb.esem[e], 1)
            if e == "sp":
                for d in final_waits:
                    wait_for(d)

        with nc.Block() as block:
            @block.tensor
            def _(eng):
                body("pe", eng)

            @block.scalar
            def _(eng):
                body("act", eng)

            @block.vector
            def _(eng):
                body("dve", eng)

            @block.gpsimd
            def _(eng):
                body("pool", eng)

            @block.sync
            def _(eng):
                body("sp", eng)


def build(nc, NT, layers=(0, 1, 2, 3), dbg=None):
    NB = NT // TB
    NCH = TB // T8
    kb = KB(nc)
    R = kb.R

    def din(name, shape, dt=F32):
        return nc.dram_tensor(name, list(shape), dt, kind="ExternalInput").ap()

    xT = din("xT", [D, NT])
    memT = din("memT", [D, NMEM])
    w_in = din("w_in", [4, 64, 128, 32 * 128])
    w_out = din("w_out", [4, 32, 128, 32 * 128])
    w_kv = din("w_kv", [4, 16, 128, 32 * 128])
    w_glu = din("w_glu", [2, 24, 128, 24 * 128])
    gains_d = din("gains", [128, 3 * 4 * 32])
    s5_l2 = din("s5_l2", [2, 128, 3 * 96])
    s5_bl2 = din("s5_bl2", [2, 128, 2 * 96 * 32])
    s5_cl2 = din("s5_cl2", [2, 128, 2 * 96 * 32])
    s5_l1 = din("s5_l1", [2, 128, 3 * 24 * 128])
    s5_bl1 = din("s5_bl1", [2, 128, 2 * 24 * 128])
    s5_dg = din("s5_dg", [2, 128, 2 * 24])
    lru_v = din("lru_v", [2, 128, 24 * 8])
    lru_w = din("lru_w", [2, 12, 128, 2 * 2 * 256])
    outT = nc.dram_tensor("outT", [D, NT], F32, kind="ExternalOutput").ap()
    o_scr = nc.dram_tensor("o_scr", [D, TB], F32, kind="Internal").ap()
    zmat_d = nc.dram_tensor("zmat", [24, 128, 16 * 128], BF16, kind="Internal").ap()
    camat_d = nc.dram_tensor("camat", [128, 2 * 8 * 96 * 32], BF16, kind="Internal").ap()
    kmat_d = nc.dram_tensor("kmat", [24, 128, 8 * 128], BF16, kind="Internal").ap()

    wbf_in = nc.dram_tensor("wbf_in", [1, 64, 128, 32 * 128], BF16, kind="Internal").ap()
    wbf_out = nc.dram_tensor("wbf_out", [1, 32, 128, 32 * 128], BF16, kind="Internal").ap()
    wbf_glu = nc.dram_tensor("wbf_glu", [1, 24, 128, 24 * 128], BF16, kind="Internal").ap()

    def sb(name, shape, dt=F32):
        return nc.alloc_sbuf_tensor("sb_" + name, list(shape), dt)

    ones_bf = sb("ones_bf", [128, 128], BF16)
    gains = sb("gains", [128, 3, 4, 32])
    hnT = sb("hnT", [128, 32, TB], BF16)
    mixed = sb("mixed", [128, 32, TB], BF16)
    u = sb("u", [128, 24, 3 + TB], BF16)
    gsil = sb("gsil", [128, 24, TB], BF16)
    qT = sb("qT", [128, 8, TB], BF16)
    gqsil = sb("gqsil", [128, 8, TB], BF16)
    NWS = 3
    wb = [sb("wb%d" % i, [128, 32, 128], BF16) for i in range(NWS)]
    NHS = 4
    hs = [sb("hs%d" % i, [128, TB]) for i in range(NHS)]
    sqb = [sb("sqb%d" % i, [128, TB], BF16) for i in range(2)]
    rstd = sb("rstd", [128, TB])
    rstd_m = sb("rstd_m", [128, NMEM])
    KT = sb("KT", [128, 8, NMEM], BF16)
    Vt = sb("Vt", [128, 2, 1024], BF16)
    expT = sb("expT", [128, 2, TB], BF16)
    rden = sb("rden", [128, TB])
    NTMP = 5
    tmpf = [sb("tmpf%d" % i, [128, TB]) for i in range(NTMP)]
    osb = [sb("osb%d" % i, [128, TB]) for i in range(2)]
    S = sb("S5S", [128, 192])
    AAB = sb("S5AAB", [128, 2, 192])
    sct = sb("s5sct", [128, 3, 192])
    dg = sb("s5dg", [128, 2, 24])
    hbias = sb("hbias", [128, 24])
    uq = sb("uq", [128, 2, 4, TB], BF16)
    lv = sb("lruv", [128, 24, 8])
    negc = sb("negc", [128, 24])
    hst = sb("hst", [128, 24])
    AFN = 9216
    ABN = 22528
    arF = sb("arenaF", [128, AFN])
    arB = sb("arenaB", [128, ABN], BF16)

    def carve(ar, off, shape):
        n = 1
        for d_ in shape:
            n *= d_
        v = ar[:, off:off + n]
        if len(shape) == 2:
            return v.rearrange("p (a b) -> p a b", a=shape[0])
        if len(shape) == 3:
            return v.rearrange("p (a b c) -> p a b c", a=shape[0], b=shape[1])
        if len(shape) == 4:
            return v.rearrange("p (a b c d) -> p a b c d", a=shape[0], b=shape[1], c=shape[2])
        return v

    Zsb = carve(arF, 0, [NCH, 192])
    Shist = carve(arB, 0, [NCH, 192])
    gT = carve(arB, 6144, [24, TB])
    zmat = [carve(arB, 12288 + i * 2048, [16, 128]) for i in range(2)]
    camat = [carve(arB, 16384 + i * 2048, [2, 8, 4, 32]) for i in range(2)]
    kmat = [carve(arB, 20480 + i * 1024, [8, 128]) for i in range(2)]
    xc = carve(arF, 0, [24, TB])
    a4 = carve(arF, 6144, [4, TB])
    g4 = carve(arF, 7168, [4, TB])
    v4 = carve(arF, 8192, [4, TB])
    xcb = carve(arB, 0, [24, TB])
    gw = [carve(arB, 6144 + i * 1024, [2, 2, 256]) for i in range(2)]

    ps = [nc.alloc_psum_tensor("ps%d" % i, [128, 512], F32) for i in range(8)]
    PS_SS = 0

    state = {"mm": 0, "w": 0, "tmp": 0, "z": 0, "y": 0, "hs": 0}

    def mm_bank():
        state["mm"] = (state["mm"] + 1) % 3
        return 1 + state["mm"]

    def z_bank():
        state["z"] = (state["z"] + 1) % 2
        return 4 + state["z"]

    def y_bank():
        state["y"] = (state["y"] + 1) % 2
        return 6 + state["y"]

    def tmp():
        state["tmp"] = (state["tmp"] + 1) % NTMP
        i = state["tmp"]
        return tmpf[i], R("tmpf", i)

    def hs_slot():
        state["hs"] = (state["hs"] + 1) % NHS
        i = state["hs"]
        return i, hs[i], R("hs", i)

    def V_tt(out, in0, in1, op, r, w):
        kb.op("dve", lambda e: e.tensor_tensor(out=out, in0=in0, in1=in1, op=op), r, w)

    def V_ts(out, in0, s1, s2, op0, op1, r, w):
        if op1 is None:
            kb.op("dve", lambda e: e.tensor_scalar(out=out, in0=in0, scalar1=s1, scalar2=None, op0=op0), r, w)
        else:
            kb.op("dve", lambda e: e.tensor_scalar(out=out, in0=in0, scalar1=s1, scalar2=s2, op0=op0, op1=op1), r, w)

    def V_stt(out, in0, scalar, in1, op0, op1, r, w):
        kb.op("dve", lambda e: e.scalar_tensor_tensor(out=out, in0=in0, scalar=scalar, in1=in1, op0=op0, op1=op1), r, w)

    def V_copy(out, in_, r, w):
        kb.op("dve", lambda e: e.tensor_copy(out=out, in_=in_), r, w)

    def V_recip(out, in_, r, w):
        kb.op("dve", lambda e: e.reciprocal(out=out, in_=in_), r, w)

    def V_memset(out, val, r, w):
        kb.op("dve", lambda e: e.memset(out, val), r, w)

    def A_act(out, in_, func, r, w, bias=None, scale=None):
        kw = {}
        if bias is not None:
            kw["bias"] = bias
        if scale is not None:
            kw["scale"] = scale
        kb.op("act", lambda e: e.activation(out=out, in_=in_, func=func, **kw), r, w)

    def MM(out, lhsT, rhs, start, stop, r, w, tp=None, sync=False):
        if tp is None:
            o = kb.op("pe", lambda e: e.matmul(out, lhsT=lhsT, rhs=rhs, start=start, stop=stop), r, w)
        else:
            o = kb.op("pe", lambda e: e.matmul(out, lhsT=lhsT, rhs=rhs, start=start, stop=stop, tile_position=tp), r, w)
        o.pesync = sync

    def PSR(b):
        return [R("ps", b, 0), R("ps", b, 1)]

    def load_w(kind, l, c, first=True):
        i = state["w"] = (state["w"] + 1) % NWS
        src32 = {"kv": w_kv, "in": w_in, "out": w_out}[kind][l, c] if kind != "glu" else w_glu[l // 2, c]
        n = src32.shape[-1] // 128
        if kind == "kv":
            kb.dma("pool", wb[i][:, 0:n, :], src32.rearrange("p (k c) -> p k c", c=128), writes=[R("wb", i)], chan="wb%d" % i)
            return wb[i], R("wb", i)
        dcopy = {"in": wbf_in, "glu": wbf_glu, "out": wbf_out}[kind][0, c]
        if not WCACHE:
            kb.dma("pool", wb[i][:, 0:n, :], src32.rearrange("p (k c) -> p k c", c=128), writes=[R("wb", i)], chan="wb%d" % i)
        elif first:
            kb.dma("pool", wb[i][:, 0:n, :], src32.rearrange("p (k c) -> p k c", c=128), writes=[R("wb", i)], chan="wb%d" % i)
            kb.dma("sp", dcopy.rearrange("p (k c) -> p k c", c=128), wb[i][:, 0:n, :], reads=[R("wb", i)],
                   writes=[R("wbf", kind, c)], chan="wst%d" % i)
        else:
            kb.dma("pool", wb[i][:, 0:n, :], dcopy.rearrange("p (k c) -> p k c", c=128), reads=[R("wbf", kind, c)],
                   writes=[R("wb", i)], chan="wb%d" % i)
        return wb[i], R("wb", i)

    def rsqrt_mean(dst, rdst, psap, rps):
        V_ts(dst, psap, 1.0 / D, EPS, ALU.mult, ALU.add, rps, [rdst])
        A_act(dst, dst, AF.Sqrt, [rdst], [rdst])
        V_recip(dst, dst, [rdst], [rdst])

    V_memset(ones_bf[:], 1.0, (), [R("ones")])
    V_memset(uq[:], 0.0, (), [R("uq", 0), R("uq", 1)])
    kb.dma("sp", gains[:], gains_d.rearrange("p (a l k) -> p a l k", a=3, l=4), writes=[R("gains")], chan="misc")

    def silu_from_psum(out_bf, psap, rps, wres):
        t, rt = tmp()
        A_act(t[:], psap, AF.Tanh, rps, [rt], scale=0.5)
        V_ts(t[:], t[:], 0.5, 0.5, ALU.mult, ALU.add, [rt], [rt])
        V_tt(out_bf, psap, t[:], ALU.mult, rps + [rt], [wres])

    mem_state = {"rstd": False}

    def phase_kv(l):
        if not mem_state["rstd"]:
            for kt in range(32):
                si, ht, rh = hs_slot()
                kb.dma("sp", ht[:], memT[kt * 128:(kt + 1) * 128, :], writes=[rh], chan="hs%d" % si)
                i = kt % 2
                A_act(sqb[i][:], ht[:], AF.Square, [rh], [R("sqb", i)])
                MM(ps[PS_SS][:, 0:NMEM], ones_bf[:], sqb[i][:], kt == 0, kt == 31, [R("ones"), R("sqb", i)], [*PSR(PS_SS)])
            rsqrt_mean(rstd_m[:], R("rstd_m"), ps[PS_SS][:, 0:NMEM], PSR(PS_SS))
            mem_state["rstd"] = True
        for kt in range(32):
            si, ht, rh = hs_slot()
            kb.dma("sp", ht[:], memT[kt * 128:(kt + 1) * 128, :], writes=[rh], chan="hs%d" % si)
            V_stt(hnT[:, kt, :], ht[:], gains[:, 2, l, kt:kt + 1], rstd_m[:], ALU.mult, ALU.mult,
                  [rh, R("gains"), R("rstd_m")], [R("hnT", kt)])
        for c in range(16):
            wt, rw = load_w("kv", l, c)
            b = mm_bank()
            if c < 8:
                for kt in range(32):
                    MM(ps[b][:, 0:NMEM], wt[:, kt, :], hnT[:, kt, :], kt == 0, kt == 31, [rw, R("hnT", kt)], [*PSR(b)])
                A_act(KT[:, c, :], ps[b][:, 0:NMEM], AF.Copy, [*PSR(b)], [R("KT")])
            else:
                cv = c - 8
                for j in range(2):
                    for kt in range(32):
                        MM(ps[b][:, j * 128:(j + 1) * 128], hnT[:, kt, j * 128:(j + 1) * 128], wt[:, kt, :], kt == 0, kt == 31,
                           [rw, R("hnT", kt)], [*PSR(b)])
                A_act(Vt[:, :, cv * 128:(cv + 1) * 128], ps[b][:, 0:256].rearrange("p (j c) -> p j c", j=2), AF.Copy,
                      [*PSR(b)], [R("Vt")])

    PN = 384
    p2 = [carve(arF, i * PN, [1, PN])[:, 0, :] for i in range(8)]
    pin = arF[:, 8 * PN:11 * PN]
    PSCR = 11 * PN

    def cplx_disc(n, lre, lim, ls):
        ar, ai, cre, cim, t0, t1, t2, t3 = [p2[i][:, 0:n] for i in range(8)]
        rr = [R("prep", i) for i in range(8)]
        rin = [R("prepin")]
        A_act(t0, ls, AF.Exp, rin, [rr[4]])
        V_ts(t1, lre, -1e-4, None, ALU.min, None, rin, [rr[5]])
        V_tt(t2, t1, t0, ALU.mult, [rr[4], rr[5]], [rr[6]])
        A_act(t2, t2, AF.Exp, [rr[6]], [rr[6]])
        V_tt(t3, lim, t0, ALU.mult, rin + [rr[4]], [rr[7]])
        A_act(ai, t3, AF.Sin, [rr[7]], [rr[1]], scale=0.125)
        A_act(ar, t3, AF.Sin, [rr[7]], [rr[0]], scale=-0.125, bias=PI / 2)
        for _ in range(3):
            V_tt(t3, ar, ai, ALU.mult, [rr[0], rr[1]], [rr[7]])
            V_tt(ar, ar, ar, ALU.mult, [rr[0]], [rr[0]])
            V_tt(ai, ai, ai, ALU.mult, [rr[1]], [rr[1]])
            V_tt(ar, ar, ai, ALU.subtract, [rr[0], rr[1]], [rr[0]])
            V_ts(ai, t3, 2.0, None, ALU.mult, None, [rr[7]], [rr[1]])
        V_tt(ar, ar, t2, ALU.mult, [rr[0], rr[6]], [rr[0]])
        V_tt(ai, ai, t2, ALU.mult, [rr[1], rr[6]], [rr[1]])
        V_tt(t0, t1, t1, ALU.mult, [rr[5]], [rr[4]])
        V_tt(t2, lim, lim, ALU.mult, rin, [rr[6]])
        V_tt(t0, t0, t2, ALU.add, [rr[4], rr[6]], [rr[4]])
        V_recip(t0, t0, [rr[4]], [rr[4]])
        V_ts(t2, ar, -1.0, None, ALU.add, None, [rr[0]], [rr[6]])
        V_tt(cre, t2, t1, ALU.mult, [rr[6], rr[5]], [rr[2]])
        V_tt(t3, ai, lim, ALU.mult, [rr[1]] + rin, [rr[7]])
        V_tt(cre, cre, t3, ALU.add, [rr[2], rr[7]], [rr[2]])
        V_tt(cre, cre, t0, ALU.mult, [rr[2], rr[4]], [rr[2]])
        V_tt(cim, ai, t1, ALU.mult, [rr[1], rr[5]], [rr[3]])
        V_tt(t3, t2, lim, ALU.mult, [rr[6]] + rin, [rr[7]])
        V_tt(cim, cim, t3, ALU.subtract, [rr[3], rr[7]], [rr[3]])
        V_tt(cim, cim, t0, ALU.mult, [rr[3], rr[4]], [rr[3]])
        return ar, ai, cre, cim

    def cmul(ore, oim, are, aim, bre, bim, t0, t1, r, w):
        V_tt(t0, are, bre, ALU.mult, r + w, w)
        V_tt(t1, aim, bim, ALU.mult, r + w, w)
        V_tt(t0, t0, t1, ALU.subtract, w, w)
        V_tt(t1, are, bim, ALU.mult, r + w, w)
        V_tt(oim, aim, bre, ALU.mult, r + w, w)
        V_tt(oim, oim, t1, ALU.add, w, w)
        V_copy(ore, t0, w, w)

    def s5_prep(j):
        RX = R("prepbig")
        RB = R("prepbf")
        RK = R("prepK")
        r03 = [R("prep", 0), R("prep", 1), R("prep", 2), R("prep", 3)]
        wp = [R("prep", 4), R("prep", 5), R("prep", 6), R("prep", 7)]
        n2 = 96
        kb.dma("sp", pin[:, 0:3 * n2], s5_l2[j], writes=[R("prepin")], chan="misc")
        ar, ai, cre, cim = cplx_disc(n2, pin[:, 0:n2], pin[:, n2:2 * n2], pin[:, 2 * n2:3 * n2])
        p8r, p8i, t6, t7 = p2[4][:, 0:n2], p2[5][:, 0:n2], p2[6][:, 0:n2], p2[7][:, 0:n2]
        V_copy(p8r, ar, r03, wp)
        V_copy(p8i, ai, r03, wp)
        for _ in range(3):
            cmul(p8r, p8i, p8r, p8i, p8r, p8i, t6, t7, [], wp)
        V_copy(AAB[:, 0, 0:96], p8r, wp, [R("AAB")])
        V_copy(AAB[:, 0, 96:192], p8r, wp, [R("AAB")])
        V_ts(AAB[:, 1, 0:96], p8i, -1.0, None, ALU.mult, None, wp, [R("AAB")])
        V_copy(AAB[:, 1, 96:192], p8i, wp, [R("AAB")])
        NQ = 24
        Bf = carve(arF, PSCR, [2, NQ, 32])
        Cf = carve(arF, PSCR + 1536, [2, NQ, 32])
        T0 = carve(arF, PSCR + 3072, [NQ, 32])
        T1 = carve(arF, PSCR + 3840, [NQ, 32])
        Bb = carve(arB, 0, [2, NQ, 32])
        CAb = carve(arB, 1536, [2, NQ, 32])
        Kst = carve(arB, 3072, [6, 128])
        bl2 = s5_bl2[j].rearrange("p (a b c) -> p a b c", a=2, b=96)
        cl2 = s5_cl2[j].rearrange("p (a b c) -> p a b c", a=2, b=96)
        km_v = kmat_d.rearrange("t p (l c) -> p t l c", l=8)
        for qt in range(4):
            kb.dma("sp", Bf, bl2[:, :, qt * NQ:(qt + 1) * NQ, :], writes=[RX], chan="misc")
            kb.dma("sp", Cf, cl2[:, :, qt * NQ:(qt + 1) * NQ, :], writes=[RX], chan="misc")

            def bc(x):
                return x[:, qt * NQ:(qt + 1) * NQ].unsqueeze(2).to_broadcast([128, NQ, 32])

            cmul(Bf[:, 0], Bf[:, 1], bc(cre), bc(cim), Bf[:, 0], Bf[:, 1], T0, T1, r03, [RX])
            V_copy(Bb[:, 0], Bf[:, 0], [RX], [RB])
            V_copy(Bb[:, 1], Bf[:, 1], [RX], [RB])
            for jj in range(9):
                V_copy(CAb[:, 0], Cf[:, 0], [RX], [RB])
                V_ts(CAb[:, 1], Cf[:, 1], -1.0, None, ALU.mult, None, [RX], [RB])
                if jj >= 1:
                    for ri in range(2):
                        o0 = ((ri * 8 + jj - 1) * 96 + qt * NQ) * 32
                        kb.dma("sp", camat_d[:, o0:o0 + NQ * 32], CAb[:, ri].rearrange("p a c -> p (a c)"), reads=[RB],
                               writes=[R("camat_d")], chan="prepst")
                if jj <= 7:
                    V_memset(Kst, 0.0, [RK], [RK])
                    for tg in range(2):
                        b = z_bank()
                        for tl in range(3):
                            for q in range(4):
                                pr = (tg * 3 + tl) * 4 + q
                                for ri in range(2):
                                    MM(ps[b][32 * q:32 * q + 32, tl * 128 + 32 * q: tl * 128 + 32 * q + 32],
                                       Bb[:, ri, pr, :], CAb[:, ri, pr, :], ri == 0, ri == 1, [RB], [*PSR(b)], tp=(0, 32 * q), sync=True)
                        for q in range(4):
                            src = ps[b][32 * q:32 * q + 32, 0:384].rearrange("p (t c) -> p t c", t=3)[:, :, 32 * q:32 * q + 32]
                            dst = Kst[32 * q:32 * q + 32, tg * 3:tg * 3 + 3, 32 * q:32 * q + 32]
                            A_act(dst, src, AF.Copy, [*PSR(b)], [RK])
                    kb.dma("sp", km_v[:, qt * 6:(qt + 1) * 6, jj, :], Kst, reads=[RK], writes=[R("kmat_d")], chan="prepst")
                    cmul(Cf[:, 0], Cf[:, 1], bc(ar), bc(ai), Cf[:, 0], Cf[:, 1], T0, T1, r03, [RX])
        n1 = 3 * 128
        l1v = s5_l1[j].rearrange("p (a n) -> p a n", a=3)
        b1v = s5_bl1[j].rearrange("p (a n) -> p a n", a=2)
        zm_v = zmat_d.rearrange("t p (k c) -> p t k c", k=16)
        Wr = arF[:, PSCR:PSCR + n1]
        Wi = arF[:, PSCR + n1:PSCR + 2 * n1]
        U0 = arF[:, PSCR + 2 * n1:PSCR + 3 * n1]
        U1 = arF[:, PSCR + 3 * n1:PSCR + 4 * n1]
        Zst = carve(arB, 4096, [3, 16, 128])
        for ch in range(8):
            kb.dma("sp", pin[:, 0:3 * n1].rearrange("p (a n) -> p a n", a=3), l1v[:, :, ch * n1:(ch + 1) * n1],
                   writes=[R("prepin")], chan="misc")
            ar, ai, cre, cim = cplx_disc(n1, pin[:, 0:n1], pin[:, n1:2 * n1], pin[:, 2 * n1:3 * n1])
            kb.dma("sp", arF[:, PSCR:PSCR + 2 * n1].rearrange("p (a n) -> p a n", a=2), b1v[:, :, ch * n1:(ch + 1) * n1],
                   writes=[RX], chan="misc")
            cmul(Wr, Wi, cre, cim, Wr, Wi, U0, U1, r03, [RX])
            for k in range(7, -1, -1):
                V_copy(Zst[:, :, 2 * k, :], Wr.rearrange("p (t c) -> p t c", t=3), [RX], [RB])
                V_copy(Zst[:, :, 2 * k + 1, :], Wi.rearrange("p (t c) -> p t c", t=3), [RX], [RB])
                if k > 0:
                    cmul(Wr, Wi, ar, ai, Wr, Wi, U0, U1, r03, [RX])
            kb.dma("sp", zm_v[:, ch * 3:(ch + 1) * 3], Zst, reads=[RB], writes=[R("zmat_d")], chan="prepst")
        kb.dma("sp", dg[:], s5_dg[j].rearrange("p (a t) -> p a t", a=2), writes=[R("dg")], chan="misc")
        V_ts(hbias[:], dg[:, 1, :], 0.5, None, ALU.mult, None, [R("dg")], [R("hb")])
        V_memset(S[:], 0.0, [R("S")], [R("S")])

    def s5_block(j, blk):
        RZ = R("Zsb")
        RS = R("S")
        RH = R("Shist")
        RT = R("sct")
        for t in range(24):
            i = t % 2
            par = t % 2
            kb.dma("sp", zmat[i], zmat_d[t].rearrange("p (k c) -> p k c", k=16), reads=[R("zmat_d")], writes=[R("zmat", i)],
                   chan="zm%d" % i)
            for q in range(4):
                A_act(uq[32 * q:32 * q + 32, par, q, :], u[32 * q:32 * q + 32, t, 3:3 + TB], AF.Copy, [R("u", t)], [R("uq", par)])
            b = z_bank()
            for q in range(4):
                uv = uq[:, par, q, :].rearrange("p (n k) -> p k n", k=T8)
                for ri in range(2):
                    for k in range(T8):
                        MM(ps[b][:, (ri * 4 + q) * NCH:(ri * 4 + q + 1) * NCH], zmat[i][:, 2 * k + ri, :],
                           uv[:, k, :], k == 0, k == T8 - 1, [R("zmat", i), R("uq", par)], [*PSR(b)])
            dst = Zsb.rearrange("p n (r g) -> p n r g", r=2)[:, :, :, 4 * t:4 * t + 4]
            src = ps[b][:, 0:8 * NCH].rearrange("p (r q n) -> p n r q", r=2, q=4)
            A_act(dst, src, AF.Copy, [*PSR(b)], [RZ])
        for n in range(NCH):
            A_act(Shist[:, n, :], S[:], AF.Copy, [RS], [RH])
            V_tt(sct[:, 0, :], AAB[:, 0, :], S[:], ALU.mult, [RS, R("AAB")], [RT])
            V_tt(sct[:, 1, 0:96], AAB[:, 1, 0:96], S[:, 96:192], ALU.mult, [RS, R("AAB")], [RT])
            V_tt(sct[:, 1, 96:192], AAB[:, 1, 96:192], S[:, 0:96], ALU.mult, [RS, R("AAB")], [RT])
            V_tt(sct[:, 0, :], sct[:, 0, :], sct[:, 1, :], ALU.add, [RT], [RT])
            V_tt(S[:], sct[:, 0, :], Zsb[:, n, :], ALU.add, [RT, RZ], [RS])
        for t in range(24):
            i = t % 2
            kb.dma("sp", camat[i].rearrange("p r k q c -> p (r k) (q c)"),
                   camat_d.rearrange("p (rk pr c) -> p rk pr c", rk=16, pr=96)[:, :, 4 * t:4 * t + 4, :].rearrange("p a q c -> p a (q c)"),
                   reads=[R("camat_d")], writes=[R("camat", i)], chan="cm%d" % i)
            kb.dma("sp", kmat[i], kmat_d[t].rearrange("p (l c) -> p l c", l=8), reads=[R("kmat_d")], writes=[R("kmat", i)],
                   chan="km%d" % i)
            b = y_bank()
            uv = u[:, t, 3:3 + TB].rearrange("p (n k) -> p k n", k=T8)
            for k in range(T8):
                yo = ps[b][:, k * NCH:(k + 1) * NCH]
                for l in range(k + 1):
                    MM(yo, kmat[i][:, l, :], uv[:, k - l, :], l == 0, False, [R("kmat", i), R("u", t)], [*PSR(b)])
                for ri in range(2):
                    for q in range(4):
                        MM(ps[b][32 * q:32 * q + 32, k * NCH:(k + 1) * NCH], camat[i][:, ri, k, q, :],
                           Shist[:, :, ri * 96 + 4 * t + q], False, (ri == 1), [R("camat", i), RH], [*PSR(b)], tp=(0, 32 * q),
                           sync=(YSYNC == 1 or (YSYNC == 2 and q == 0)))
            yt, ry = tmp()
            V_stt(yt[:].rearrange("p (n k) -> p n k", k=T8), u[:, t, 3:3 + TB].rearrange("p (n k) -> p n k", k=T8), dg[:, 0, t:t + 1],
                  ps[b][:, 0:TB].rearrange("p (k n) -> p n k", k=T8), ALU.mult, ALU.add, [R("u", t), R("dg"), *PSR(b)], [ry])
            t2, r2 = tmp()
            V_tt(t2[:], yt[:], yt[:], ALU.mult, [ry], [r2])
            V_ts(t2[:], t2[:], 0.044715, 1.0, ALU.mult, ALU.add, [r2], [r2])
            V_tt(t2[:], t2[:], yt[:], ALU.mult, [r2, ry], [r2])
            A_act(t2[:], t2[:], AF.Tanh, [r2], [r2], scale=math.sqrt(2.0 / PI))
            V_ts(t2[:], t2[:], 0.5, 0.5, ALU.mult, ALU.add, [r2], [r2])
            V_tt(gT[:, t, :], t2[:], yt[:], ALU.mult, [r2, ry], [R("gT", t)])
        for c in range(24):
            wt, rw = load_w("glu", 2 * j, c, blk == 0)
            b = mm_bank()
            for kt in range(24):
                MM(ps[b][:, 0:TB], wt[:, kt, :], gT[:, kt, :], kt == 0, kt == 23, [rw, R("gT", kt)], [*PSR(b)])
            t2, r2 = tmp()
            A_act(t2[:], ps[b][:, 0:TB], AF.Tanh, [*PSR(b), R("hb")], [r2], scale=0.5, bias=hbias[:, c:c + 1])
            V_ts(t2[:], t2[:], 0.5, 0.5, ALU.mult, ALU.add, [r2], [r2])
            V_tt(t2[:], t2[:], gT[:, c, :], ALU.mult, [r2, R("gT", c)], [r2])
            V_tt(mixed[:, c, :], t2[:], gsil[:, c, :], ALU.mult, [r2, R("gsil", c)], [R("mixed", c)])

    def lru_prep(j):
        kb.dma("sp", lv[:], lru_v[j].rearrange("p (t k) -> p t k", k=8), writes=[R("lv")], chan="misc")
        A_act(negc[:], lv[:, :, 7], AF.Exp, [R("lv")], [R("negc")], scale=-1.0)
        A_act(negc[:], negc[:], AF.Ln, [R("negc")], [R("negc")], bias=1.0)
        V_ts(negc[:], negc[:], -8.0, None, ALU.mult, None, [R("negc")], [R("negc")])
        V_ts(lv[:, :, 5:7], lv[:, :, 5:7], 0.5, None, ALU.mult, None, [R("lv")], [R("lv")])
        V_memset(hst[:], 0.0, [R("hst")], [R("hst")])
        V_memset(u[:, :, 0:3], 0.0, [R("utail")], [R("utail")])

    def lru_block(j, blk):
        for t in range(24):
            ru = [R("u", t), R("utail"), R("lv")]
            V_ts(xc[:, t, :], u[:, t, 3:3 + TB], lv[:, t, 3:4], lv[:, t, 4:5], ALU.mult, ALU.add, ru, [R("xc", t)])
            for jj in range(3):
                V_stt(xc[:, t, :], u[:, t, jj:jj + TB], lv[:, t, jj:jj + 1], xc[:, t, :], ALU.mult, ALU.add, ru + [R("xc", t)], [R("xc", t)])
            A_act(xcb[:, t, :], xc[:, t, :], AF.Copy, [R("xc", t)], [R("xcb", t)])
        V_copy(u[:, :, 0:3], u[:, :, TB:TB + 3], [R("u", t) for t in range(24)] + [R("xc", t) for t in range(24)], [R("utail")])
        for h2 in range(6):
            R4 = R("lru4")
            for hh in range(2):
                h = 2 * h2 + hh
                i = h % 2
                kb.dma("pool", gw[i], lru_w[j, h].rearrange("p (a i j) -> p a i j", a=2, i=2), writes=[R("gw", i)], chan="gw%d" % i)
                for j2 in range(2):
                    t = 2 * h + j2
                    k4 = 2 * hh + j2
                    b = z_bank()
                    for ax in range(2):
                        for i2 in range(2):
                            MM(ps[b][:, ax * TB:(ax + 1) * TB], gw[i][:, ax, i2, j2 * 128:(j2 + 1) * 128], xcb[:, 2 * h + i2, :],
                               i2 == 0, i2 == 1, [R("gw", i), R("xcb", 2 * h + i2)], [*PSR(b)])
                    r_, rr_ = tmp()
                    A_act(r_[:], ps[b][:, 0:TB], AF.Tanh, [*PSR(b), R("lv")], [rr_], scale=0.5, bias=lv[:, t, 5:6])
                    A_act(g4[:, k4, :], ps[b][:, TB:2 * TB], AF.Tanh, [*PSR(b), R("lv")], [R4], scale=0.5, bias=lv[:, t, 6:7])
                    V_ts(r_[:], r_[:], 0.5, 0.5, ALU.mult, ALU.add, [rr_], [rr_])
                    V_ts(g4[:, k4, :], g4[:, k4, :], 0.5, 0.5, ALU.mult, ALU.add, [R4], [R4])
                    A_act(a4[:, k4, :], r_[:], AF.Exp, [rr_, R("negc")], [R4], scale=negc[:, t:t + 1])
                    V_tt(v4[:, k4, :], a4[:, k4, :], a4[:, k4, :], ALU.mult, [R4], [R4])
                    V_ts(v4[:, k4, :], v4[:, k4, :], -1.0, 1.0, ALU.mult, ALU.add, [R4], [R4])
                    V_tt(g4[:, k4, :], g4[:, k4, :], xc[:, t, :], ALU.mult, [R4, R("xc", t)], [R4])
            A_act(v4, v4, AF.Sqrt, [R4], [R4])
            if blk == 0:
                V_memset(v4[:, :, 0:1], 1.0, [R4], [R4])
            V_tt(g4, g4, v4, ALU.mult, [R4], [R4])
            for k4 in range(4):
                t = 4 * h2 + k4
                kb.op("dve", lambda e, k4=k4, t=t: e.tensor_tensor_scan(out=v4[:, k4, :], data0=a4[:, k4, :], data1=g4[:, k4, :],
                                                                         initial=hst[:, t:t + 1], op0=ALU.mult, op1=ALU.add),
                      [R4, R("hst")], [R4])
                V_copy(hst[:, t:t + 1], v4[:, k4, TB - 1:TB], [R4], [R("hst")])
                V_tt(mixed[:, t, :], v4[:, k4, :], gsil[:, t, :], ALU.mult, [R4, R("gsil", t)], [R("mixed", t)])

    last_stores = []
    V_memset(u[:, :, 0:3], 0.0, (), [R("utail")])
    for li, l in enumerate(layers):
        is_s5 = (l % 2 == 0)
        j = l // 2
        src = xT if li == 0 else outT
        kb.barrier()
        phase_kv(l)
        if is_s5:
            s5_prep(j)
        else:
            lru_prep(j)
        kb.barrier()
        for blk in range(NB):
            tok = slice(blk * TB, (blk + 1) * TB)
            for kt in range(32):
                rsrc = [] if li == 0 else [R("outT", blk, kt)]
                si, ht, rh = hs_slot()
                kb.dma("sp", ht[:], src[kt * 128:(kt + 1) * 128, tok], reads=rsrc, writes=[rh], chan="hs%d" % si)
                i = kt % 2
                A_act(sqb[i][:], ht[:], AF.Square, [rh], [R("sqb", i)])
                MM(ps[PS_SS][:, 0:TB], ones_bf[:], sqb[i][:], kt == 0, kt == 31, [R("ones"), R("sqb", i)], [*PSR(PS_SS)])
            rsqrt_mean(rstd[:], R("rstd"), ps[PS_SS][:, 0:TB], PSR(PS_SS))
            for kt in range(32):
                rsrc = [] if li == 0 else [R("outT", blk, kt)]
                si, ht, rh = hs_slot()
                kb.dma("sp", ht[:], src[kt * 128:(kt + 1) * 128, tok], reads=rsrc, writes=[rh], chan="hs%d" % si)
                V_stt(hnT[:, kt, :], ht[:], gains[:, 0, l, kt:kt + 1], rstd[:], ALU.mult, ALU.mult,
                      [rh, R("gains"), R("rstd")], [R("hnT", kt)])
            for c in range(64):
                wt, rw = load_w("in", l, c, blk == 0)
                b = mm_bank()
                for kt in range(32):
                    MM(ps[b][:, 0:TB], wt[:, kt, :], hnT[:, kt, :], kt == 0, kt == 31, [rw, R("hnT", kt)], [*PSR(b)])
                if c < 24:
                    A_act(u[:, c, 3:3 + TB], ps[b][:, 0:TB], AF.Copy, [*PSR(b)], [R("u", c)])
                elif c < 48:
                    silu_from_psum(gsil[:, c - 24, :], ps[b][:, 0:TB], PSR(b), R("gsil", c - 24))
                elif c < 56:
                    A_act(qT[:, c - 48, :], ps[b][:, 0:TB], AF.Copy, [*PSR(b)], [R("qT", c - 48)])
                else:
                    silu_from_psum(gqsil[:, c - 56, :], ps[b][:, 0:TB], PSR(b), R("gqsil", c - 56))
            if is_s5:
                s5_block(j, blk)
            else:
                lru_block(j, blk)
            for hd in range(4):
                for jn in range(2):
                    b = z_bank()
                    for dt_ in range(2):
                        MM(ps[b][:, 0:TB], KT[:, 2 * hd + dt_, jn * 128:(jn + 1) * 128], qT[:, 2 * hd + dt_, :], dt_ == 0, dt_ == 1,
                           [R("KT"), R("qT", 2 * hd + dt_)], [*PSR(b)])
                    A_act(expT[:, jn, :], ps[b][:, 0:TB], AF.Exp, [*PSR(b)], [R("expT", jn)], scale=1.0 / 16.0)
                for jn in range(2):
                    MM(ps[PS_SS][:, 0:TB], ones_bf[:], expT[:, jn, :], jn == 0, jn == 1, [R("ones"), R("expT", jn)], [*PSR(PS_SS)])
                V_recip(rden[:], ps[PS_SS][:, 0:TB], [*PSR(PS_SS)], [R("rden")])
                for dt_ in range(2):
                    b = y_bank()
                    c = 2 * hd + dt_
                    for jn in range(2):
                        MM(ps[b][:, 0:TB], Vt[:, jn, c * 128:(c + 1) * 128], expT[:, jn, :], jn == 0, jn == 1,
                           [R("Vt"), R("expT", jn)], [*PSR(b)])
                    t2, r2 = tmp()
                    V_tt(t2[:], ps[b][:, 0:TB], rden[:], ALU.mult, [*PSR(b), R("rden")], [r2])
                    V_tt(mixed[:, 24 + c, :], t2[:], gqsil[:, c, :], ALU.mult, [r2, R("gqsil", c)], [R("mixed", 24 + c)])
            for c in range(32):
                wt, rw = load_w("out", l, c, blk == 0)
                b = mm_bank()
                for kt in range(32):
                    MM(ps[b][:, 0:TB], wt[:, kt, :], mixed[:, kt, :], kt == 0, kt == 31, [rw, R("mixed", kt)], [*PSR(b)])
                i = c % 2
                A_act(osb[i][:], ps[b][:, 0:TB], AF.Copy, [*PSR(b)], [R("osb", i)])
                A_act(sqb[i][:], ps[b][:, 0:TB], AF.Square, [*PSR(b)], [R("sqb", i)])
                MM(ps[PS_SS][:, 0:TB], ones_bf[:], sqb[i][:], c == 0, c == 31, [R("ones"), R("sqb", i)], [*PSR(PS_SS)])
                kb.dma("sp", o_scr[c * 128:(c + 1) * 128, :], osb[i][:], reads=[R("osb", i)], writes=[R("o_scr", c)], chan="ost%d" % i)
            rsqrt_mean(rstd[:], R("rstd"), ps[PS_SS][:, 0:TB], PSR(PS_SS))
            for c in range(32):
                i = c % 2
                rsrc = [] if li == 0 else [R("outT", blk, c)]
                si, ht, rh = hs_slot()
                kb.dma("sp", ht[:], src[c * 128:(c + 1) * 128, tok], reads=rsrc, writes=[rh], chan="hs%d" % si)
                kb.dma("sp", osb[i][:], o_scr[c * 128:(c + 1) * 128, :], reads=[R("o_scr", c)], writes=[R("osb", i)], chan="old%d" % i)
                V_stt(osb[i][:], osb[i][:], gains[:, 1, l, c:c + 1], rstd[:], ALU.mult, ALU.mult, [R("osb", i), R("gains"), R("rstd")],
                      [R("osb", i)])
                V_tt(ht[:], ht[:], osb[i][:], ALU.add, [rh, R("osb", i)], [rh])
                st = kb.dma("sp", outT[c * 128:(c + 1) * 128, tok], ht[:], reads=[rh], writes=[R("outT", blk, c)], chan="hst")
                if li == len(layers) - 1:
                    last_stores.append(st)
    kb.emit(final_waits=last_stores)
    return nc


def _panel(w, ktn):
    K, C = w.shape
    a = w.reshape(ktn, 128, C // 128, 128)
    return np.ascontiguousarray(a.transpose(2, 1, 0, 3)).reshape(C // 128, 128, ktn * 128)


def _chan_major(v):
    n = v.shape[-1] // 128
    a = v.reshape(v.shape[:-1] + (n, 128))
    return np.ascontiguousarray(np.moveaxis(a, -1, 0))


def prep_shared(inp):
    f = np.float32
    sh = {}
    sh["w_in"] = np.stack([_panel(np.asarray(inp["w_in"][l], f), 32) for l in range(4)])
    sh["w_out"] = np.stack([_panel(np.asarray(inp["w_out"][l], f), 32) for l in range(4)])
    sh["w_kv"] = np.stack([_panel(np.asarray(inp["w_kv"][l], f), 32) for l in range(4)])
    sh["w_glu"] = np.stack([_panel(np.asarray(inp["s5_w_glu"][j], f), 24) for j in range(2)])
    g = np.stack([_chan_major(np.asarray(inp[k], f)) for k in ("pre_norm", "post_norm", "mem_norm")], axis=1)
    sh["gains"] = np.ascontiguousarray(g).reshape(128, 3 * 4 * 32)
    lre = np.asarray(inp["s5_lam_re"], f)
    lim = np.asarray(inp["s5_lam_im"], f)
    ls = np.asarray(inp["s5_log_step"], f)
    bre = np.asarray(inp["s5_b_re"], f)
    bim = np.asarray(inp["s5_b_im"], f)
    cre = np.asarray(inp["s5_c_re"], f)
    cim = np.asarray(inp["s5_c_im"], f)

    def l2(v):
        return v.reshape(96, 2, 64).transpose(1, 2, 0).reshape(128, 96)

    def l1(v):
        a = v.reshape(24, 4, 2, 64)
        a = np.broadcast_to(a[:, :, None, :, :], (24, 4, 32, 2, 64))
        return np.ascontiguousarray(a.transpose(1, 2, 0, 3, 4)).reshape(128, 24 * 128)

    s5_l2, s5_bl2, s5_cl2, s5_l1, s5_bl1, s5_dg = [], [], [], [], [], []
    for j in range(2):
        lsb = np.broadcast_to(ls[j][:, None], (192, 64))
        s5_l2.append(np.stack([l2(lre[j]), l2(lim[j]), l2(lsb)], axis=1).reshape(128, 3 * 96))
        s5_l1.append(np.stack([l1(lre[j]), l1(lim[j]), l1(lsb)], axis=1).reshape(128, 3 * 24 * 128))

        def bl2(b):
            o = np.zeros((2, 64, 96, 2, 16), f)
            bb = b.reshape(96, 2, 64, 16)
            for g2 in range(2):
                o[g2, :, :, g2, :] = bb[:, g2].transpose(1, 0, 2)
            return o.reshape(128, 96 * 32)

        def cl2(c):
            return bl2(np.ascontiguousarray(c.transpose(0, 2, 1)))

        def bl1(b):
            o = np.zeros((4, 2, 16, 24, 2, 64), f)
            bb = b.reshape(24, 4, 2, 64, 16)
            for g2 in range(2):
                o[:, g2, :, :, g2, :] = bb[:, :, g2].transpose(1, 3, 0, 2)
            return o.reshape(128, 24 * 128)

        s5_bl2.append(np.stack([bl2(bre[j]), bl2(bim[j])], axis=1).reshape(128, -1))
        s5_cl2.append(np.stack([cl2(cre[j]), cl2(cim[j])], axis=1).reshape(128, -1))
        s5_bl1.append(np.stack([bl1(bre[j]), bl1(bim[j])], axis=1).reshape(128, -1))
        s5_dg.append(np.stack([_chan_major(np.asarray(inp["s5_d"][j], f)), _chan_major(np.asarray(inp["s5_b_glu"][j], f))],
                              axis=1).reshape(128, 48))
    sh["s5_l2"] = np.stack(s5_l2)
    sh["s5_bl2"] = np.stack(s5_bl2)
    sh["s5_cl2"] = np.stack(s5_cl2)
    sh["s5_l1"] = np.stack(s5_l1)
    sh["s5_bl1"] = np.stack(s5_bl1)
    sh["s5_dg"] = np.stack(s5_dg)
    lv, lw = [], []
    for j in range(2):
        cw = np.asarray(inp["lru_conv_w"][j], f)
        cols = [cw[0], cw[1], cw[2], cw[3], np.asarray(inp["lru_conv_b"][j], f),
                np.asarray(inp["lru_b_a"][j], f).reshape(-1), np.asarray(inp["lru_b_x"][j], f).reshape(-1),
                np.asarray(inp["lru_lam"][j], f)]
        v = np.stack([_chan_major(c) for c in cols], axis=2)
        lv.append(v.reshape(128, 24 * 8))
        wa = np.asarray(inp["lru_w_a"][j], f).reshape(12, 2, 128, 256)
        wx = np.asarray(inp["lru_w_x"][j], f).reshape(12, 2, 128, 256)
        w = np.stack([wa, wx], axis=1)
        lw.append(np.ascontiguousarray(w.transpose(0, 3, 1, 2, 4)).reshape(12, 128, 2 * 2 * 256))
    sh["lru_v"] = np.stack(lv)
    sh["lru_w"] = np.stack(lw)
    return {k: np.ascontiguousarray(v, dtype=f) for k, v in sh.items()}


def kernel(**inputs):
    x = np.asarray(inputs["x"], np.float32)
    mem = np.asarray(inputs["mem"], np.float32)
    B, L, _ = x.shape
    sh = prep_shared(inputs)
    nc = bass.Bass("TRN2", target_bir_lowering=False)
    build(nc, L)
    in_maps = []
    for b in range(B):
        m = dict(sh)
        m["xT"] = np.ascontiguousarray(x[b].T)
        m["memT"] = np.ascontiguousarray(mem[b].T)
        in_maps.append(m)
    res = run_bass_kernel_spmd(nc, in_maps, core_ids=list(range(B)))
    out = np.stack([np.ascontiguousarray(res.results[b]["outT"].T) for b in range(B)])
    return out.astype(np.float32)
```

```python
import math
import os
import numpy as np
import concourse.bass as bass
import concourse.mybir as mybir
from concourse.bass_utils import run_bass_kernel_spmd

F32 = mybir.dt.float32
BF16 = mybir.dt.bfloat16
AF = mybir.ActivationFunctionType
ALU = mybir.AluOpType

D = 4096
NMEM = 256
REC = 3072
TB = 256
T8 = 8
EPS = 1e-6
PI = math.pi
PESKIP = os.environ.get('K_PESKIP', '1') == '1'
WCACHE = os.environ.get('K_WCACHE', '1') == '1'
ZSYNC = int(os.environ.get('K_ZSYNC', '1'))
YSYNC = int(os.environ.get('K_YSYNC', '2'))


class Res:
    __slots__ = ("w", "rs")

    def __init__(self):
        self.w = None
        self.rs = {}


class Op:
    __slots__ = ("eng", "fn", "deps", "is_dma", "chan", "seq", "signal", "g", "pesync")
    _n = 0

    def __init__(self, eng, fn, is_dma=False, chan=None):
        Op._n += 1
        self.g = Op._n
        self.pesync = False
        self.eng = eng
        self.fn = fn
        self.deps = []
        self.is_dma = is_dma
        self.chan = chan
        self.seq = None
        self.signal = False


class KB:
    ENGS = ("pe", "act", "dve", "pool", "sp")

    def __init__(self, nc):
        self.nc = nc
        self.streams = {e: [] for e in self.ENGS}
        self.esem = {e: nc.alloc_semaphore(name="s_" + e) for e in self.ENGS}
        self.chans = {}
        self.rmap = {}
        self.fence = []

    def R(self, *key):
        r = self.rmap.get(key)
        if r is None:
            r = self.rmap[key] = Res()
        return r

    def chan(self, name):
        if name not in self.chans:
            self.chans[name] = [self.nc.alloc_semaphore(name="c_" + name), 0]
        return name

    def _track(self, op, reads, writes):
        seen = set()
        for r in reads:
            if r.w is not None and id(r.w) not in seen:
                seen.add(id(r.w))
                op.deps.append(r.w)
        for w in writes:
            if w.w is not None and id(w.w) not in seen:
                seen.add(id(w.w))
                op.deps.append(w.w)
            for d in w.rs.values():
                if id(d) not in seen and d is not op:
                    seen.add(id(d))
                    op.deps.append(d)
        for r in reads:
            r.rs[(op.eng, op.chan)] = op
        for w in writes:
            w.w = op
            w.rs = {}

    def barrier(self):
        f = []
        for e in self.ENGS:
            for o in reversed(self.streams[e]):
                if not o.is_dma:
                    f.append(o)
                    break
        lastd = {}
        for e in self.ENGS:
            for o in self.streams[e]:
                if o.is_dma:
                    lastd[o.chan] = o
        f.extend(lastd.values())
        self.fence = f

    def op(self, eng, fn, reads=(), writes=()):
        o = Op(eng, fn)
        self._track(o, reads, writes)
        o.deps.extend(self.fence)
        self.streams[eng].append(o)
        return o

    def dma(self, queue, out, in_, reads=(), writes=(), chan=None):
        chan = self.chan(chan or ("q_" + queue))
        o = Op(queue, lambda e: e.dma_start(out=out, in_=in_), is_dma=True, chan=chan)
        self._track(o, reads, writes)
        o.deps.extend(self.fence)
        self.streams[queue].append(o)
        c = self.chans[chan]
        c[1] += 1
        o.seq = c[1]
        return o

    def emit(self, final_waits=()):
        nc = self.nc
        for e in self.ENGS:
            for o in self.streams[e]:
                for d in o.deps:
                    if not d.is_dma and not (PESKIP and e == "pe" and d.eng == "pe" and not o.pesync):
                        d.signal = True
        for e in self.ENGS:
            k = 0
            for o in self.streams[e]:
                if not o.is_dma and o.signal:
                    k += 1
                    o.seq = k
        kb = self

        def body(e, eng):
            waited = {}

            def dkey(d):
                if d.is_dma:
                    return ("c", d.chan), 16 * d.seq
                return ("e", d.eng), d.seq

            def wait_for(d):
                if d.is_dma:
                    sem, val, key = kb.chans[d.chan][0], 16 * d.seq, ("c", d.chan)
                else:
                    sem, val, key = kb.esem[d.eng], d.seq, ("e", d.eng)
                if waited.get(key, 0) >= val:
                    return
                waited[key] = val
                eng.wait_ge(sem, val)

            ops = kb.streams[e]
            LOOK = int(os.environ.get('K_LOOK', '80'))
            for idx, o in enumerate(ops):
                need = {}
                for d in o.deps:
                    if PESKIP and e == "pe" and d.eng == "pe" and not d.is_dma and not o.pesync:
                        continue
                    key, val = dkey(d)
                    if waited.get(key, 0) < val and need.get(key, (0, None))[0] < val:
                        need[key] = (val, d)
                if need:
                    for o2 in ops[idx + 1: idx + 1 + LOOK]:
                        for d in o2.deps:
                            if PESKIP and e == "pe" and d.eng == "pe" and not d.is_dma and not o2.pesync:
                                continue
                            key, val = dkey(d)
                            if key in need and d.g < o.g and val > need[key][0]:
                                need[key] = (val, d)
                    for key, (val, d) in need.items():
                        wait_for(d)
                ins = o.fn(eng)
                if o.is_dma:
                    ins.then_inc(kb.chans[o.chan][0], 16)
                elif o.signal:
                    ins.then_inc(kb.esem[e], 1)
            if e == "sp":
                for d in final_waits:
                    wait_for(d)

        with nc.Block() as block:
            @block.tensor
            def _(eng):
                body("pe", eng)

            @block.scalar
            def _(eng):
                body("act", eng)

            @block.vector
            def _(eng):
                body("dve", eng)

            @block.gpsimd
            def _(eng):
                body("pool", eng)

            @block.sync
            def _(eng):
                body("sp", eng)


def build(nc, NT, layers=(0, 1, 2, 3), dbg=None):
    NB = NT // TB
    NCH = TB // T8
    kb = KB(nc)
    R = kb.R

    def din(name, shape, dt=F32):
        return nc.dram_tensor(name, list(shape), dt, kind="ExternalInput").ap()

    xT = din("xT", [D, NT])
    memT = din("memT", [D, NMEM])
    w_in = din("w_in", [4, 64, 128, 32 * 128])
    w_out = din("w_out", [4, 32, 128, 32 * 128])
    w_kv = din("w_kv", [4, 16, 128, 32 * 128])
    w_glu = din("w_glu", [2, 24, 128, 24 * 128])
    gains_d = din("gains", [128, 3 * 4 * 32])
    s5_l2 = din("s5_l2", [2, 128, 3 * 96])
    s5_bl2 = din("s5_bl2", [2, 128, 2 * 96 * 32])
    s5_cl2 = din("s5_cl2", [2, 128, 2 * 96 * 32])
    s5_l1 = din("s5_l1", [2, 128, 3 * 24 * 128])
    s5_bl1 = din("s5_bl1", [2, 128, 2 * 24 * 128])
    s5_dg = din("s5_dg", [2, 128, 2 * 24])
    lru_v = din("lru_v", [2, 128, 24 * 8])
    lru_w = din("lru_w", [2, 12, 128, 2 * 2 * 256])
    outT = nc.dram_tensor("outT", [D, NT], F32, kind="ExternalOutput").ap()
    o_scr = nc.dram_tensor("o_scr", [D, TB], F32, kind="Internal").ap()
    zmat_d = nc.dram_tensor("zmat", [24, 128, 16 * 128], BF16, kind="Internal").ap()
    camat_d = nc.dram_tensor("camat", [128, 2 * 8 * 96 * 32], BF16, kind="Internal").ap()
    kmat_d = nc.dram_tensor("kmat", [24, 128, 8 * 128], BF16, kind="Internal").ap()

    wbf_in = nc.dram_tensor("wbf_in", [1, 64, 128, 32 * 128], BF16, kind="Internal").ap()
    wbf_out = nc.dram_tensor("wbf_out", [1, 32, 128, 32 * 128], BF16, kind="Internal").ap()
    wbf_glu = nc.dram_tensor("wbf_glu", [1, 24, 128, 24 * 128], BF16, kind="Internal").ap()

    def sb(name, shape, dt=F32):
        return nc.alloc_sbuf_tensor("sb_" + name, list(shape), dt)

    ones_bf = sb("ones_bf", [128, 128], BF16)
    gains = sb("gains", [128, 3, 4, 32])
    hnT = sb("hnT", [128, 32, TB], BF16)
    mixed = sb("mixed", [128, 32, TB], BF16)
    u = sb("u", [128, 24, 3 + TB], BF16)
    gsil = sb("gsil", [128, 24, TB], BF16)
    qT = sb("qT", [128, 8, TB], BF16)
    gqsil = sb("gqsil", [128, 8, TB], BF16)
    NWS = 3
    wb = [sb("wb%d" % i, [128, 32, 128], BF16) for i in range(NWS)]
    NHS = 4
    hs = [sb("hs%d" % i, [128, TB]) for i in range(NHS)]
    sqb = [sb("sqb%d" % i, [128, TB], BF16) for i in range(2)]
    rstd = sb("rstd", [128, TB])
    rstdA = sb("rstdA", [128, TB])
    rstd_m = sb("rstd_m", [128, NMEM])
    KT = sb("KT", [128, 8, NMEM], BF16)
    Vt = sb("Vt", [128, 2, 1024], BF16)
    expT = sb("expT", [128, 2, TB], BF16)
    rden = sb("rden", [128, TB])
    NTMP = 5
    tmpf = [sb("tmpf%d" % i, [128, TB]) for i in range(NTMP)]
    osb = [sb("osb%d" % i, [128, TB]) for i in range(2)]
    S = sb("S5S", [128, 192])
    AAB = sb("S5AAB", [128, 2, 192])
    sct = sb("s5sct", [128, 3, 192])
    dg = sb("s5dg", [128, 2, 24])
    hbias = sb("hbias", [128, 24])
    uq = sb("uq", [128, 2, 4, TB], BF16)
    lv = sb("lruv", [128, 24, 8])
    negc = sb("negc", [128, 24])
    hst = sb("hst", [128, 24])
    AFN = 9216
    ABN = 22528
    arF = sb("arenaF", [128, AFN])
    arB = sb("arenaB", [128, ABN], BF16)

    def carve(ar, off, shape):
        n = 1
        for d_ in shape:
            n *= d_
        v = ar[:, off:off + n]
        if len(shape) == 2:
            return v.rearrange("p (a b) -> p a b", a=shape[0])
        if len(shape) == 3:
            return v.rearrange("p (a b c) -> p a b c", a=shape[0], b=shape[1])
        if len(shape) == 4:
            return v.rearrange("p (a b c d) -> p a b c d", a=shape[0], b=shape[1], c=shape[2])
        return v

    Zsb = carve(arF, 0, [NCH, 192])
    Shist = carve(arB, 0, [NCH, 192])
    gT = carve(arB, 6144, [24, TB])
    zmat = [carve(arB, 12288 + i * 2048, [16, 128]) for i in range(2)]
    camat = [carve(arB, 16384 + i * 2048, [2, 8, 4, 32]) for i in range(2)]
    kmat = [carve(arB, 20480 + i * 1024, [8, 128]) for i in range(2)]
    xc = carve(arF, 0, [24, TB])
    a4 = carve(arF, 6144, [4, TB])
    g4 = carve(arF, 7168, [4, TB])
    v4 = carve(arF, 8192, [4, TB])
    xcb = carve(arB, 0, [24, TB])
    gw = [carve(arB, 6144 + i * 1024, [2, 2, 256]) for i in range(2)]

    ps = [nc.alloc_psum_tensor("ps%d" % i, [128, 512], F32) for i in range(8)]
    PS_SS = 0

    state = {"mm": 0, "w": 0, "tmp": 0, "z": 0, "y": 0, "hs": 0}

    def mm_bank():
        state["mm"] = (state["mm"] + 1) % 3
        return 1 + state["mm"]

    def z_bank():
        state["z"] = (state["z"] + 1) % 2
        return 4 + state["z"]

    def y_bank():
        state["y"] = (state["y"] + 1) % 2
        return 6 + state["y"]

    def tmp():
        state["tmp"] = (state["tmp"] + 1) % NTMP
        i = state["tmp"]
        return tmpf[i], R("tmpf", i)

    def hs_slot():
        state["hs"] = (state["hs"] + 1) % NHS
        i = state["hs"]
        return i, hs[i], R("hs", i)

    def V_tt(out, in0, in1, op, r, w):
        kb.op("dve", lambda e: e.tensor_tensor(out=out, in0=in0, in1=in1, op=op), r, w)

    def V_ts(out, in0, s1, s2, op0, op1, r, w):
        if op1 is None:
            kb.op("dve", lambda e: e.tensor_scalar(out=out, in0=in0, scalar1=s1, scalar2=None, op0=op0), r, w)
        else:
            kb.op("dve", lambda e: e.tensor_scalar(out=out, in0=in0, scalar1=s1, scalar2=s2, op0=op0, op1=op1), r, w)

    def V_stt(out, in0, scalar, in1, op0, op1, r, w):
        kb.op("dve", lambda e: e.scalar_tensor_tensor(out=out, in0=in0, scalar=scalar, in1=in1, op0=op0, op1=op1), r, w)

    def V_copy(out, in_, r, w):
        kb.op("dve", lambda e: e.tensor_copy(out=out, in_=in_), r, w)

    def V_recip(out, in_, r, w):
        kb.op("dve", lambda e: e.reciprocal(out=out, in_=in_), r, w)

    def V_memset(out, val, r, w):
        kb.op("dve", lambda e: e.memset(out, val), r, w)

    def A_act(out, in_, func, r, w, bias=None, scale=None):
        kw = {}
        if bias is not None:
            kw["bias"] = bias
        if scale is not None:
            kw["scale"] = scale
        kb.op("act", lambda e: e.activation(out=out, in_=in_, func=func, **kw), r, w)

    def MM(out, lhsT, rhs, start, stop, r, w, tp=None, sync=False):
        if tp is None:
            o = kb.op("pe", lambda e: e.matmul(out, lhsT=lhsT, rhs=rhs, start=start, stop=stop), r, w)
        else:
            o = kb.op("pe", lambda e: e.matmul(out, lhsT=lhsT, rhs=rhs, start=start, stop=stop, tile_position=tp), r, w)
        o.pesync = sync

    def PSR(b):
        return [R("ps", b, 0), R("ps", b, 1)]

    def load_w(kind, l, c, first=True):
        i = state["w"] = (state["w"] + 1) % NWS
        src32 = {"kv": w_kv, "in": w_in, "out": w_out}[kind][l, c] if kind != "glu" else w_glu[l // 2, c]
        n = src32.shape[-1] // 128
        if kind == "kv":
            kb.dma("pool", wb[i][:, 0:n, :], src32.rearrange("p (k c) -> p k c", c=128), writes=[R("wb", i)], chan="wb%d" % i)
            return wb[i], R("wb", i)
        dcopy = {"in": wbf_in, "glu": wbf_glu, "out": wbf_out}[kind][0, c]
        if not WCACHE:
            kb.dma("pool", wb[i][:, 0:n, :], src32.rearrange("p (k c) -> p k c", c=128), writes=[R("wb", i)], chan="wb%d" % i)
        elif first:
            kb.dma("pool", wb[i][:, 0:n, :], src32.rearrange("p (k c) -> p k c", c=128), writes=[R("wb", i)], chan="wb%d" % i)
            kb.dma("sp", dcopy.rearrange("p (k c) -> p k c", c=128), wb[i][:, 0:n, :], reads=[R("wb", i)],
                   writes=[R("wbf", kind, c)], chan="wst%d" % i)
        else:
            kb.dma("pool", wb[i][:, 0:n, :], dcopy.rearrange("p (k c) -> p k c", c=128), reads=[R("wbf", kind, c)],
                   writes=[R("wb", i)], chan="wb%d" % i)
        return wb[i], R("wb", i)

    def rsqrt_mean(dst, rdst, psap, rps):
        V_ts(dst, psap, 1.0 / D, EPS, ALU.mult, ALU.add, rps, [rdst])
        A_act(dst, dst, AF.Sqrt, [rdst], [rdst])
        V_recip(dst, dst, [rdst], [rdst])

    V_memset(ones_bf[:], 1.0, (), [R("ones")])
    V_memset(uq[:], 0.0, (), [R("uq", 0), R("uq", 1)])
    kb.dma("sp", gains[:], gains_d.rearrange("p (a l k) -> p a l k", a=3, l=4), writes=[R("gains")], chan="misc")

    def silu_from_psum(out_bf, psap, rps, wres):
        t, rt = tmp()
        A_act(t[:], psap, AF.Tanh, rps, [rt], scale=0.5)
        V_ts(t[:], t[:], 0.5, 0.5, ALU.mult, ALU.add, [rt], [rt])
        V_tt(out_bf, psap, t[:], ALU.mult, rps + [rt], [wres])

    mem_state = {"rstd": False}

    def phase_kv(l):
        if not mem_state["rstd"]:
            for kt in range(32):
                si, ht, rh = hs_slot()
                kb.dma("sp", ht[:], memT[kt * 128:(kt + 1) * 128, :], writes=[rh], chan="hs%d" % si)
                i = kt % 2
                A_act(sqb[i][:], ht[:], AF.Square, [rh], [R("sqb", i)])
                MM(ps[PS_SS][:, 0:NMEM], ones_bf[:], sqb[i][:], kt == 0, kt == 31, [R("ones"), R("sqb", i)], [*PSR(PS_SS)])
            rsqrt_mean(rstd_m[:], R("rstd_m"), ps[PS_SS][:, 0:NMEM], PSR(PS_SS))
            mem_state["rstd"] = True
        for kt in range(32):
            si, ht, rh = hs_slot()
            kb.dma("sp", ht[:], memT[kt * 128:(kt + 1) * 128, :], writes=[rh], chan="hs%d" % si)
            V_stt(hnT[:, kt, :], ht[:], gains[:, 2, l, kt:kt + 1], rstd_m[:], ALU.mult, ALU.mult,
                  [rh, R("gains"), R("rstd_m")], [R("hnT", kt)])
        for c in range(16):
            wt, rw = load_w("kv", l, c)
            b = mm_bank()
            if c < 8:
                for kt in range(32):
                    MM(ps[b][:, 0:NMEM], wt[:, kt, :], hnT[:, kt, :], kt == 0, kt == 31, [rw, R("hnT", kt)], [*PSR(b)])
                A_act(KT[:, c, :], ps[b][:, 0:NMEM], AF.Copy, [*PSR(b)], [R("KT")])
            else:
                cv = c - 8
                for j in range(2):
                    for kt in range(32):
                        MM(ps[b][:, j * 128:(j + 1) * 128], hnT[:, kt, j * 128:(j + 1) * 128], wt[:, kt, :], kt == 0, kt == 31,
                           [rw, R("hnT", kt)], [*PSR(b)])
                A_act(Vt[:, :, cv * 128:(cv + 1) * 128], ps[b][:, 0:256].rearrange("p (j c) -> p j c", j=2), AF.Copy,
                      [*PSR(b)], [R("Vt")])

    PN = 384
    p2 = [carve(arF, i * PN, [1, PN])[:, 0, :] for i in range(8)]
    pin = arF[:, 8 * PN:11 * PN]
    PSCR = 11 * PN

    def cplx_disc(n, lre, lim, ls):
        ar, ai, cre, cim, t0, t1, t2, t3 = [p2[i][:, 0:n] for i in range(8)]
        rr = [R("prep", i) for i in range(8)]
        rin = [R("prepin")]
        A_act(t0, ls, AF.Exp, rin, [rr[4]])
        V_ts(t1, lre, -1e-4, None, ALU.min, None, rin, [rr[5]])
        V_tt(t2, t1, t0, ALU.mult, [rr[4], rr[5]], [rr[6]])
        A_act(t2, t2, AF.Exp, [rr[6]], [rr[6]])
        V_tt(t3, lim, t0, ALU.mult, rin + [rr[4]], [rr[7]])
        A_act(ai, t3, AF.Sin, [rr[7]], [rr[1]], scale=0.125)
        A_act(ar, t3, AF.Sin, [rr[7]], [rr[0]], scale=-0.125, bias=PI / 2)
        for _ in range(3):
            V_tt(t3, ar, ai, ALU.mult, [rr[0], rr[1]], [rr[7]])
            V_tt(ar, ar, ar, ALU.mult, [rr[0]], [rr[0]])
            V_tt(ai, ai, ai, ALU.mult, [rr[1]], [rr[1]])
            V_tt(ar, ar, ai, ALU.subtract, [rr[0], rr[1]], [rr[0]])
            V_ts(ai, t3, 2.0, None, ALU.mult, None, [rr[7]], [rr[1]])
        V_tt(ar, ar, t2, ALU.mult, [rr[0], rr[6]], [rr[0]])
        V_tt(ai, ai, t2, ALU.mult, [rr[1], rr[6]], [rr[1]])
        V_tt(t0, t1, t1, ALU.mult, [rr[5]], [rr[4]])
        V_tt(t2, lim, lim, ALU.mult, rin, [rr[6]])
        V_tt(t0, t0, t2, ALU.add, [rr[4], rr[6]], [rr[4]])
        V_recip(t0, t0, [rr[4]], [rr[4]])
        V_ts(t2, ar, -1.0, None, ALU.add, None, [rr[0]], [rr[6]])
        V_tt(cre, t2, t1, ALU.mult, [rr[6], rr[5]], [rr[2]])
        V_tt(t3, ai, lim, ALU.mult, [rr[1]] + rin, [rr[7]])
        V_tt(cre, cre, t3, ALU.add, [rr[2], rr[7]], [rr[2]])
        V_tt(cre, cre, t0, ALU.mult, [rr[2], rr[4]], [rr[2]])
        V_tt(cim, ai, t1, ALU.mult, [rr[1], rr[5]], [rr[3]])
        V_tt(t3, t2, lim, ALU.mult, [rr[6]] + rin, [rr[7]])
        V_tt(cim, cim, t3, ALU.subtract, [rr[3], rr[7]], [rr[3]])
        V_tt(cim, cim, t0, ALU.mult, [rr[3], rr[4]], [rr[3]])
        return ar, ai, cre, cim

    def cmul(ore, oim, are, aim, bre, bim, t0, t1, r, w):
        V_tt(t0, are, bre, ALU.mult, r + w, w)
        V_tt(t1, aim, bim, ALU.mult, r + w, w)
        V_tt(t0, t0, t1, ALU.subtract, w, w)
        V_tt(t1, are, bim, ALU.mult, r + w, w)
        V_tt(oim, aim, bre, ALU.mult, r + w, w)
        V_tt(oim, oim, t1, ALU.add, w, w)
        V_copy(ore, t0, w, w)

    def s5_prep(j):
        RX = R("prepbig")
        RB = R("prepbf")
        RK = R("prepK")
        r03 = [R("prep", 0), R("prep", 1), R("prep", 2), R("prep", 3)]
        wp = [R("prep", 4), R("prep", 5), R("prep", 6), R("prep", 7)]
        n2 = 96
        kb.dma("sp", pin[:, 0:3 * n2], s5_l2[j], writes=[R("prepin")], chan="misc")
        ar, ai, cre, cim = cplx_disc(n2, pin[:, 0:n2], pin[:, n2:2 * n2], pin[:, 2 * n2:3 * n2])
        p8r, p8i, t6, t7 = p2[4][:, 0:n2], p2[5][:, 0:n2], p2[6][:, 0:n2], p2[7][:, 0:n2]
        V_copy(p8r, ar, r03, wp)
        V_copy(p8i, ai, r03, wp)
        for _ in range(3):
            cmul(p8r, p8i, p8r, p8i, p8r, p8i, t6, t7, [], wp)
        V_copy(AAB[:, 0, 0:96], p8r, wp, [R("AAB")])
        V_copy(AAB[:, 0, 96:192], p8r, wp, [R("AAB")])
        V_ts(AAB[:, 1, 0:96], p8i, -1.0, None, ALU.mult, None, wp, [R("AAB")])
        V_copy(AAB[:, 1, 96:192], p8i, wp, [R("AAB")])
        NQ = 24
        Bf = carve(arF, PSCR, [2, NQ, 32])
        Cf = carve(arF, PSCR + 1536, [2, NQ, 32])
        T0 = carve(arF, PSCR + 3072, [NQ, 32])
        T1 = carve(arF, PSCR + 3840, [NQ, 32])
        Bb = carve(arB, 0, [2, NQ, 32])
        CAb = carve(arB, 1536, [2, NQ, 32])
        Kst = carve(arB, 3072, [6, 128])
        bl2 = s5_bl2[j].rearrange("p (a b c) -> p a b c", a=2, b=96)
        cl2 = s5_cl2[j].rearrange("p (a b c) -> p a b c", a=2, b=96)
        km_v = kmat_d.rearrange("t p (l c) -> p t l c", l=8)
        for qt in range(4):
            kb.dma("sp", Bf, bl2[:, :, qt * NQ:(qt + 1) * NQ, :], writes=[RX], chan="misc")
            kb.dma("sp", Cf, cl2[:, :, qt * NQ:(qt + 1) * NQ, :], writes=[RX], chan="misc")

            def bc(x):
                return x[:, qt * NQ:(qt + 1) * NQ].unsqueeze(2).to_broadcast([128, NQ, 32])

            cmul(Bf[:, 0], Bf[:, 1], bc(cre), bc(cim), Bf[:, 0], Bf[:, 1], T0, T1, r03, [RX])
            V_copy(Bb[:, 0], Bf[:, 0], [RX], [RB])
            V_copy(Bb[:, 1], Bf[:, 1], [RX], [RB])
            for jj in range(9):
                V_copy(CAb[:, 0], Cf[:, 0], [RX], [RB])
                V_ts(CAb[:, 1], Cf[:, 1], -1.0, None, ALU.mult, None, [RX], [RB])
                if jj >= 1:
                    for ri in range(2):
                        o0 = ((ri * 8 + jj - 1) * 96 + qt * NQ) * 32
                        kb.dma("sp", camat_d[:, o0:o0 + NQ * 32], CAb[:, ri].rearrange("p a c -> p (a c)"), reads=[RB],
                               writes=[R("camat_d")], chan="prepst")
                if jj <= 7:
                    V_memset(Kst, 0.0, [RK], [RK])
                    for tg in range(2):
                        b = z_bank()
                        for tl in range(3):
                            for q in range(4):
                                pr = (tg * 3 + tl) * 4 + q
                                for ri in range(2):
                                    MM(ps[b][32 * q:32 * q + 32, tl * 128 + 32 * q: tl * 128 + 32 * q + 32],
                                       Bb[:, ri, pr, :], CAb[:, ri, pr, :], ri == 0, ri == 1, [RB], [*PSR(b)], tp=(0, 32 * q), sync=True)
                        for q in range(4):
                            src = ps[b][32 * q:32 * q + 32, 0:384].rearrange("p (t c) -> p t c", t=3)[:, :, 32 * q:32 * q + 32]
                            dst = Kst[32 * q:32 * q + 32, tg * 3:tg * 3 + 3, 32 * q:32 * q + 32]
                            A_act(dst, src, AF.Copy, [*PSR(b)], [RK])
                    kb.dma("sp", km_v[:, qt * 6:(qt + 1) * 6, jj, :], Kst, reads=[RK], writes=[R("kmat_d")], chan="prepst")
                    cmul(Cf[:, 0], Cf[:, 1], bc(ar), bc(ai), Cf[:, 0], Cf[:, 1], T0, T1, r03, [RX])
        n1 = 3 * 128
        l1v = s5_l1[j].rearrange("p (a n) -> p a n", a=3)
        b1v = s5_bl1[j].rearrange("p (a n) -> p a n", a=2)
        zm_v = zmat_d.rearrange("t p (k c) -> p t k c", k=16)
        Wr = arF[:, PSCR:PSCR + n1]
        Wi = arF[:, PSCR + n1:PSCR + 2 * n1]
        U0 = arF[:, PSCR + 2 * n1:PSCR + 3 * n1]
        U1 = arF[:, PSCR + 3 * n1:PSCR + 4 * n1]
        Zst = carve(arB, 4096, [3, 16, 128])
        for ch in range(8):
            kb.dma("sp", pin[:, 0:3 * n1].rearrange("p (a n) -> p a n", a=3), l1v[:, :, ch * n1:(ch + 1) * n1],
                   writes=[R("prepin")], chan="misc")
            ar, ai, cre, cim = cplx_disc(n1, pin[:, 0:n1], pin[:, n1:2 * n1], pin[:, 2 * n1:3 * n1])
            kb.dma("sp", arF[:, PSCR:PSCR + 2 * n1].rearrange("p (a n) -> p a n", a=2), b1v[:, :, ch * n1:(ch + 1) * n1],
                   writes=[RX], chan="misc")
            cmul(Wr, Wi, cre, cim, Wr, Wi, U0, U1, r03, [RX])
            for k in range(7, -1, -1):
                V_copy(Zst[:, :, 2 * k, :], Wr.rearrange("p (t c) -> p t c", t=3), [RX], [RB])
                V_copy(Zst[:, :, 2 * k + 1, :], Wi.rearrange("p (t c) -> p t c", t=3), [RX], [RB])
                if k > 0:
                    cmul(Wr, Wi, ar, ai, Wr, Wi, U0, U1, r03, [RX])
            kb.dma("sp", zm_v[:, ch * 3:(ch + 1) * 3], Zst, reads=[RB], writes=[R("zmat_d")], chan="prepst")
        kb.dma("sp", dg[:], s5_dg[j].rearrange("p (a t) -> p a t", a=2), writes=[R("dg")], chan="misc")
        V_ts(hbias[:], dg[:, 1, :], 0.5, None, ALU.mult, None, [R("dg")], [R("hb")])
        V_memset(S[:], 0.0, [R("S")], [R("S")])

    def s5_block(j, blk):
        RZ = R("Zsb")
        RS = R("S")
        RH = R("Shist")
        RT = R("sct")
        for t in range(24):
            i = t % 2
            par = t % 2
            kb.dma("sp", zmat[i], zmat_d[t].rearrange("p (k c) -> p k c", k=16), reads=[R("zmat_d")], writes=[R("zmat", i)],
                   chan="zm%d" % i)
            for q in range(4):
                A_act(uq[32 * q:32 * q + 32, par, q, :], u[32 * q:32 * q + 32, t, 3:3 + TB], AF.Copy, [R("u", t)], [R("uq", par)])
            b = z_bank()
            for q in range(4):
                uv = uq[:, par, q, :].rearrange("p (n k) -> p k n", k=T8)
                for ri in range(2):
                    for k in range(T8):
                        MM(ps[b][:, (ri * 4 + q) * NCH:(ri * 4 + q + 1) * NCH], zmat[i][:, 2 * k + ri, :],
                           uv[:, k, :], k == 0, k == T8 - 1, [R("zmat", i), R("uq", par)], [*PSR(b)])
            dst = Zsb.rearrange("p n (r g) -> p n r g", r=2)[:, :, :, 4 * t:4 * t + 4]
            src = ps[b][:, 0:8 * NCH].rearrange("p (r q n) -> p n r q", r=2, q=4)
            A_act(dst, src, AF.Copy, [*PSR(b)], [RZ])
        for n in range(NCH):
            A_act(Shist[:, n, :], S[:], AF.Copy, [RS], [RH])
            V_tt(sct[:, 0, :], AAB[:, 0, :], S[:], ALU.mult, [RS, R("AAB")], [RT])
            V_tt(sct[:, 1, 0:96], AAB[:, 1, 0:96], S[:, 96:192], ALU.mult, [RS, R("AAB")], [RT])
            V_tt(sct[:, 1, 96:192], AAB[:, 1, 96:192], S[:, 0:96], ALU.mult, [RS, R("AAB")], [RT])
            V_tt(sct[:, 0, :], sct[:, 0, :], sct[:, 1, :], ALU.add, [RT], [RT])
            V_tt(S[:], sct[:, 0, :], Zsb[:, n, :], ALU.add, [RT, RZ], [RS])
        for t in range(24):
            i = t % 2
            kb.dma("sp", camat[i].rearrange("p r k q c -> p (r k) (q c)"),
                   camat_d.rearrange("p (rk pr c) -> p rk pr c", rk=16, pr=96)[:, :, 4 * t:4 * t + 4, :].rearrange("p a q c -> p a (q c)"),
                   reads=[R("camat_d")], writes=[R("camat", i)], chan="cm%d" % i)
            kb.dma("sp", kmat[i], kmat_d[t].rearrange("p (l c) -> p l c", l=8), reads=[R("kmat_d")], writes=[R("kmat", i)],
                   chan="km%d" % i)
            b = y_bank()
            uv = u[:, t, 3:3 + TB].rearrange("p (n k) -> p k n", k=T8)
            for k in range(T8):
                yo = ps[b][:, k * NCH:(k + 1) * NCH]
                for l in range(k + 1):
                    MM(yo, kmat[i][:, l, :], uv[:, k - l, :], l == 0, False, [R("kmat", i), R("u", t)], [*PSR(b)])
                for ri in range(2):
                    for q in range(4):
                        MM(ps[b][32 * q:32 * q + 32, k * NCH:(k + 1) * NCH], camat[i][:, ri, k, q, :],
                           Shist[:, :, ri * 96 + 4 * t + q], False, (ri == 1), [R("camat", i), RH], [*PSR(b)], tp=(0, 32 * q),
                           sync=(YSYNC == 1 or (YSYNC == 2 and q == 0)))
            yt, ry = tmp()
            V_stt(yt[:].rearrange("p (n k) -> p n k", k=T8), u[:, t, 3:3 + TB].rearrange("p (n k) -> p n k", k=T8), dg[:, 0, t:t + 1],
                  ps[b][:, 0:TB].rearrange("p (k n) -> p n k", k=T8), ALU.mult, ALU.add, [R("u", t), R("dg"), *PSR(b)], [ry])
            t2, r2 = tmp()
            V_tt(t2[:], yt[:], yt[:], ALU.mult, [ry], [r2])
            V_ts(t2[:], t2[:], 0.044715, 1.0, ALU.mult, ALU.add, [r2], [r2])
            V_tt(t2[:], t2[:], yt[:], ALU.mult, [r2, ry], [r2])
            A_act(t2[:], t2[:], AF.Tanh, [r2], [r2], scale=math.sqrt(2.0 / PI))
            V_ts(t2[:], t2[:], 0.5, 0.5, ALU.mult, ALU.add, [r2], [r2])
            V_tt(gT[:, t, :], t2[:], yt[:], ALU.mult, [r2, ry], [R("gT", t)])
        for c in range(24):
            wt, rw = load_w("glu", 2 * j, c, blk == 0)
            b = mm_bank()
            for kt in range(24):
                MM(ps[b][:, 0:TB], wt[:, kt, :], gT[:, kt, :], kt == 0, kt == 23, [rw, R("gT", kt)], [*PSR(b)])
            t2, r2 = tmp()
            A_act(t2[:], ps[b][:, 0:TB], AF.Tanh, [*PSR(b), R("hb")], [r2], scale=0.5, bias=hbias[:, c:c + 1])
            V_ts(t2[:], t2[:], 0.5, 0.5, ALU.mult, ALU.add, [r2], [r2])
            V_tt(t2[:], t2[:], gT[:, c, :], ALU.mult, [r2, R("gT", c)], [r2])
            V_tt(mixed[:, c, :], t2[:], gsil[:, c, :], ALU.mult, [r2, R("gsil", c)], [R("mixed", c)])

    def lru_prep(j):
        kb.dma("sp", lv[:], lru_v[j].rearrange("p (t k) -> p t k", k=8), writes=[R("lv")], chan="misc")
        A_act(negc[:], lv[:, :, 7], AF.Exp, [R("lv")], [R("negc")], scale=-1.0)
        A_act(negc[:], negc[:], AF.Ln, [R("negc")], [R("negc")], bias=1.0)
        V_ts(negc[:], negc[:], -8.0, None, ALU.mult, None, [R("negc")], [R("negc")])
        V_ts(lv[:, :, 5:7], lv[:, :, 5:7], 0.5, None, ALU.mult, None, [R("lv")], [R("lv")])
        V_memset(hst[:], 0.0, [R("hst")], [R("hst")])
        V_memset(u[:, :, 0:3], 0.0, [R("utail")], [R("utail")])

    def lru_block(j, blk):
        for t in range(24):
            ru = [R("u", t), R("utail"), R("lv")]
            V_ts(xc[:, t, :], u[:, t, 3:3 + TB], lv[:, t, 3:4], lv[:, t, 4:5], ALU.mult, ALU.add, ru, [R("xc", t)])
            for jj in range(3):
                V_stt(xc[:, t, :], u[:, t, jj:jj + TB], lv[:, t, jj:jj + 1], xc[:, t, :], ALU.mult, ALU.add, ru + [R("xc", t)], [R("xc", t)])
            A_act(xcb[:, t, :], xc[:, t, :], AF.Copy, [R("xc", t)], [R("xcb", t)])
        V_copy(u[:, :, 0:3], u[:, :, TB:TB + 3], [R("u", t) for t in range(24)] + [R("xc", t) for t in range(24)], [R("utail")])
        for h2 in range(6):
            R4 = R("lru4")
            for hh in range(2):
                h = 2 * h2 + hh
                i = h % 2
                kb.dma("pool", gw[i], lru_w[j, h].rearrange("p (a i j) -> p a i j", a=2, i=2), writes=[R("gw", i)], chan="gw%d" % i)
                for j2 in range(2):
                    t = 2 * h + j2
                    k4 = 2 * hh + j2
                    b = z_bank()
                    for ax in range(2):
                        for i2 in range(2):
                            MM(ps[b][:, ax * TB:(ax + 1) * TB], gw[i][:, ax, i2, j2 * 128:(j2 + 1) * 128], xcb[:, 2 * h + i2, :],
                               i2 == 0, i2 == 1, [R("gw", i), R("xcb", 2 * h + i2)], [*PSR(b)])
                    r_, rr_ = tmp()
                    A_act(r_[:], ps[b][:, 0:TB], AF.Tanh, [*PSR(b), R("lv")], [rr_], scale=0.5, bias=lv[:, t, 5:6])
                    A_act(g4[:, k4, :], ps[b][:, TB:2 * TB], AF.Tanh, [*PSR(b), R("lv")], [R4], scale=0.5, bias=lv[:, t, 6:7])
                    V_ts(r_[:], r_[:], 0.5, 0.5, ALU.mult, ALU.add, [rr_], [rr_])
                    V_ts(g4[:, k4, :], g4[:, k4, :], 0.5, 0.5, ALU.mult, ALU.add, [R4], [R4])
                    A_act(a4[:, k4, :], r_[:], AF.Exp, [rr_, R("negc")], [R4], scale=negc[:, t:t + 1])
                    V_tt(v4[:, k4, :], a4[:, k4, :], a4[:, k4, :], ALU.mult, [R4], [R4])
                    V_ts(v4[:, k4, :], v4[:, k4, :], -1.0, 1.0, ALU.mult, ALU.add, [R4], [R4])
                    V_tt(g4[:, k4, :], g4[:, k4, :], xc[:, t, :], ALU.mult, [R4, R("xc", t)], [R4])
            A_act(v4, v4, AF.Sqrt, [R4], [R4])
            if blk == 0:
                V_memset(v4[:, :, 0:1], 1.0, [R4], [R4])
            V_tt(g4, g4, v4, ALU.mult, [R4], [R4])
            for k4 in range(4):
                t = 4 * h2 + k4
                kb.op("dve", lambda e, k4=k4, t=t: e.tensor_tensor_scan(out=v4[:, k4, :], data0=a4[:, k4, :], data1=g4[:, k4, :],
                                                                         initial=hst[:, t:t + 1], op0=ALU.mult, op1=ALU.add),
                      [R4, R("hst")], [R4])
                V_copy(hst[:, t:t + 1], v4[:, k4, TB - 1:TB], [R4], [R("hst")])
                V_tt(mixed[:, t, :], v4[:, k4, :], gsil[:, t, :], ALU.mult, [R4, R("gsil", t)], [R("mixed", t)])

    def phase_A(l, li, src, blk):
        tok = slice(blk * TB, (blk + 1) * TB)
        for kt in range(32):
            rsrc = [] if li == 0 else [R("outT", blk, kt)]
            si, ht, rh = hs_slot()
            kb.dma("sp", ht[:], src[kt * 128:(kt + 1) * 128, tok], reads=rsrc, writes=[rh], chan="hs%d" % si)
            i = kt % 2
            A_act(sqb[i][:], ht[:], AF.Square, [rh], [R("sqb", i)])
            MM(ps[PS_SS][:, 0:TB], ones_bf[:], sqb[i][:], kt == 0, kt == 31, [R("ones"), R("sqb", i)], [*PSR(PS_SS)])
        rsqrt_mean(rstdA[:], R("rstdA"), ps[PS_SS][:, 0:TB], PSR(PS_SS))
        for kt in range(32):
            rsrc = [] if li == 0 else [R("outT", blk, kt)]
            si, ht, rh = hs_slot()
            kb.dma("sp", ht[:], src[kt * 128:(kt + 1) * 128, tok], reads=rsrc, writes=[rh], chan="hs%d" % si)
            V_stt(hnT[:, kt, :], ht[:], gains[:, 0, l, kt:kt + 1], rstdA[:], ALU.mult, ALU.mult,
                  [rh, R("gains"), R("rstdA")], [R("hnT", kt)])

    last_stores = []
    V_memset(u[:, :, 0:3], 0.0, (), [R("utail")])
    for li, l in enumerate(layers):
        is_s5 = (l % 2 == 0)
        j = l // 2
        src = xT if li == 0 else outT
        kb.barrier()
        phase_kv(l)
        if is_s5:
            s5_prep(j)
        else:
            lru_prep(j)
        kb.barrier()
        for blk in range(NB):
            tok = slice(blk * TB, (blk + 1) * TB)
            if blk == 0:
                phase_A(l, li, src, blk)
            for c in range(64):
                wt, rw = load_w("in", l, c, blk == 0)
                b = mm_bank()
                for kt in range(32):
                    MM(ps[b][:, 0:TB], wt[:, kt, :], hnT[:, kt, :], kt == 0, kt == 31, [rw, R("hnT", kt)], [*PSR(b)])
                if c < 24:
                    A_act(u[:, c, 3:3 + TB], ps[b][:, 0:TB], AF.Copy, [*PSR(b)], [R("u", c)])
                elif c < 48:
                    silu_from_psum(gsil[:, c - 24, :], ps[b][:, 0:TB], PSR(b), R("gsil", c - 24))
                elif c < 56:
                    A_act(qT[:, c - 48, :], ps[b][:, 0:TB], AF.Copy, [*PSR(b)], [R("qT", c - 48)])
                else:
                    silu_from_psum(gqsil[:, c - 56, :], ps[b][:, 0:TB], PSR(b), R("gqsil", c - 56))
            if is_s5:
                s5_block(j, blk)
            else:
                lru_block(j, blk)
            for hd in range(4):
                for jn in range(2):
                    b = z_bank()
                    for dt_ in range(2):
                        MM(ps[b][:, 0:TB], KT[:, 2 * hd + dt_, jn * 128:(jn + 1) * 128], qT[:, 2 * hd + dt_, :], dt_ == 0, dt_ == 1,
                           [R("KT"), R("qT", 2 * hd + dt_)], [*PSR(b)])
                    A_act(expT[:, jn, :], ps[b][:, 0:TB], AF.Exp, [*PSR(b)], [R("expT", jn)], scale=1.0 / 16.0)
                for jn in range(2):
                    MM(ps[PS_SS][:, 0:TB], ones_bf[:], expT[:, jn, :], jn == 0, jn == 1, [R("ones"), R("expT", jn)], [*PSR(PS_SS)])
                V_recip(rden[:], ps[PS_SS][:, 0:TB], [*PSR(PS_SS)], [R("rden")])
                for dt_ in range(2):
                    b = y_bank()
                    c = 2 * hd + dt_
                    for jn in range(2):
                        MM(ps[b][:, 0:TB], Vt[:, jn, c * 128:(c + 1) * 128], expT[:, jn, :], jn == 0, jn == 1,
                           [R("Vt"), R("expT", jn)], [*PSR(b)])
                    t2, r2 = tmp()
                    V_tt(t2[:], ps[b][:, 0:TB], rden[:], ALU.mult, [*PSR(b), R("rden")], [r2])
                    V_tt(mixed[:, 24 + c, :], t2[:], gqsil[:, c, :], ALU.mult, [r2, R("gqsil", c)], [R("mixed", 24 + c)])
            if blk + 1 < NB:
                phase_A(l, li, src, blk + 1)
            for c in range(32):
                wt, rw = load_w("out", l, c, blk == 0)
                b = mm_bank()
                for kt in range(32):
                    MM(ps[b][:, 0:TB], wt[:, kt, :], mixed[:, kt, :], kt == 0, kt == 31, [rw, R("mixed", kt)], [*PSR(b)])
                i = c % 2
                A_act(osb[i][:], ps[b][:, 0:TB], AF.Copy, [*PSR(b)], [R("osb", i)])
                A_act(sqb[i][:], ps[b][:, 0:TB], AF.Square, [*PSR(b)], [R("sqb", i)])
                MM(ps[PS_SS][:, 0:TB], ones_bf[:], sqb[i][:], c == 0, c == 31, [R("ones"), R("sqb", i)], [*PSR(PS_SS)])
                kb.dma("sp", o_scr[c * 128:(c + 1) * 128, :], osb[i][:], reads=[R("osb", i)], writes=[R("o_scr", c)], chan="ost%d" % i)
            rsqrt_mean(rstd[:], R("rstd"), ps[PS_SS][:, 0:TB], PSR(PS_SS))
            for c in range(32):
                i = c % 2
                rsrc = [] if li == 0 else [R("outT", blk, c)]
                si, ht, rh = hs_slot()
                kb.dma("sp", ht[:], src[c * 128:(c + 1) * 128, tok], reads=rsrc, writes=[rh], chan="hs%d" % si)
                kb.dma("sp", osb[i][:], o_scr[c * 128:(c + 1) * 128, :], reads=[R("o_scr", c)], writes=[R("osb", i)], chan="old%d" % i)
                V_stt(osb[i][:], osb[i][:], gains[:, 1, l, c:c + 1], rstd[:], ALU.mult, ALU.mult, [R("osb", i), R("gains"), R("rstd")],
                      [R("osb", i)])
                V_tt(ht[:], ht[:], osb[i][:], ALU.add, [rh, R("osb", i)], [rh])
                st = kb.dma("sp", outT[c * 128:(c + 1) * 128, tok], ht[:], reads=[rh], writes=[R("outT", blk, c)], chan="hst")
                if li == len(layers) - 1:
                    last_stores.append(st)
    kb.emit(final_waits=last_stores)
    return nc


def _panel(w, ktn):
    K, C = w.shape
    a = w.reshape(ktn, 128, C // 128, 128)
    return np.ascontiguousarray(a.transpose(2, 1, 0, 3)).reshape(C // 128, 128, ktn * 128)


def _chan_major(v):
    n = v.shape[-1] // 128
    a = v.reshape(v.shape[:-1] + (n, 128))
    return np.ascontiguousarray(np.moveaxis(a, -1, 0))


def prep_shared(inp):
    f = np.float32
    sh = {}
    sh["w_in"] = np.stack([_panel(np.asarray(inp["w_in"][l], f), 32) for l in range(4)])
    sh["w_out"] = np.stack([_panel(np.asarray(inp["w_out"][l], f), 32) for l in range(4)])
    sh["w_kv"] = np.stack([_panel(np.asarray(inp["w_kv"][l], f), 32) for l in range(4)])
    sh["w_glu"] = np.stack([_panel(np.asarray(inp["s5_w_glu"][j], f), 24) for j in range(2)])
    g = np.stack([_chan_major(np.asarray(inp[k], f)) for k in ("pre_norm", "post_norm", "mem_norm")], axis=1)
    sh["gains"] = np.ascontiguousarray(g).reshape(128, 3 * 4 * 32)
    lre = np.asarray(inp["s5_lam_re"], f)
    lim = np.asarray(inp["s5_lam_im"], f)
    ls = np.asarray(inp["s5_log_step"], f)
    bre = np.asarray(inp["s5_b_re"], f)
    bim = np.asarray(inp["s5_b_im"], f)
    cre = np.asarray(inp["s5_c_re"], f)
    cim = np.asarray(inp["s5_c_im"], f)

    def l2(v):
        return v.reshape(96, 2, 64).transpose(1, 2, 0).reshape(128, 96)

    def l1(v):
        a = v.reshape(24, 4, 2, 64)
        a = np.broadcast_to(a[:, :, None, :, :], (24, 4, 32, 2, 64))
        return np.ascontiguousarray(a.transpose(1, 2, 0, 3, 4)).reshape(128, 24 * 128)

    s5_l2, s5_bl2, s5_cl2, s5_l1, s5_bl1, s5_dg = [], [], [], [], [], []
    for j in range(2):
        lsb = np.broadcast_to(ls[j][:, None], (192, 64))
        s5_l2.append(np.stack([l2(lre[j]), l2(lim[j]), l2(lsb)], axis=1).reshape(128, 3 * 96))
        s5_l1.append(np.stack([l1(lre[j]), l1(lim[j]), l1(lsb)], axis=1).reshape(128, 3 * 24 * 128))

        def bl2(b):
            o = np.zeros((2, 64, 96, 2, 16), f)
            bb = b.reshape(96, 2, 64, 16)
            for g2 in range(2):
                o[g2, :, :, g2, :] = bb[:, g2].transpose(1, 0, 2)
            return o.reshape(128, 96 * 32)

        def cl2(c):
            return bl2(np.ascontiguousarray(c.transpose(0, 2, 1)))

        def bl1(b):
            o = np.zeros((4, 2, 16, 24, 2, 64), f)
            bb = b.reshape(24, 4, 2, 64, 16)
            for g2 in range(2):
                o[:, g2, :, :, g2, :] = bb[:, :, g2].transpose(1, 3, 0, 2)
            return o.reshape(128, 24 * 128)

        s5_bl2.append(np.stack([bl2(bre[j]), bl2(bim[j])], axis=1).reshape(128, -1))
        s5_cl2.append(np.stack([cl2(cre[j]), cl2(cim[j])], axis=1).reshape(128, -1))
        s5_bl1.append(np.stack([bl1(bre[j]), bl1(bim[j])], axis=1).reshape(128, -1))
        s5_dg.append(np.stack([_chan_major(np.asarray(inp["s5_d"][j], f)), _chan_major(np.asarray(inp["s5_b_glu"][j], f))],
                              axis=1).reshape(128, 48))
    sh["s5_l2"] = np.stack(s5_l2)
    sh["s5_bl2"] = np.stack(s5_bl2)
    sh["s5_cl2"] = np.stack(s5_cl2)
    sh["s5_l1"] = np.stack(s5_l1)
    sh["s5_bl1"] = np.stack(s5_bl1)
    sh["s5_dg"] = np.stack(s5_dg)
    lv, lw = [], []
    for j in range(2):
        cw = np.asarray(inp["lru_conv_w"][j], f)
        cols = [cw[0], cw[1], cw[2], cw[3], np.asarray(inp["lru_conv_b"][j], f),
                np.asarray(inp["lru_b_a"][j], f).reshape(-1), np.asarray(inp["lru_b_x"][j], f).reshape(-1),
                np.asarray(inp["lru_lam"][j], f)]
        v = np.stack([_chan_major(c) for c in cols], axis=2)
        lv.append(v.reshape(128, 24 * 8))
        wa = np.asarray(inp["lru_w_a"][j], f).reshape(12, 2, 128, 256)
        wx = np.asarray(inp["lru_w_x"][j], f).reshape(12, 2, 128, 256)
        w = np.stack([wa, wx], axis=1)
        lw.append(np.ascontiguousarray(w.transpose(0, 3, 1, 2, 4)).reshape(12, 128, 2 * 2 * 256))
    sh["lru_v"] = np.stack(lv)
    sh["lru_w"] = np.stack(lw)
    return {k: np.ascontiguousarray(v, dtype=f) for k, v in sh.items()}


def kernel(**inputs):
    x = np.asarray(inputs["x"], np.float32)
    mem = np.asarray(inputs["mem"], np.float32)
    B, L, _ = x.shape
    sh = prep_shared(inputs)
    nc = bass.Bass("TRN2", target_bir_lowering=False)
    build(nc, L)
    in_maps = []
    for b in range(B):
        m = dict(sh)
        m["xT"] = np.ascontiguousarray(x[b].T)
        m["memT"] = np.ascontiguousarray(mem[b].T)
        in_maps.append(m)
    res = run_bass_kernel_spmd(nc, in_maps, core_ids=list(range(B)))
    out = np.stack([np.ascontiguousarray(res.results[b]["outT"].T) for b in range(B)])
    return out.astype(np.float32)
```

```python
import math
import os
import numpy as np
import concourse.bass as bass
import concourse.mybir as mybir
from concourse.bass_utils import run_bass_kernel_spmd

F32 = mybir.dt.float32
BF16 = mybir.dt.bfloat16
AF = mybir.ActivationFunctionType
ALU = mybir.AluOpType

D = 4096
NMEM = 256
REC = 3072
TB = 256
T8 = 8
EPS = 1e-6
PI = math.pi
PESKIP = os.environ.get('K_PESKIP', '1') == '1'
WCACHE = os.environ.get('K_WCACHE', '1') == '1'
ZSYNC = int(os.environ.get('K_ZSYNC', '1'))
YSYNC = int(os.environ.get('K_YSYNC', '0'))


class Res:
    __slots__ = ("w", "rs")

    def __init__(self):
        self.w = None
        self.rs = {}


class Op:
    __slots__ = ("eng", "fn", "deps", "is_dma", "chan", "seq", "signal", "g", "pesync")
    _n = 0

    def __init__(self, eng, fn, is_dma=False, chan=None):
        Op._n += 1
        self.g = Op._n
        self.pesync = False
        self.eng = eng
        self.fn = fn
        self.deps = []
        self.is_dma = is_dma
        self.chan = chan
        self.seq = None
        self.signal = False


class KB:
    ENGS = ("pe", "act", "dve", "pool", "sp")

    def __init__(self, nc):
        self.nc = nc
        self.streams = {e: [] for e in self.ENGS}
        self.esem = {e: nc.alloc_semaphore(name="s_" + e) for e in self.ENGS}
        self.chans = {}
        self.rmap = {}
        self.fence = []

    def R(self, *key):
        r = self.rmap.get(key)
        if r is None:
            r = self.rmap[key] = Res()
        return r

    def chan(self, name):
        if name not in self.chans:
            self.chans[name] = [self.nc.alloc_semaphore(name="c_" + name), 0]
        return name

    def _track(self, op, reads, writes):
        seen = set()
        for r in reads:
            if r.w is not None and id(r.w) not in seen:
                seen.add(id(r.w))
                op.deps.append(r.w)
        for w in writes:
            if w.w is not None and id(w.w) not in seen:
                seen.add(id(w.w))
                op.deps.append(w.w)
            for d in w.rs.values():
                if id(d) not in seen and d is not op:
                    seen.add(id(d))
                    op.deps.append(d)
        for r in reads:
            r.rs[(op.eng, op.chan)] = op
        for w in writes:
            w.w = op
            w.rs = {}

    def barrier(self):
        f = []
        for e in self.ENGS:
            for o in reversed(self.streams[e]):
                if not o.is_dma:
                    f.append(o)
                    break
        lastd = {}
        for e in self.ENGS:
            for o in self.streams[e]:
                if o.is_dma:
                    lastd[o.chan] = o
        f.extend(lastd.values())
        self.fence = f

    def op(self, eng, fn, reads=(), writes=()):
        o = Op(eng, fn)
        self._track(o, reads, writes)
        o.deps.extend(self.fence)
        self.streams[eng].append(o)
        return o

    def dma(self, queue, out, in_, reads=(), writes=(), chan=None):
        chan = self.chan(chan or ("q_" + queue))
        o = Op(queue, lambda e: e.dma_start(out=out, in_=in_), is_dma=True, chan=chan)
        self._track(o, reads, writes)
        o.deps.extend(self.fence)
        self.streams[queue].append(o)
        c = self.chans[chan]
        c[1] += 1
        o.seq = c[1]
        return o

    def emit(self, final_waits=()):
        nc = self.nc
        for e in self.ENGS:
            for o in self.streams[e]:
                for d in o.deps:
                    if not d.is_dma and not (PESKIP and e == "pe" and d.eng == "pe" and not o.pesync):
                        d.signal = True
        for e in self.ENGS:
            k = 0
            for o in self.streams[e]:
                if not o.is_dma and o.signal:
                    k += 1
                    o.seq = k
        kb = self

        def body(e, eng):
            waited = {}

            def dkey(d):
                if d.is_dma:
                    return ("c", d.chan), 16 * d.seq
                return ("e", d.eng), d.seq

            def wait_for(d):
                if d.is_dma:
                    sem, val, key = kb.chans[d.chan][0], 16 * d.seq, ("c", d.chan)
                else:
                    sem, val, key = kb.esem[d.eng], d.seq, ("e", d.eng)
                if waited.get(key, 0) >= val:
                    return
                waited[key] = val
                eng.wait_ge(sem, val)

            ops = kb.streams[e]
            LOOK = int(os.environ.get('K_LOOK', '80'))
            for idx, o in enumerate(ops):
                need = {}
                for d in o.deps:
                    if PESKIP and e == "pe" and d.eng == "pe" and not d.is_dma and not o.pesync:
                        continue
                    key, val = dkey(d)
                    if waited.get(key, 0) < val and need.get(key, (0, None))[0] < val:
                        need[key] = (val, d)
                if need:
                    for o2 in ops[idx + 1: idx + 1 + LOOK]:
                        for d in o2.deps:
                            if PESKIP and e == "pe" and d.eng == "pe" and not d.is_dma and not o2.pesync:
                                continue
                            key, val = dkey(d)
                            if key in need and d.g < o.g and val > need[key][0]:
                                need[key] = (val, d)
                    for key, (val, d) in need.items():
                        wait_for(d)
                ins = o.fn(eng)
                if o.is_dma:
                    ins.then_inc(kb.chans[o.chan][0], 16)
                elif o.signal:
                    ins.then_inc(kb.esem[e], 1)
            if e == "sp":
                for d in final_waits:
                    wait_for(d)

        with nc.Block() as block:
            @block.tensor
            def _(eng):
                body("pe", eng)

            @block.scalar
            def _(eng):
                body("act", eng)

            @block.vector
            def _(eng):
                body("dve", eng)

            @block.gpsimd
            def _(eng):
                body("pool", eng)

            @block.sync
            def _(eng):
                body("sp", eng)


def build(nc, NT, layers=(0, 1, 2, 3), dbg=None):
    NB = NT // TB
    NCH = TB // T8
    kb = KB(nc)
    R = kb.R

    def din(name, shape, dt=F32):
        return nc.dram_tensor(name, list(shape), dt, kind="ExternalInput").ap()

    xT = din("xT", [D, NT])
    memT = din("memT", [D, NMEM])
    w_in = din("w_in", [4, 64, 128, 32 * 128])
    w_out = din("w_out", [4, 32, 128, 32 * 128])
    w_kv = din("w_kv", [4, 16, 128, 32 * 128])
    w_glu = din("w_glu", [2, 24, 128, 24 * 128])
    gains_d = din("gains", [128, 3 * 4 * 32])
    s5_l2 = din("s5_l2", [2, 128, 3 * 96])
    s5_bl2 = din("s5_bl2", [2, 128, 2 * 96 * 32])
    s5_cl2 = din("s5_cl2", [2, 128, 2 * 96 * 32])
    s5_l1 = din("s5_l1", [2, 128, 3 * 24 * 128])
    s5_bl1 = din("s5_bl1", [2, 128, 2 * 24 * 128])
    s5_dg = din("s5_dg", [2, 128, 2 * 24])
    lru_v = din("lru_v", [2, 128, 24 * 8])
    lru_w = din("lru_w", [2, 12, 128, 2 * 2 * 256])
    outT = nc.dram_tensor("outT", [D, NT], F32, kind="ExternalOutput").ap()
    o_scr = nc.dram_tensor("o_scr", [D, TB], F32, kind="Internal").ap()
    zmat_d = nc.dram_tensor("zmat", [24, 128, 16 * 128], BF16, kind="Internal").ap()
    camat_d = nc.dram_tensor("camat", [128, 2 * 8 * 96 * 32], BF16, kind="Internal").ap()
    kmat_d = nc.dram_tensor("kmat", [24, 128, 8 * 128], BF16, kind="Internal").ap()

    wbf_in = nc.dram_tensor("wbf_in", [1, 64, 128, 32 * 128], BF16, kind="Internal").ap()
    wbf_out = nc.dram_tensor("wbf_out", [1, 32, 128, 32 * 128], BF16, kind="Internal").ap()
    wbf_glu = nc.dram_tensor("wbf_glu", [1, 24, 128, 24 * 128], BF16, kind="Internal").ap()

    def sb(name, shape, dt=F32):
        return nc.alloc_sbuf_tensor("sb_" + name, list(shape), dt)

    ones_bf = sb("ones_bf", [128, 128], BF16)
    gains = sb("gains", [128, 3, 4, 32])
    hnT = sb("hnT", [128, 32, TB], BF16)
    mixed = sb("mixed", [128, 32, TB], BF16)
    u = sb("u", [128, 24, 3 + TB], BF16)
    gsil = sb("gsil", [128, 24, TB], BF16)
    qT = sb("qT", [128, 8, TB], BF16)
    gqsil = sb("gqsil", [128, 8, TB], BF16)
    NWS = 3
    wb = [sb("wb%d" % i, [128, 32, 128], BF16) for i in range(NWS)]
    NHS = 4
    hs = [sb("hs%d" % i, [128, TB]) for i in range(NHS)]
    sqb = [sb("sqb%d" % i, [128, TB], BF16) for i in range(2)]
    rstd = sb("rstd", [128, TB])
    rstdA = sb("rstdA", [128, TB])
    rstd_m = sb("rstd_m", [128, NMEM])
    KT = sb("KT", [128, 8, NMEM], BF16)
    Vt = sb("Vt", [128, 2, 1024], BF16)
    expT = sb("expT", [128, 2, TB], BF16)
    rden = sb("rden", [128, TB])
    NTMP = 5
    tmpf = [sb("tmpf%d" % i, [128, TB]) for i in range(NTMP)]
    osb = [sb("osb%d" % i, [128, TB]) for i in range(2)]
    S = sb("S5S", [128, 192])
    AAB = sb("S5AAB", [128, 2, 192])
    sct = sb("s5sct", [128, 3, 192])
    dg = sb("s5dg", [128, 2, 24])
    hbias = sb("hbias", [128, 24])
    uq = sb("uq", [128, 2, 4, TB], BF16)
    lv = sb("lruv", [128, 24, 8])
    negc = sb("negc", [128, 24])
    hst = sb("hst", [128, 24])
    AFN = 9216
    ABN = 22528
    arF = sb("arenaF", [128, AFN])
    arB = sb("arenaB", [128, ABN], BF16)

    def carve(ar, off, shape):
        n = 1
        for d_ in shape:
            n *= d_
        v = ar[:, off:off + n]
        if len(shape) == 2:
            return v.rearrange("p (a b) -> p a b", a=shape[0])
        if len(shape) == 3:
            return v.rearrange("p (a b c) -> p a b c", a=shape[0], b=shape[1])
        if len(shape) == 4:
            return v.rearrange("p (a b c d) -> p a b c d", a=shape[0], b=shape[1], c=shape[2])
        return v

    Zsb = carve(arF, 0, [NCH, 192])
    Shist = carve(arB, 0, [NCH, 192])
    gT = carve(arB, 6144, [24, TB])
    zmat = [carve(arB, 12288 + i * 2048, [16, 128]) for i in range(2)]
    camat = [carve(arB, 16384 + i * 2048, [2, 8, 4, 32]) for i in range(2)]
    kmat = [carve(arB, 20480 + i * 1024, [8, 128]) for i in range(2)]
    xc = carve(arF, 0, [24, TB])
    a4 = carve(arF, 6144, [4, TB])
    g4 = carve(arF, 7168, [4, TB])
    v4 = carve(arF, 8192, [4, TB])
    xcb = carve(arB, 0, [24, TB])
    gw = [carve(arB, 6144 + i * 1024, [2, 2, 256]) for i in range(2)]

    ps = [nc.alloc_psum_tensor("ps%d" % i, [128, 512], F32) for i in range(8)]
    PS_SS = 0

    state = {"mm": 0, "w": 0, "tmp": 0, "z": 0, "y": 0, "hs": 0}

    def mm_bank():
        state["mm"] = (state["mm"] + 1) % 3
        return 1 + state["mm"]

    def z_bank():
        state["z"] = (state["z"] + 1) % 2
        return 4 + state["z"]

    def y_bank():
        state["y"] = (state["y"] + 1) % 2
        return 6 + state["y"]

    def tmp():
        state["tmp"] = (state["tmp"] + 1) % NTMP
        i = state["tmp"]
        return tmpf[i], R("tmpf", i)

    def hs_slot():
        state["hs"] = (state["hs"] + 1) % NHS
        i = state["hs"]
        return i, hs[i], R("hs", i)

    def V_tt(out, in0, in1, op, r, w):
        kb.op("dve", lambda e: e.tensor_tensor(out=out, in0=in0, in1=in1, op=op), r, w)

    def V_ts(out, in0, s1, s2, op0, op1, r, w):
        if op1 is None:
            kb.op("dve", lambda e: e.tensor_scalar(out=out, in0=in0, scalar1=s1, scalar2=None, op0=op0), r, w)
        else:
            kb.op("dve", lambda e: e.tensor_scalar(out=out, in0=in0, scalar1=s1, scalar2=s2, op0=op0, op1=op1), r, w)

    def V_stt(out, in0, scalar, in1, op0, op1, r, w):
        kb.op("dve", lambda e: e.scalar_tensor_tensor(out=out, in0=in0, scalar=scalar, in1=in1, op0=op0, op1=op1), r, w)

    def V_copy(out, in_, r, w):
        kb.op("dve", lambda e: e.tensor_copy(out=out, in_=in_), r, w)

    def V_recip(out, in_, r, w):
        kb.op("dve", lambda e: e.reciprocal(out=out, in_=in_), r, w)

    def V_memset(out, val, r, w):
        kb.op("dve", lambda e: e.memset(out, val), r, w)

    def A_act(out, in_, func, r, w, bias=None, scale=None):
        kw = {}
        if bias is not None:
            kw["bias"] = bias
        if scale is not None:
            kw["scale"] = scale
        kb.op("act", lambda e: e.activation(out=out, in_=in_, func=func, **kw), r, w)

    def MM(out, lhsT, rhs, start, stop, r, w, tp=None, sync=False):
        if tp is None:
            o = kb.op("pe", lambda e: e.matmul(out, lhsT=lhsT, rhs=rhs, start=start, stop=stop), r, w)
        else:
            o = kb.op("pe", lambda e: e.matmul(out, lhsT=lhsT, rhs=rhs, start=start, stop=stop, tile_position=tp), r, w)
        o.pesync = sync

    def PSR(b):
        return [R("ps", b, 0), R("ps", b, 1)]

    def load_w(kind, l, c, first=True):
        i = state["w"] = (state["w"] + 1) % NWS
        src32 = {"kv": w_kv, "in": w_in, "out": w_out}[kind][l, c] if kind != "glu" else w_glu[l // 2, c]
        n = src32.shape[-1] // 128
        if kind == "kv":
            kb.dma("pool", wb[i][:, 0:n, :], src32.rearrange("p (k c) -> p k c", c=128), writes=[R("wb", i)], chan="wb%d" % i)
            return wb[i], R("wb", i)
        dcopy = {"in": wbf_in, "glu": wbf_glu, "out": wbf_out}[kind][0, c]
        if not WCACHE:
            kb.dma("pool", wb[i][:, 0:n, :], src32.rearrange("p (k c) -> p k c", c=128), writes=[R("wb", i)], chan="wb%d" % i)
        elif first:
            kb.dma("pool", wb[i][:, 0:n, :], src32.rearrange("p (k c) -> p k c", c=128), writes=[R("wb", i)], chan="wb%d" % i)
            kb.dma("sp", dcopy.rearrange("p (k c) -> p k c", c=128), wb[i][:, 0:n, :], reads=[R("wb", i)],
                   writes=[R("wbf", kind, c)], chan="wst%d" % i)
        else:
            kb.dma("pool", wb[i][:, 0:n, :], dcopy.rearrange("p (k c) -> p k c", c=128), reads=[R("wbf", kind, c)],
                   writes=[R("wb", i)], chan="wb%d" % i)
        return wb[i], R("wb", i)

    def rsqrt_mean(dst, rdst, psap, rps):
        V_ts(dst, psap, 1.0 / D, EPS, ALU.mult, ALU.add, rps, [rdst])
        A_act(dst, dst, AF.Sqrt, [rdst], [rdst])
        V_recip(dst, dst, [rdst], [rdst])

    V_memset(ones_bf[:], 1.0, (), [R("ones")])
    V_memset(uq[:], 0.0, (), [R("uq", 0), R("uq", 1)])
    kb.dma("sp", gains[:], gains_d.rearrange("p (a l k) -> p a l k", a=3, l=4), writes=[R("gains")], chan="misc")

    def silu_from_psum(out_bf, psap, rps, wres):
        t, rt = tmp()
        A_act(t[:], psap, AF.Tanh, rps, [rt], scale=0.5)
        V_ts(t[:], t[:], 0.5, 0.5, ALU.mult, ALU.add, [rt], [rt])
        V_tt(out_bf, psap, t[:], ALU.mult, rps + [rt], [wres])

    mem_state = {"rstd": False}

    def phase_kv(l):
        if not mem_state["rstd"]:
            for kt in range(32):
                si, ht, rh = hs_slot()
                kb.dma("sp", ht[:], memT[kt * 128:(kt + 1) * 128, :], writes=[rh], chan="hs%d" % si)
                i = kt % 2
                A_act(sqb[i][:], ht[:], AF.Square, [rh], [R("sqb", i)])
                MM(ps[PS_SS][:, 0:NMEM], ones_bf[:], sqb[i][:], kt == 0, kt == 31, [R("ones"), R("sqb", i)], [*PSR(PS_SS)])
            rsqrt_mean(rstd_m[:], R("rstd_m"), ps[PS_SS][:, 0:NMEM], PSR(PS_SS))
            mem_state["rstd"] = True
        for kt in range(32):
            si, ht, rh = hs_slot()
            kb.dma("sp", ht[:], memT[kt * 128:(kt + 1) * 128, :], writes=[rh], chan="hs%d" % si)
            V_stt(hnT[:, kt, :], ht[:], gains[:, 2, l, kt:kt + 1], rstd_m[:], ALU.mult, ALU.mult,
                  [rh, R("gains"), R("rstd_m")], [R("hnT", kt)])
        for c in range(16):
            wt, rw = load_w("kv", l, c)
            b = mm_bank()
            if c < 8:
                for kt in range(32):
                    MM(ps[b][:, 0:NMEM], wt[:, kt, :], hnT[:, kt, :], kt == 0, kt == 31, [rw, R("hnT", kt)], [*PSR(b)])
                A_act(KT[:, c, :], ps[b][:, 0:NMEM], AF.Copy, [*PSR(b)], [R("KT")])
            else:
                cv = c - 8
                for j in range(2):
                    for kt in range(32):
                        MM(ps[b][:, j * 128:(j + 1) * 128], hnT[:, kt, j * 128:(j + 1) * 128], wt[:, kt, :], kt == 0, kt == 31,
                           [rw, R("hnT", kt)], [*PSR(b)])
                A_act(Vt[:, :, cv * 128:(cv + 1) * 128], ps[b][:, 0:256].rearrange("p (j c) -> p j c", j=2), AF.Copy,
                      [*PSR(b)], [R("Vt")])

    PN = 384
    p2 = [carve(arF, i * PN, [1, PN])[:, 0, :] for i in range(8)]
    pin = arF[:, 8 * PN:11 * PN]
    PSCR = 11 * PN

    def cplx_disc(n, lre, lim, ls):
        ar, ai, cre, cim, t0, t1, t2, t3 = [p2[i][:, 0:n] for i in range(8)]
        rr = [R("prep", i) for i in range(8)]
        rin = [R("prepin")]
        A_act(t0, ls, AF.Exp, rin, [rr[4]])
        V_ts(t1, lre, -1e-4, None, ALU.min, None, rin, [rr[5]])
        V_tt(t2, t1, t0, ALU.mult, [rr[4], rr[5]], [rr[6]])
        A_act(t2, t2, AF.Exp, [rr[6]], [rr[6]])
        V_tt(t3, lim, t0, ALU.mult, rin + [rr[4]], [rr[7]])
        A_act(ai, t3, AF.Sin, [rr[7]], [rr[1]], scale=0.125)
        A_act(ar, t3, AF.Sin, [rr[7]], [rr[0]], scale=-0.125, bias=PI / 2)
        for _ in range(3):
            V_tt(t3, ar, ai, ALU.mult, [rr[0], rr[1]], [rr[7]])
            V_tt(ar, ar, ar, ALU.mult, [rr[0]], [rr[0]])
            V_tt(ai, ai, ai, ALU.mult, [rr[1]], [rr[1]])
            V_tt(ar, ar, ai, ALU.subtract, [rr[0], rr[1]], [rr[0]])
            V_ts(ai, t3, 2.0, None, ALU.mult, None, [rr[7]], [rr[1]])
        V_tt(ar, ar, t2, ALU.mult, [rr[0], rr[6]], [rr[0]])
        V_tt(ai, ai, t2, ALU.mult, [rr[1], rr[6]], [rr[1]])
        V_tt(t0, t1, t1, ALU.mult, [rr[5]], [rr[4]])
        V_tt(t2, lim, lim, ALU.mult, rin, [rr[6]])
        V_tt(t0, t0, t2, ALU.add, [rr[4], rr[6]], [rr[4]])
        V_recip(t0, t0, [rr[4]], [rr[4]])
        V_ts(t2, ar, -1.0, None, ALU.add, None, [rr[0]], [rr[6]])
        V_tt(cre, t2, t1, ALU.mult, [rr[6], rr[5]], [rr[2]])
        V_tt(t3, ai, lim, ALU.mult, [rr[1]] + rin, [rr[7]])
        V_tt(cre, cre, t3, ALU.add, [rr[2], rr[7]], [rr[2]])
        V_tt(cre, cre, t0, ALU.mult, [rr[2], rr[4]], [rr[2]])
        V_tt(cim, ai, t1, ALU.mult, [rr[1], rr[5]], [rr[3]])
        V_tt(t3, t2, lim, ALU.mult, [rr[6]] + rin, [rr[7]])
        V_tt(cim, cim, t3, ALU.subtract, [rr[3], rr[7]], [rr[3]])
        V_tt(cim, cim, t0, ALU.mult, [rr[3], rr[4]], [rr[3]])
        return ar, ai, cre, cim

    def cmul(ore, oim, are, aim, bre, bim, t0, t1, r, w):
        V_tt(t0, are, bre, ALU.mult, r + w, w)
        V_tt(t1, aim, bim, ALU.mult, r + w, w)
        V_tt(t0, t0, t1, ALU.subtract, w, w)
        V_tt(t1, are, bim, ALU.mult, r + w, w)
        V_tt(oim, aim, bre, ALU.mult, r + w, w)
        V_tt(oim, oim, t1, ALU.add, w, w)
        V_copy(ore, t0, w, w)

    def s5_prep(j):
        RX = R("prepbig")
        RB = R("prepbf")
        RK = R("prepK")
        r03 = [R("prep", 0), R("prep", 1), R("prep", 2), R("prep", 3)]
        wp = [R("prep", 4), R("prep", 5), R("prep", 6), R("prep", 7)]
        n2 = 96
        kb.dma("sp", pin[:, 0:3 * n2], s5_l2[j], writes=[R("prepin")], chan="misc")
        ar, ai, cre, cim = cplx_disc(n2, pin[:, 0:n2], pin[:, n2:2 * n2], pin[:, 2 * n2:3 * n2])
        p8r, p8i, t6, t7 = p2[4][:, 0:n2], p2[5][:, 0:n2], p2[6][:, 0:n2], p2[7][:, 0:n2]
        V_copy(p8r, ar, r03, wp)
        V_copy(p8i, ai, r03, wp)
        for _ in range(3):
            cmul(p8r, p8i, p8r, p8i, p8r, p8i, t6, t7, [], wp)
        V_copy(AAB[:, 0, 0:96], p8r, wp, [R("AAB")])
        V_copy(AAB[:, 0, 96:192], p8r, wp, [R("AAB")])
        V_ts(AAB[:, 1, 0:96], p8i, -1.0, None, ALU.mult, None, wp, [R("AAB")])
        V_copy(AAB[:, 1, 96:192], p8i, wp, [R("AAB")])
        NQ = 24
        Bf = carve(arF, PSCR, [2, NQ, 32])
        Cf = carve(arF, PSCR + 1536, [2, NQ, 32])
        T0 = carve(arF, PSCR + 3072, [NQ, 32])
        T1 = carve(arF, PSCR + 3840, [NQ, 32])
        Bb = carve(arB, 0, [2, NQ, 32])
        CAb = carve(arB, 1536, [2, NQ, 32])
        Kst = carve(arB, 3072, [6, 128])
        bl2 = s5_bl2[j].rearrange("p (a b c) -> p a b c", a=2, b=96)
        cl2 = s5_cl2[j].rearrange("p (a b c) -> p a b c", a=2, b=96)
        km_v = kmat_d.rearrange("t p (l c) -> p t l c", l=8)
        for qt in range(4):
            kb.dma("sp", Bf, bl2[:, :, qt * NQ:(qt + 1) * NQ, :], writes=[RX], chan="misc")
            kb.dma("sp", Cf, cl2[:, :, qt * NQ:(qt + 1) * NQ, :], writes=[RX], chan="misc")

            def bc(x):
                return x[:, qt * NQ:(qt + 1) * NQ].unsqueeze(2).to_broadcast([128, NQ, 32])

            cmul(Bf[:, 0], Bf[:, 1], bc(cre), bc(cim), Bf[:, 0], Bf[:, 1], T0, T1, r03, [RX])
            V_copy(Bb[:, 0], Bf[:, 0], [RX], [RB])
            V_copy(Bb[:, 1], Bf[:, 1], [RX], [RB])
            for jj in range(9):
                V_copy(CAb[:, 0], Cf[:, 0], [RX], [RB])
                V_ts(CAb[:, 1], Cf[:, 1], -1.0, None, ALU.mult, None, [RX], [RB])
                if jj >= 1:
                    for ri in range(2):
                        o0 = ((ri * 8 + jj - 1) * 96 + qt * NQ) * 32
                        kb.dma("sp", camat_d[:, o0:o0 + NQ * 32], CAb[:, ri].rearrange("p a c -> p (a c)"), reads=[RB],
                               writes=[R("camat_d")], chan="prepst")
                if jj <= 7:
                    V_memset(Kst, 0.0, [RK], [RK])
                    for tg in range(2):
                        b = z_bank()
                        for tl in range(3):
                            for q in range(4):
                                pr = (tg * 3 + tl) * 4 + q
                                for ri in range(2):
                                    MM(ps[b][32 * q:32 * q + 32, tl * 128 + 32 * q: tl * 128 + 32 * q + 32],
                                       Bb[:, ri, pr, :], CAb[:, ri, pr, :], ri == 0, ri == 1, [RB], [*PSR(b)], tp=(0, 32 * q), sync=True)
                        for q in range(4):
                            src = ps[b][32 * q:32 * q + 32, 0:384].rearrange("p (t c) -> p t c", t=3)[:, :, 32 * q:32 * q + 32]
                            dst = Kst[32 * q:32 * q + 32, tg * 3:tg * 3 + 3, 32 * q:32 * q + 32]
                            A_act(dst, src, AF.Copy, [*PSR(b)], [RK])
                    kb.dma("sp", km_v[:, qt * 6:(qt + 1) * 6, jj, :], Kst, reads=[RK], writes=[R("kmat_d")], chan="prepst")
                    cmul(Cf[:, 0], Cf[:, 1], bc(ar), bc(ai), Cf[:, 0], Cf[:, 1], T0, T1, r03, [RX])
        n1 = 3 * 128
        l1v = s5_l1[j].rearrange("p (a n) -> p a n", a=3)
        b1v = s5_bl1[j].rearrange("p (a n) -> p a n", a=2)
        zm_v = zmat_d.rearrange("t p (k c) -> p t k c", k=16)
        Wr = arF[:, PSCR:PSCR + n1]
        Wi = arF[:, PSCR + n1:PSCR + 2 * n1]
        U0 = arF[:, PSCR + 2 * n1:PSCR + 3 * n1]
        U1 = arF[:, PSCR + 3 * n1:PSCR + 4 * n1]
        Zst = carve(arB, 4096, [3, 16, 128])
        for ch in range(8):
            kb.dma("sp", pin[:, 0:3 * n1].rearrange("p (a n) -> p a n", a=3), l1v[:, :, ch * n1:(ch + 1) * n1],
                   writes=[R("prepin")], chan="misc")
            ar, ai, cre, cim = cplx_disc(n1, pin[:, 0:n1], pin[:, n1:2 * n1], pin[:, 2 * n1:3 * n1])
            kb.dma("sp", arF[:, PSCR:PSCR + 2 * n1].rearrange("p (a n) -> p a n", a=2), b1v[:, :, ch * n1:(ch + 1) * n1],
                   writes=[RX], chan="misc")
            cmul(Wr, Wi, cre, cim, Wr, Wi, U0, U1, r03, [RX])
            for k in range(7, -1, -1):
                V_copy(Zst[:, :, 2 * k, :], Wr.rearrange("p (t c) -> p t c", t=3), [RX], [RB])
                V_copy(Zst[:, :, 2 * k + 1, :], Wi.rearrange("p (t c) -> p t c", t=3), [RX], [RB])
                if k > 0:
                    cmul(Wr, Wi, ar, ai, Wr, Wi, U0, U1, r03, [RX])
            kb.dma("sp", zm_v[:, ch * 3:(ch + 1) * 3], Zst, reads=[RB], writes=[R("zmat_d")], chan="prepst")
        kb.dma("sp", dg[:], s5_dg[j].rearrange("p (a t) -> p a t", a=2), writes=[R("dg")], chan="misc")
        V_ts(hbias[:], dg[:, 1, :], 0.5, None, ALU.mult, None, [R("dg")], [R("hb")])
        V_memset(S[:], 0.0, [R("S")], [R("S")])

    def s5_block(j, blk):
        RZ = R("Zsb")
        RS = R("S")
        RH = R("Shist")
        RT = R("sct")
        for t in range(24):
            i = t % 2
            par = t % 2
            kb.dma("sp", zmat[i], zmat_d[t].rearrange("p (k c) -> p k c", k=16), reads=[R("zmat_d")], writes=[R("zmat", i)],
                   chan="zm%d" % i)
            for q in range(4):
                A_act(uq[32 * q:32 * q + 32, par, q, :], u[32 * q:32 * q + 32, t, 3:3 + TB], AF.Copy, [R("u", t)], [R("uq", par)])
            b = z_bank()
            for q in range(4):
                uv = uq[:, par, q, :].rearrange("p (n k) -> p k n", k=T8)
                for ri in range(2):
                    for k in range(T8):
                        MM(ps[b][:, (ri * 4 + q) * NCH:(ri * 4 + q + 1) * NCH], zmat[i][:, 2 * k + ri, :],
                           uv[:, k, :], k == 0, k == T8 - 1, [R("zmat", i), R("uq", par)], [*PSR(b)])
            dst = Zsb.rearrange("p n (r g) -> p n r g", r=2)[:, :, :, 4 * t:4 * t + 4]
            src = ps[b][:, 0:8 * NCH].rearrange("p (r q n) -> p n r q", r=2, q=4)
            A_act(dst, src, AF.Copy, [*PSR(b)], [RZ])
        for n in range(NCH):
            A_act(Shist[:, n, :], S[:], AF.Copy, [RS], [RH])
            V_tt(sct[:, 0, :], AAB[:, 0, :], S[:], ALU.mult, [RS, R("AAB")], [RT])
            V_tt(sct[:, 1, 0:96], AAB[:, 1, 0:96], S[:, 96:192], ALU.mult, [RS, R("AAB")], [RT])
            V_tt(sct[:, 1, 96:192], AAB[:, 1, 96:192], S[:, 0:96], ALU.mult, [RS, R("AAB")], [RT])
            V_tt(sct[:, 0, :], sct[:, 0, :], sct[:, 1, :], ALU.add, [RT], [RT])
            V_tt(S[:], sct[:, 0, :], Zsb[:, n, :], ALU.add, [RT, RZ], [RS])
        for t in range(24):
            i = t % 2
            kb.dma("sp", camat[i].rearrange("p r k q c -> p (r k) (q c)"),
                   camat_d.rearrange("p (rk pr c) -> p rk pr c", rk=16, pr=96)[:, :, 4 * t:4 * t + 4, :].rearrange("p a q c -> p a (q c)"),
                   reads=[R("camat_d")], writes=[R("camat", i)], chan="cm%d" % i)
            kb.dma("sp", kmat[i], kmat_d[t].rearrange("p (l c) -> p l c", l=8), reads=[R("kmat_d")], writes=[R("kmat", i)],
                   chan="km%d" % i)
            b = y_bank()
            uv = u[:, t, 3:3 + TB].rearrange("p (n k) -> p k n", k=T8)
            for k in range(T8):
                yo = ps[b][:, k * NCH:(k + 1) * NCH]
                for l in range(k + 1):
                    MM(yo, kmat[i][:, l, :], uv[:, k - l, :], l == 0, False, [R("kmat", i), R("u", t)], [*PSR(b)])
                for ri in range(2):
                    for q in range(4):
                        MM(ps[b][32 * q:32 * q + 32, k * NCH:(k + 1) * NCH], camat[i][:, ri, k, q, :],
                           Shist[:, :, ri * 96 + 4 * t + q], False, (ri == 1), [R("camat", i), RH], [*PSR(b)], tp=(0, 32 * q),
                           sync=(YSYNC == 1 or (YSYNC == 2 and q == 0)))
            yt, ry = tmp()
            V_stt(yt[:].rearrange("p (n k) -> p n k", k=T8), u[:, t, 3:3 + TB].rearrange("p (n k) -> p n k", k=T8), dg[:, 0, t:t + 1],
                  ps[b][:, 0:TB].rearrange("p (k n) -> p n k", k=T8), ALU.mult, ALU.add, [R("u", t), R("dg"), *PSR(b)], [ry])
            t2, r2 = tmp()
            V_tt(t2[:], yt[:], yt[:], ALU.mult, [ry], [r2])
            V_ts(t2[:], t2[:], 0.044715, 1.0, ALU.mult, ALU.add, [r2], [r2])
            V_tt(t2[:], t2[:], yt[:], ALU.mult, [r2, ry], [r2])
            A_act(t2[:], t2[:], AF.Tanh, [r2], [r2], scale=math.sqrt(2.0 / PI))
            V_ts(t2[:], t2[:], 0.5, 0.5, ALU.mult, ALU.add, [r2], [r2])
            V_tt(gT[:, t, :], t2[:], yt[:], ALU.mult, [r2, ry], [R("gT", t)])
        for c in range(24):
            wt, rw = load_w("glu", 2 * j, c, blk == 0)
            b = mm_bank()
            for kt in range(24):
                MM(ps[b][:, 0:TB], wt[:, kt, :], gT[:, kt, :], kt == 0, kt == 23, [rw, R("gT", kt)], [*PSR(b)])
            t2, r2 = tmp()
            A_act(t2[:], ps[b][:, 0:TB], AF.Tanh, [*PSR(b), R("hb")], [r2], scale=0.5, bias=hbias[:, c:c + 1])
            V_ts(t2[:], t2[:], 0.5, 0.5, ALU.mult, ALU.add, [r2], [r2])
            V_tt(t2[:], t2[:], gT[:, c, :], ALU.mult, [r2, R("gT", c)], [r2])
            V_tt(mixed[:, c, :], t2[:], gsil[:, c, :], ALU.mult, [r2, R("gsil", c)], [R("mixed", c)])

    def lru_prep(j):
        kb.dma("sp", lv[:], lru_v[j].rearrange("p (t k) -> p t k", k=8), writes=[R("lv")], chan="misc")
        A_act(negc[:], lv[:, :, 7], AF.Exp, [R("lv")], [R("negc")], scale=-1.0)
        A_act(negc[:], negc[:], AF.Ln, [R("negc")], [R("negc")], bias=1.0)
        V_ts(negc[:], negc[:], -8.0, None, ALU.mult, None, [R("negc")], [R("negc")])
        V_ts(lv[:, :, 5:7], lv[:, :, 5:7], 0.5, None, ALU.mult, None, [R("lv")], [R("lv")])
        V_memset(hst[:], 0.0, [R("hst")], [R("hst")])
        V_memset(u[:, :, 0:3], 0.0, [R("utail")], [R("utail")])

    def lru_block(j, blk):
        for t in range(24):
            ru = [R("u", t), R("utail"), R("lv")]
            V_ts(xc[:, t, :], u[:, t, 3:3 + TB], lv[:, t, 3:4], lv[:, t, 4:5], ALU.mult, ALU.add, ru, [R("xc", t)])
            for jj in range(3):
                V_stt(xc[:, t, :], u[:, t, jj:jj + TB], lv[:, t, jj:jj + 1], xc[:, t, :], ALU.mult, ALU.add, ru + [R("xc", t)], [R("xc", t)])
            A_act(xcb[:, t, :], xc[:, t, :], AF.Copy, [R("xc", t)], [R("xcb", t)])
        V_copy(u[:, :, 0:3], u[:, :, TB:TB + 3], [R("u", t) for t in range(24)] + [R("xc", t) for t in range(24)], [R("utail")])
        for h2 in range(6):
            R4 = R("lru4")
            for hh in range(2):
                h = 2 * h2 + hh
                i = h % 2
                kb.dma("pool", gw[i], lru_w[j, h].rearrange("p (a i j) -> p a i j", a=2, i=2), writes=[R("gw", i)], chan="gw%d" % i)
                for j2 in range(2):
                    t = 2 * h + j2
                    k4 = 2 * hh + j2
                    b = z_bank()
                    for ax in range(2):
                        for i2 in range(2):
                            MM(ps[b][:, ax * TB:(ax + 1) * TB], gw[i][:, ax, i2, j2 * 128:(j2 + 1) * 128], xcb[:, 2 * h + i2, :],
                               i2 == 0, i2 == 1, [R("gw", i), R("xcb", 2 * h + i2)], [*PSR(b)])
                    r_, rr_ = tmp()
                    A_act(r_[:], ps[b][:, 0:TB], AF.Tanh, [*PSR(b), R("lv")], [rr_], scale=0.5, bias=lv[:, t, 5:6])
                    A_act(g4[:, k4, :], ps[b][:, TB:2 * TB], AF.Tanh, [*PSR(b), R("lv")], [R4], scale=0.5, bias=lv[:, t, 6:7])
                    V_ts(r_[:], r_[:], 0.5, 0.5, ALU.mult, ALU.add, [rr_], [rr_])
                    V_ts(g4[:, k4, :], g4[:, k4, :], 0.5, 0.5, ALU.mult, ALU.add, [R4], [R4])
                    A_act(a4[:, k4, :], r_[:], AF.Exp, [rr_, R("negc")], [R4], scale=negc[:, t:t + 1])
                    V_tt(v4[:, k4, :], a4[:, k4, :], a4[:, k4, :], ALU.mult, [R4], [R4])
                    V_ts(v4[:, k4, :], v4[:, k4, :], -1.0, 1.0, ALU.mult, ALU.add, [R4], [R4])
                    V_tt(g4[:, k4, :], g4[:, k4, :], xc[:, t, :], ALU.mult, [R4, R("xc", t)], [R4])
            A_act(v4, v4, AF.Sqrt, [R4], [R4])
            if blk == 0:
                V_memset(v4[:, :, 0:1], 1.0, [R4], [R4])
            V_tt(g4, g4, v4, ALU.mult, [R4], [R4])
            for k4 in range(4):
                t = 4 * h2 + k4
                kb.op("dve", lambda e, k4=k4, t=t: e.tensor_tensor_scan(out=v4[:, k4, :], data0=a4[:, k4, :], data1=g4[:, k4, :],
                                                                         initial=hst[:, t:t + 1], op0=ALU.mult, op1=ALU.add),
                      [R4, R("hst")], [R4])
                V_copy(hst[:, t:t + 1], v4[:, k4, TB - 1:TB], [R4], [R("hst")])
                V_tt(mixed[:, t, :], v4[:, k4, :], gsil[:, t, :], ALU.mult, [R4, R("gsil", t)], [R("mixed", t)])

    def phase_A(l, li, src, blk):
        tok = slice(blk * TB, (blk + 1) * TB)
        for kt in range(32):
            rsrc = [] if li == 0 else [R("outT", blk, kt)]
            si, ht, rh = hs_slot()
            kb.dma("sp", ht[:], src[kt * 128:(kt + 1) * 128, tok], reads=rsrc, writes=[rh], chan="hs%d" % si)
            i = kt % 2
            A_act(sqb[i][:], ht[:], AF.Square, [rh], [R("sqb", i)])
            MM(ps[PS_SS][:, 0:TB], ones_bf[:], sqb[i][:], kt == 0, kt == 31, [R("ones"), R("sqb", i)], [*PSR(PS_SS)])
        rsqrt_mean(rstdA[:], R("rstdA"), ps[PS_SS][:, 0:TB], PSR(PS_SS))
        for kt in range(32):
            rsrc = [] if li == 0 else [R("outT", blk, kt)]
            si, ht, rh = hs_slot()
            kb.dma("sp", ht[:], src[kt * 128:(kt + 1) * 128, tok], reads=rsrc, writes=[rh], chan="hs%d" % si)
            V_stt(hnT[:, kt, :], ht[:], gains[:, 0, l, kt:kt + 1], rstdA[:], ALU.mult, ALU.mult,
                  [rh, R("gains"), R("rstdA")], [R("hnT", kt)])

    last_stores = []
    V_memset(u[:, :, 0:3], 0.0, (), [R("utail")])
    for li, l in enumerate(layers):
        is_s5 = (l % 2 == 0)
        j = l // 2
        src = xT if li == 0 else outT
        kb.barrier()
        phase_kv(l)
        if is_s5:
            s5_prep(j)
        else:
            lru_prep(j)
        kb.barrier()
        for blk in range(NB):
            tok = slice(blk * TB, (blk + 1) * TB)
            if blk == 0:
                phase_A(l, li, src, blk)
            for c in range(64):
                wt, rw = load_w("in", l, c, blk == 0)
                b = mm_bank()
                for kt in range(32):
                    MM(ps[b][:, 0:TB], wt[:, kt, :], hnT[:, kt, :], kt == 0, kt == 31, [rw, R("hnT", kt)], [*PSR(b)])
                if c < 24:
                    A_act(u[:, c, 3:3 + TB], ps[b][:, 0:TB], AF.Copy, [*PSR(b)], [R("u", c)])
                elif c < 48:
                    silu_from_psum(gsil[:, c - 24, :], ps[b][:, 0:TB], PSR(b), R("gsil", c - 24))
                elif c < 56:
                    A_act(qT[:, c - 48, :], ps[b][:, 0:TB], AF.Copy, [*PSR(b)], [R("qT", c - 48)])
                else:
                    silu_from_psum(gqsil[:, c - 56, :], ps[b][:, 0:TB], PSR(b), R("gqsil", c - 56))
            if is_s5:
                s5_block(j, blk)
            else:
                lru_block(j, blk)
            for hd in range(4):
                for jn in range(2):
                    b = z_bank()
                    for dt_ in range(2):
                        MM(ps[b][:, 0:TB], KT[:, 2 * hd + dt_, jn * 128:(jn + 1) * 128], qT[:, 2 * hd + dt_, :], dt_ == 0, dt_ == 1,
                           [R("KT"), R("qT", 2 * hd + dt_)], [*PSR(b)])
                    A_act(expT[:, jn, :], ps[b][:, 0:TB], AF.Exp, [*PSR(b)], [R("expT", jn)], scale=1.0 / 16.0)
                for jn in range(2):
                    MM(ps[PS_SS][:, 0:TB], ones_bf[:], expT[:, jn, :], jn == 0, jn == 1, [R("ones"), R("expT", jn)], [*PSR(PS_SS)])
                V_recip(rden[:], ps[PS_SS][:, 0:TB], [*PSR(PS_SS)], [R("rden")])
                for dt_ in range(2):
                    b = y_bank()
                    c = 2 * hd + dt_
                    for jn in range(2):
                        MM(ps[b][:, 0:TB], Vt[:, jn, c * 128:(c + 1) * 128], expT[:, jn, :], jn == 0, jn == 1,
                           [R("Vt"), R("expT", jn)], [*PSR(b)])
                    t2, r2 = tmp()
                    V_tt(t2[:], ps[b][:, 0:TB], rden[:], ALU.mult, [*PSR(b), R("rden")], [r2])
                    V_tt(mixed[:, 24 + c, :], t2[:], gqsil[:, c, :], ALU.mult, [r2, R("gqsil", c)], [R("mixed", 24 + c)])
            if blk + 1 < NB:
                phase_A(l, li, src, blk + 1)
            for c in range(32):
                wt, rw = load_w("out", l, c, blk == 0)
                b = mm_bank()
                for kt in range(32):
                    MM(ps[b][:, 0:TB], wt[:, kt, :], mixed[:, kt, :], kt == 0, kt == 31, [rw, R("mixed", kt)], [*PSR(b)])
                i = c % 2
                A_act(osb[i][:], ps[b][:, 0:TB], AF.Copy, [*PSR(b)], [R("osb", i)])
                A_act(sqb[i][:], ps[b][:, 0:TB], AF.Square, [*PSR(b)], [R("sqb", i)])
                MM(ps[PS_SS][:, 0:TB], ones_bf[:], sqb[i][:], c == 0, c == 31, [R("ones"), R("sqb", i)], [*PSR(PS_SS)])
                kb.dma("sp", o_scr[c * 128:(c + 1) * 128, :], osb[i][:], reads=[R("osb", i)], writes=[R("o_scr", c)], chan="ost%d" % i)
            rsqrt_mean(rstd[:], R("rstd"), ps[PS_SS][:, 0:TB], PSR(PS_SS))
            for c in range(32):
                i = c % 2
                rsrc = [] if li == 0 else [R("outT", blk, c)]
                si, ht, rh = hs_slot()
                kb.dma("sp", ht[:], src[c * 128:(c + 1) * 128, tok], reads=rsrc, writes=[rh], chan="hs%d" % si)
                kb.dma("sp", osb[i][:], o_scr[c * 128:(c + 1) * 128, :], reads=[R("o_scr", c)], writes=[R("osb", i)], chan="old%d" % i)
                V_stt(osb[i][:], osb[i][:], gains[:, 1, l, c:c + 1], rstd[:], ALU.mult, ALU.mult, [R("osb", i), R("gains"), R("rstd")],
                      [R("osb", i)])
                V_tt(ht[:], ht[:], osb[i][:], ALU.add, [rh, R("osb", i)], [rh])
                st = kb.dma("sp", outT[c * 128:(c + 1) * 128, tok], ht[:], reads=[rh], writes=[R("outT", blk, c)], chan="hst")
                if li == len(layers) - 1:
                    last_stores.append(st)
    kb.emit(final_waits=last_stores)
    return nc


def _panel(w, ktn):
    K, C = w.shape
    a = w.reshape(ktn, 128, C // 128, 128)
    return np.ascontiguousarray(a.transpose(2, 1, 0, 3)).reshape(C // 128, 128, ktn * 128)


def _chan_major(v):
    n = v.shape[-1] // 128
    a = v.reshape(v.shape[:-1] + (n, 128))
    return np.ascontiguousarray(np.moveaxis(a, -1, 0))


def prep_shared(inp):
    f = np.float32
    sh = {}
    sh["w_in"] = np.stack([_panel(np.asarray(inp["w_in"][l], f), 32) for l in range(4)])
    sh["w_out"] = np.stack([_panel(np.asarray(inp["w_out"][l], f), 32) for l in range(4)])
    sh["w_kv"] = np.stack([_panel(np.asarray(inp["w_kv"][l], f), 32) for l in range(4)])
    sh["w_glu"] = np.stack([_panel(np.asarray(inp["s5_w_glu"][j], f), 24) for j in range(2)])
    g = np.stack([_chan_major(np.asarray(inp[k], f)) for k in ("pre_norm", "post_norm", "mem_norm")], axis=1)
    sh["gains"] = np.ascontiguousarray(g).reshape(128, 3 * 4 * 32)
    lre = np.asarray(inp["s5_lam_re"], f)
    lim = np.asarray(inp["s5_lam_im"], f)
    ls = np.asarray(inp["s5_log_step"], f)
    bre = np.asarray(inp["s5_b_re"], f)
    bim = np.asarray(inp["s5_b_im"], f)
    cre = np.asarray(inp["s5_c_re"], f)
    cim = np.asarray(inp["s5_c_im"], f)

    def l2(v):
        return v.reshape(96, 2, 64).transpose(1, 2, 0).reshape(128, 96)

    def l1(v):
        a = v.reshape(24, 4, 2, 64)
        a = np.broadcast_to(a[:, :, None, :, :], (24, 4, 32, 2, 64))
        return np.ascontiguousarray(a.transpose(1, 2, 0, 3, 4)).reshape(128, 24 * 128)

    s5_l2, s5_bl2, s5_cl2, s5_l1, s5_bl1, s5_dg = [], [], [], [], [], []
    for j in range(2):
        lsb = np.broadcast_to(ls[j][:, None], (192, 64))
        s5_l2.append(np.stack([l2(lre[j]), l2(lim[j]), l2(lsb)], axis=1).reshape(128, 3 * 96))
        s5_l1.append(np.stack([l1(lre[j]), l1(lim[j]), l1(lsb)], axis=1).reshape(128, 3 * 24 * 128))

        def bl2(b):
            o = np.zeros((2, 64, 96, 2, 16), f)
            bb = b.reshape(96, 2, 64, 16)
            for g2 in range(2):
                o[g2, :, :, g2, :] = bb[:, g2].transpose(1, 0, 2)
            return o.reshape(128, 96 * 32)

        def cl2(c):
            return bl2(np.ascontiguousarray(c.transpose(0, 2, 1)))

        def bl1(b):
            o = np.zeros((4, 2, 16, 24, 2, 64), f)
            bb = b.reshape(24, 4, 2, 64, 16)
            for g2 in range(2):
                o[:, g2, :, :, g2, :] = bb[:, :, g2].transpose(1, 3, 0, 2)
            return o.reshape(128, 24 * 128)

        s5_bl2.append(np.stack([bl2(bre[j]), bl2(bim[j])], axis=1).reshape(128, -1))
        s5_cl2.append(np.stack([cl2(cre[j]), cl2(cim[j])], axis=1).reshape(128, -1))
        s5_bl1.append(np.stack([bl1(bre[j]), bl1(bim[j])], axis=1).reshape(128, -1))
        s5_dg.append(np.stack([_chan_major(np.asarray(inp["s5_d"][j], f)), _chan_major(np.asarray(inp["s5_b_glu"][j], f))],
                              axis=1).reshape(128, 48))
    sh["s5_l2"] = np.stack(s5_l2)
    sh["s5_bl2"] = np.stack(s5_bl2)
    sh["s5_cl2"] = np.stack(s5_cl2)
    sh["s5_l1"] = np.stack(s5_l1)
    sh["s5_bl1"] = np.stack(s5_bl1)
    sh["s5_dg"] = np.stack(s5_dg)
    lv, lw = [], []
    for j in range(2):
        cw = np.asarray(inp["lru_conv_w"][j], f)
        cols = [cw[0], cw[1], cw[2], cw[3], np.asarray(inp["lru_conv_b"][j], f),
                np.asarray(inp["lru_b_a"][j], f).reshape(-1), np.asarray(inp["lru_b_x"][j], f).reshape(-1),
                np.asarray(inp["lru_lam"][j], f)]
        v = np.stack([_chan_major(c) for c in cols], axis=2)
        lv.append(v.reshape(128, 24 * 8))
        wa = np.asarray(inp["lru_w_a"][j], f).reshape(12, 2, 128, 256)
        wx = np.asarray(inp["lru_w_x"][j], f).reshape(12, 2, 128, 256)
        w = np.stack([wa, wx], axis=1)
        lw.append(np.ascontiguousarray(w.transpose(0, 3, 1, 2, 4)).reshape(12, 128, 2 * 2 * 256))
    sh["lru_v"] = np.stack(lv)
    sh["lru_w"] = np.stack(lw)
    return {k: np.ascontiguousarray(v, dtype=f) for k, v in sh.items()}


def kernel(**inputs):
    x = np.asarray(inputs["x"], np.float32)
    mem = np.asarray(inputs["mem"], np.float32)
    B, L, _ = x.shape
    sh = prep_shared(inputs)
    nc = bass.Bass("TRN2", target_bir_lowering=False)
    build(nc, L)
    in_maps = []
    for b in range(B):
        m = dict(sh)
        m["xT"] = np.ascontiguousarray(x[b].T)
        m["memT"] = np.ascontiguousarray(mem[b].T)
        in_maps.append(m)
    res = run_bass_kernel_spmd(nc, in_maps, core_ids=list(range(B)))
    out = np.stack([np.ascontiguousarray(res.results[b]["outT"].T) for b in range(B)])
    return out.astype(np.float32)
```

```python
import math
import os
import numpy as np
import concourse.bass as bass
import concourse.mybir as mybir
from concourse.bass_utils import run_bass_kernel_spmd

F32 = mybir.dt.float32
BF16 = mybir.dt.bfloat16
AF = mybir.ActivationFunctionType
ALU = mybir.AluOpType

D = 4096
NMEM = 256
REC = 3072
TB = 256
T8 = 8
EPS = 1e-6
PI = math.pi
PESKIP = os.environ.get('K_PESKIP', '1') == '1'
WCACHE = os.environ.get('K_WCACHE', '1') == '1'
ZSYNC = int(os.environ.get('K_ZSYNC', '1'))
YSYNC = int(os.environ.get('K_YSYNC', '0'))


class Res:
    __slots__ = ("w", "rs")

    def __init__(self):
        self.w = None
        self.rs = {}


class Op:
    __slots__ = ("eng", "fn", "deps", "is_dma", "chan", "seq", "signal", "g", "pesync")
    _n = 0

    def __init__(self, eng, fn, is_dma=False, chan=None):
        Op._n += 1
        self.g = Op._n
        self.pesync = False
        self.eng = eng
        self.fn = fn
        self.deps = []
        self.is_dma = is_dma
        self.chan = chan
        self.seq = None
        self.signal = False


class KB:
    ENGS = ("pe", "act", "dve", "pool", "sp")

    def __init__(self, nc):
        self.nc = nc
        self.streams = {e: [] for e in self.ENGS}
        self.esem = {e: nc.alloc_semaphore(name="s_" + e) for e in self.ENGS}
        self.chans = {}
        self.rmap = {}
        self.fence = []

    def R(self, *key):
        r = self.rmap.get(key)
        if r is None:
            r = self.rmap[key] = Res()
        return r

    def chan(self, name):
        if name not in self.chans:
            self.chans[name] = [self.nc.alloc_semaphore(name="c_" + name), 0]
        return name

    def _track(self, op, reads, writes):
        seen = set()
        for r in reads:
            if r.w is not None and id(r.w) not in seen:
                seen.add(id(r.w))
                op.deps.append(r.w)
        for w in writes:
            if w.w is not None and id(w.w) not in seen:
                seen.add(id(w.w))
                op.deps.append(w.w)
            for d in w.rs.values():
                if id(d) not in seen and d is not op:
                    seen.add(id(d))
                    op.deps.append(d)
        for r in reads:
            r.rs[(op.eng, op.chan)] = op
        for w in writes:
            w.w = op
            w.rs = {}

    def barrier(self):
        f = []
        for e in self.ENGS:
            for o in reversed(self.streams[e]):
                if not o.is_dma:
                    f.append(o)
                    break
        lastd = {}
        for e in self.ENGS:
            for o in self.streams[e]:
                if o.is_dma:
                    lastd[o.chan] = o
        f.extend(lastd.values())
        self.fence = f

    def op(self, eng, fn, reads=(), writes=()):
        o = Op(eng, fn)
        self._track(o, reads, writes)
        o.deps.extend(self.fence)
        self.streams[eng].append(o)
        return o

    def dma(self, queue, out, in_, reads=(), writes=(), chan=None):
        chan = self.chan(chan or ("q_" + queue))
        o = Op(queue, lambda e: e.dma_start(out=out, in_=in_), is_dma=True, chan=chan)
        self._track(o, reads, writes)
        o.deps.extend(self.fence)
        self.streams[queue].append(o)
        c = self.chans[chan]
        c[1] += 1
        o.seq = c[1]
        return o

    def emit(self, final_waits=()):
        nc = self.nc
        for e in self.ENGS:
            for o in self.streams[e]:
                for d in o.deps:
                    if not d.is_dma and not (PESKIP and e == "pe" and d.eng == "pe" and not o.pesync):
                        d.signal = True
        for e in self.ENGS:
            k = 0
            for o in self.streams[e]:
                if not o.is_dma and o.signal:
                    k += 1
                    o.seq = k
        kb = self

        def body(e, eng):
            waited = {}

            def dkey(d):
                if d.is_dma:
                    return ("c", d.chan), 16 * d.seq
                return ("e", d.eng), d.seq

            def wait_for(d):
                if d.is_dma:
                    sem, val, key = kb.chans[d.chan][0], 16 * d.seq, ("c", d.chan)
                else:
                    sem, val, key = kb.esem[d.eng], d.seq, ("e", d.eng)
                if waited.get(key, 0) >= val:
                    return
                waited[key] = val
                eng.wait_ge(sem, val)

            ops = kb.streams[e]
            LOOK = int(os.environ.get('K_LOOK', '80'))
            for idx, o in enumerate(ops):
                need = {}
                for d in o.deps:
                    if PESKIP and e == "pe" and d.eng == "pe" and not d.is_dma and not o.pesync:
                        continue
                    key, val = dkey(d)
                    if waited.get(key, 0) < val and need.get(key, (0, None))[0] < val:
                        need[key] = (val, d)
                if need:
                    for o2 in ops[idx + 1: idx + 1 + LOOK]:
                        for d in o2.deps:
                            if PESKIP and e == "pe" and d.eng == "pe" and not d.is_dma and not o2.pesync:
                                continue
                            key, val = dkey(d)
                            if key in need and d.g < o.g and val > need[key][0]:
                                need[key] = (val, d)
                    for key, (val, d) in need.items():
                        wait_for(d)
                ins = o.fn(eng)
                if o.is_dma:
                    ins.then_inc(kb.chans[o.chan][0], 16)
                elif o.signal:
                    ins.then_inc(kb.esem[e], 1)
            if e == "sp":
                for d in final_waits:
                    wait_for(d)

        with nc.Block() as block:
            @block.tensor
            def _(eng):
                body("pe", eng)

            @block.scalar
            def _(eng):
                body("act", eng)

            @block.vector
            def _(eng):
                body("dve", eng)

            @block.gpsimd
            def _(eng):
                body("pool", eng)

            @block.sync
            def _(eng):
                body("sp", eng)


def build(nc, NT, layers=(0, 1, 2, 3), dbg=None):
    NB = NT // TB
    NCH = TB // T8
    kb = KB(nc)
    R = kb.R

    def din(name, shape, dt=F32):
        return nc.dram_tensor(name, list(shape), dt, kind="ExternalInput").ap()

    xT = din("xT", [D, NT])
    memT = din("memT", [D, NMEM])
    w_in = din("w_in", [4, 64, 128, 32 * 128])
    w_out = din("w_out", [4, 32, 128, 32 * 128])
    w_kv = din("w_kv", [4, 16, 128, 32 * 128])
    w_glu = din("w_glu", [2, 24, 128, 24 * 128])
    gains_d = din("gains", [128, 3 * 4 * 32])
    s5_l2 = din("s5_l2", [2, 128, 3 * 96])
    s5_bl2 = din("s5_bl2", [2, 128, 2 * 96 * 32])
    s5_cl2 = din("s5_cl2", [2, 128, 2 * 96 * 32])
    s5_l1 = din("s5_l1", [2, 128, 3 * 24 * 128])
    s5_bl1 = din("s5_bl1", [2, 128, 2 * 24 * 128])
    s5_dg = din("s5_dg", [2, 128, 2 * 24])
    lru_v = din("lru_v", [2, 128, 24 * 8])
    lru_w = din("lru_w", [2, 12, 128, 2 * 2 * 256])
    outT = nc.dram_tensor("outT", [D, NT], F32, kind="ExternalOutput").ap()
    o_scr = nc.dram_tensor("o_scr", [D, TB], F32, kind="Internal").ap()
    zmat_d = nc.dram_tensor("zmat", [24, 128, 16 * 128], BF16, kind="Internal").ap()
    camat_d = nc.dram_tensor("camat", [128, 2 * 8 * 96 * 32], BF16, kind="Internal").ap()
    kmat_d = nc.dram_tensor("kmat", [24, 128, 8 * 128], BF16, kind="Internal").ap()

    wbf_in = nc.dram_tensor("wbf_in", [1, 64, 128, 32 * 128], BF16, kind="Internal").ap()
    wbf_out = nc.dram_tensor("wbf_out", [1, 32, 128, 32 * 128], BF16, kind="Internal").ap()
    wbf_glu = nc.dram_tensor("wbf_glu", [1, 24, 128, 24 * 128], BF16, kind="Internal").ap()

    def sb(name, shape, dt=F32):
        return nc.alloc_sbuf_tensor("sb_" + name, list(shape), dt)

    ones_bf = sb("ones_bf", [128, 128], BF16)
    gains = sb("gains", [128, 3, 4, 32])
    hnT = sb("hnT", [128, 32, TB], BF16)
    mixed = sb("mixed", [128, 32, TB], BF16)
    u = sb("u", [128, 24, 3 + TB], BF16)
    gsil = sb("gsil", [128, 24, TB], BF16)
    qT = sb("qT", [128, 8, TB], BF16)
    gqsil = sb("gqsil", [128, 8, TB], BF16)
    NWS = 3
    wb = [sb("wb%d" % i, [128, 32, 128], BF16) for i in range(NWS)]
    NHS = 4
    hs = [sb("hs%d" % i, [128, TB]) for i in range(NHS)]
    sqb = [sb("sqb%d" % i, [128, TB], BF16) for i in range(2)]
    rstd = sb("rstd", [128, TB])
    rstdA = sb("rstdA", [128, TB])
    rstd_m = sb("rstd_m", [128, NMEM])
    KT = sb("KT", [128, 8, NMEM], BF16)
    Vt = sb("Vt", [128, 2, 1024], BF16)
    expT = sb("expT", [128, 2, TB], BF16)
    rden = sb("rden", [128, TB])
    NTMP = 5
    tmpf = [sb("tmpf%d" % i, [128, TB]) for i in range(NTMP)]
    osb = [sb("osb%d" % i, [128, TB]) for i in range(2)]
    S = sb("S5S", [128, 192])
    AAB = sb("S5AAB", [128, 2, 192])
    sct = sb("s5sct", [128, 3, 192])
    dg = sb("s5dg", [128, 2, 24])
    hbias = sb("hbias", [128, 24])
    uq = sb("uq", [128, 2, 4, TB], BF16)
    lv = sb("lruv", [128, 24, 8])
    negc = sb("negc", [128, 24])
    hst = sb("hst", [128, 24])
    AFN = 9216
    ABN = 22528
    arF = sb("arenaF", [128, AFN])
    arB = sb("arenaB", [128, ABN], BF16)

    def carve(ar, off, shape):
        n = 1
        for d_ in shape:
            n *= d_
        v = ar[:, off:off + n]
        if len(shape) == 2:
            return v.rearrange("p (a b) -> p a b", a=shape[0])
        if len(shape) == 3:
            return v.rearrange("p (a b c) -> p a b c", a=shape[0], b=shape[1])
        if len(shape) == 4:
            return v.rearrange("p (a b c d) -> p a b c d", a=shape[0], b=shape[1], c=shape[2])
        return v

    Zsb = carve(arF, 0, [NCH, 192])
    Shist = carve(arB, 0, [NCH, 192])
    gT = carve(arB, 6144, [24, TB])
    zmat = [carve(arB, 12288 + i * 2048, [16, 128]) for i in range(2)]
    camat = [carve(arB, 16384 + i * 2048, [2, 8, 4, 32]) for i in range(2)]
    kmat = [carve(arB, 20480 + i * 1024, [8, 128]) for i in range(2)]
    xc = carve(arF, 0, [24, TB])
    a4 = carve(arF, 6144, [4, TB])
    g4 = carve(arF, 7168, [4, TB])
    v4 = carve(arF, 8192, [4, TB])
    xcb = carve(arB, 0, [24, TB])
    gw = [carve(arB, 6144 + i * 1024, [2, 2, 256]) for i in range(2)]

    ps = [nc.alloc_psum_tensor("ps%d" % i, [128, 512], F32) for i in range(8)]
    PS_SS = 0

    state = {"mm": 0, "w": 0, "tmp": 0, "z": 0, "y": 0, "hs": 0}

    def mm_bank():
        state["mm"] = (state["mm"] + 1) % 3
        return 1 + state["mm"]

    def z_bank():
        state["z"] = (state["z"] + 1) % 2
        return 4 + state["z"]

    def y_bank():
        state["y"] = (state["y"] + 1) % 2
        return 6 + state["y"]

    def tmp():
        state["tmp"] = (state["tmp"] + 1) % NTMP
        i = state["tmp"]
        return tmpf[i], R("tmpf", i)

    def hs_slot():
        state["hs"] = (state["hs"] + 1) % NHS
        i = state["hs"]
        return i, hs[i], R("hs", i)

    def V_tt(out, in0, in1, op, r, w):
        kb.op("dve", lambda e: e.tensor_tensor(out=out, in0=in0, in1=in1, op=op), r, w)

    def V_ts(out, in0, s1, s2, op0, op1, r, w):
        if op1 is None:
            kb.op("dve", lambda e: e.tensor_scalar(out=out, in0=in0, scalar1=s1, scalar2=None, op0=op0), r, w)
        else:
            kb.op("dve", lambda e: e.tensor_scalar(out=out, in0=in0, scalar1=s1, scalar2=s2, op0=op0, op1=op1), r, w)

    def V_stt(out, in0, scalar, in1, op0, op1, r, w):
        kb.op("dve", lambda e: e.scalar_tensor_tensor(out=out, in0=in0, scalar=scalar, in1=in1, op0=op0, op1=op1), r, w)

    def V_copy(out, in_, r, w):
        kb.op("dve", lambda e: e.tensor_copy(out=out, in_=in_), r, w)

    def V_recip(out, in_, r, w):
        kb.op("dve", lambda e: e.reciprocal(out=out, in_=in_), r, w)

    def V_memset(out, val, r, w):
        kb.op("dve", lambda e: e.memset(out, val), r, w)

    def A_act(out, in_, func, r, w, bias=None, scale=None):
        kw = {}
        if bias is not None:
            kw["bias"] = bias
        if scale is not None:
            kw["scale"] = scale
        kb.op("act", lambda e: e.activation(out=out, in_=in_, func=func, **kw), r, w)

    def MM(out, lhsT, rhs, start, stop, r, w, tp=None, sync=False):
        if tp is None:
            o = kb.op("pe", lambda e: e.matmul(out, lhsT=lhsT, rhs=rhs, start=start, stop=stop), r, w)
        else:
            o = kb.op("pe", lambda e: e.matmul(out, lhsT=lhsT, rhs=rhs, start=start, stop=stop, tile_position=tp), r, w)
        o.pesync = sync

    def PSR(b):
        return [R("ps", b, 0), R("ps", b, 1)]

    def load_w(kind, l, c, first=True):
        i = state["w"] = (state["w"] + 1) % NWS
        src32 = {"kv": w_kv, "in": w_in, "out": w_out}[kind][l, c] if kind != "glu" else w_glu[l // 2, c]
        n = src32.shape[-1] // 128
        if kind == "kv":
            kb.dma("pool", wb[i][:, 0:n, :], src32.rearrange("p (k c) -> p k c", c=128), writes=[R("wb", i)], chan="wb%d" % i)
            return wb[i], R("wb", i)
        dcopy = {"in": wbf_in, "glu": wbf_glu, "out": wbf_out}[kind][0, c]
        if not WCACHE:
            kb.dma("pool", wb[i][:, 0:n, :], src32.rearrange("p (k c) -> p k c", c=128), writes=[R("wb", i)], chan="wb%d" % i)
        elif first:
            kb.dma("pool", wb[i][:, 0:n, :], src32.rearrange("p (k c) -> p k c", c=128), writes=[R("wb", i)], chan="wb%d" % i)
            kb.dma("sp", dcopy.rearrange("p (k c) -> p k c", c=128), wb[i][:, 0:n, :], reads=[R("wb", i)],
                   writes=[R("wbf", kind, c)], chan="wst%d" % i)
        else:
            kb.dma("pool", wb[i][:, 0:n, :], dcopy.rearrange("p (k c) -> p k c", c=128), reads=[R("wbf", kind, c)],
                   writes=[R("wb", i)], chan="wb%d" % i)
        return wb[i], R("wb", i)

    def rsqrt_mean(dst, rdst, psap, rps):
        V_ts(dst, psap, 1.0 / D, EPS, ALU.mult, ALU.add, rps, [rdst])
        A_act(dst, dst, AF.Sqrt, [rdst], [rdst])
        V_recip(dst, dst, [rdst], [rdst])

    V_memset(ones_bf[:], 1.0, (), [R("ones")])
    V_memset(uq[:], 0.0, (), [R("uq", 0), R("uq", 1)])
    kb.dma("sp", gains[:], gains_d.rearrange("p (a l k) -> p a l k", a=3, l=4), writes=[R("gains")], chan="misc")

    def silu_from_psum(out_bf, psap, rps, wres):
        t, rt = tmp()
        A_act(t[:], psap, AF.Tanh, rps, [rt], scale=0.5)
        V_ts(t[:], t[:], 0.5, 0.5, ALU.mult, ALU.add, [rt], [rt])
        V_tt(out_bf, psap, t[:], ALU.mult, rps + [rt], [wres])

    mem_state = {"rstd": False}

    def phase_kv(l):
        if not mem_state["rstd"]:
            for kt in range(32):
                si, ht, rh = hs_slot()
                kb.dma("sp", ht[:], memT[kt * 128:(kt + 1) * 128, :], writes=[rh], chan="hs%d" % si)
                i = kt % 2
                A_act(sqb[i][:], ht[:], AF.Square, [rh], [R("sqb", i)])
                MM(ps[PS_SS][:, 0:NMEM], ones_bf[:], sqb[i][:], kt == 0, kt == 31, [R("ones"), R("sqb", i)], [*PSR(PS_SS)])
            rsqrt_mean(rstd_m[:], R("rstd_m"), ps[PS_SS][:, 0:NMEM], PSR(PS_SS))
            mem_state["rstd"] = True
        for kt in range(32):
            si, ht, rh = hs_slot()
            kb.dma("sp", ht[:], memT[kt * 128:(kt + 1) * 128, :], writes=[rh], chan="hs%d" % si)
            V_stt(hnT[:, kt, :], ht[:], gains[:, 2, l, kt:kt + 1], rstd_m[:], ALU.mult, ALU.mult,
                  [rh, R("gains"), R("rstd_m")], [R("hnT", kt)])
        for c in range(16):
            wt, rw = load_w("kv", l, c)
            b = mm_bank()
            if c < 8:
                for kt in range(32):
                    MM(ps[b][:, 0:NMEM], wt[:, kt, :], hnT[:, kt, :], kt == 0, kt == 31, [rw, R("hnT", kt)], [*PSR(b)])
                A_act(KT[:, c, :], ps[b][:, 0:NMEM], AF.Copy, [*PSR(b)], [R("KT")])
            else:
                cv = c - 8
                for j in range(2):
                    for kt in range(32):
                        MM(ps[b][:, j * 128:(j + 1) * 128], hnT[:, kt, j * 128:(j + 1) * 128], wt[:, kt, :], kt == 0, kt == 31,
                           [rw, R("hnT", kt)], [*PSR(b)])
                A_act(Vt[:, :, cv * 128:(cv + 1) * 128], ps[b][:, 0:256].rearrange("p (j c) -> p j c", j=2), AF.Copy,
                      [*PSR(b)], [R("Vt")])

    PN = 384
    p2 = [carve(arF, i * PN, [1, PN])[:, 0, :] for i in range(8)]
    pin = arF[:, 8 * PN:11 * PN]
    PSCR = 11 * PN

    def cplx_disc(n, lre, lim, ls):
        ar, ai, cre, cim, t0, t1, t2, t3 = [p2[i][:, 0:n] for i in range(8)]
        rr = [R("prep", i) for i in range(8)]
        rin = [R("prepin")]
        A_act(t0, ls, AF.Exp, rin, [rr[4]])
        V_ts(t1, lre, -1e-4, None, ALU.min, None, rin, [rr[5]])
        V_tt(t2, t1, t0, ALU.mult, [rr[4], rr[5]], [rr[6]])
        A_act(t2, t2, AF.Exp, [rr[6]], [rr[6]])
        V_tt(t3, lim, t0, ALU.mult, rin + [rr[4]], [rr[7]])
        A_act(ai, t3, AF.Sin, [rr[7]], [rr[1]], scale=0.125)
        A_act(ar, t3, AF.Sin, [rr[7]], [rr[0]], scale=-0.125, bias=PI / 2)
        for _ in range(3):
            V_tt(t3, ar, ai, ALU.mult, [rr[0], rr[1]], [rr[7]])
            V_tt(ar, ar, ar, ALU.mult, [rr[0]], [rr[0]])
            V_tt(ai, ai, ai, ALU.mult, [rr[1]], [rr[1]])
            V_tt(ar, ar, ai, ALU.subtract, [rr[0], rr[1]], [rr[0]])
            V_ts(ai, t3, 2.0, None, ALU.mult, None, [rr[7]], [rr[1]])
        V_tt(ar, ar, t2, ALU.mult, [rr[0], rr[6]], [rr[0]])
        V_tt(ai, ai, t2, ALU.mult, [rr[1], rr[6]], [rr[1]])
        V_tt(t0, t1, t1, ALU.mult, [rr[5]], [rr[4]])
        V_tt(t2, lim, lim, ALU.mult, rin, [rr[6]])
        V_tt(t0, t0, t2, ALU.add, [rr[4], rr[6]], [rr[4]])
        V_recip(t0, t0, [rr[4]], [rr[4]])
        V_ts(t2, ar, -1.0, None, ALU.add, None, [rr[0]], [rr[6]])
        V_tt(cre, t2, t1, ALU.mult, [rr[6], rr[5]], [rr[2]])
        V_tt(t3, ai, lim, ALU.mult, [rr[1]] + rin, [rr[7]])
        V_tt(cre, cre, t3, ALU.add, [rr[2], rr[7]], [rr[2]])
        V_tt(cre, cre, t0, ALU.mult, [rr[2], rr[4]], [rr[2]])
        V_tt(cim, ai, t1, ALU.mult, [rr[1], rr[5]], [rr[3]])
        V_tt(t3, t2, lim, ALU.mult, [rr[6]] + rin, [rr[7]])
        V_tt(cim, cim, t3, ALU.subtract, [rr[3], rr[7]], [rr[3]])
        V_tt(cim, cim, t0, ALU.mult, [rr[3], rr[4]], [rr[3]])
        return ar, ai, cre, cim

    def cmul(ore, oim, are, aim, bre, bim, t0, t1, r, w):
        V_tt(t0, are, bre, ALU.mult, r + w, w)
        V_tt(t1, aim, bim, ALU.mult, r + w, w)
        V_tt(t0, t0, t1, ALU.subtract, w, w)
        V_tt(t1, are, bim, ALU.mult, r + w, w)
        V_tt(oim, aim, bre, ALU.mult, r + w, w)
        V_tt(oim, oim, t1, ALU.add, w, w)
        V_copy(ore, t0, w, w)

    def s5_prep(j):
        RX = R("prepbig")
        RB = R("prepbf")
        RK = R("prepK")
        r03 = [R("prep", 0), R("prep", 1), R("prep", 2), R("prep", 3)]
        wp = [R("prep", 4), R("prep", 5), R("prep", 6), R("prep", 7)]
        n2 = 96
        kb.dma("sp", pin[:, 0:3 * n2], s5_l2[j], writes=[R("prepin")], chan="misc")
        ar, ai, cre, cim = cplx_disc(n2, pin[:, 0:n2], pin[:, n2:2 * n2], pin[:, 2 * n2:3 * n2])
        p8r, p8i, t6, t7 = p2[4][:, 0:n2], p2[5][:, 0:n2], p2[6][:, 0:n2], p2[7][:, 0:n2]
        V_copy(p8r, ar, r03, wp)
        V_copy(p8i, ai, r03, wp)
        for _ in range(3):
            cmul(p8r, p8i, p8r, p8i, p8r, p8i, t6, t7, [], wp)
        V_copy(AAB[:, 0, 0:96], p8r, wp, [R("AAB")])
        V_copy(AAB[:, 0, 96:192], p8r, wp, [R("AAB")])
        V_ts(AAB[:, 1, 0:96], p8i, -1.0, None, ALU.mult, None, wp, [R("AAB")])
        V_copy(AAB[:, 1, 96:192], p8i, wp, [R("AAB")])
        NQ = 24
        Bf = carve(arF, PSCR, [2, NQ, 32])
        Cf = carve(arF, PSCR + 1536, [2, NQ, 32])
        T0 = carve(arF, PSCR + 3072, [NQ, 32])
        T1 = carve(arF, PSCR + 3840, [NQ, 32])
        Bb = carve(arB, 0, [2, NQ, 32])
        CAb = carve(arB, 1536, [2, NQ, 32])
        Kst = carve(arB, 3072, [6, 128])
        bl2 = s5_bl2[j].rearrange("p (a b c) -> p a b c", a=2, b=96)
        cl2 = s5_cl2[j].rearrange("p (a b c) -> p a b c", a=2, b=96)
        km_v = kmat_d.rearrange("t p (l c) -> p t l c", l=8)
        for qt in range(4):
            kb.dma("sp", Bf, bl2[:, :, qt * NQ:(qt + 1) * NQ, :], writes=[RX], chan="misc")
            kb.dma("sp", Cf, cl2[:, :, qt * NQ:(qt + 1) * NQ, :], writes=[RX], chan="misc")

            def bc(x):
                return x[:, qt * NQ:(qt + 1) * NQ].unsqueeze(2).to_broadcast([128, NQ, 32])

            cmul(Bf[:, 0], Bf[:, 1], bc(cre), bc(cim), Bf[:, 0], Bf[:, 1], T0, T1, r03, [RX])
            V_copy(Bb[:, 0], Bf[:, 0], [RX], [RB])
            V_copy(Bb[:, 1], Bf[:, 1], [RX], [RB])
            for jj in range(9):
                V_copy(CAb[:, 0], Cf[:, 0], [RX], [RB])
                V_ts(CAb[:, 1], Cf[:, 1], -1.0, None, ALU.mult, None, [RX], [RB])
                if jj >= 1:
                    for ri in range(2):
                        o0 = ((ri * 8 + jj - 1) * 96 + qt * NQ) * 32
                        kb.dma("sp", camat_d[:, o0:o0 + NQ * 32], CAb[:, ri].rearrange("p a c -> p (a c)"), reads=[RB],
                               writes=[R("camat_d")], chan="prepst")
                if jj <= 7:
                    V_memset(Kst, 0.0, [RK], [RK])
                    for tg in range(2):
                        b = z_bank()
                        for tl in range(3):
                            for q in range(4):
                                pr = (tg * 3 + tl) * 4 + q
                                for ri in range(2):
                                    MM(ps[b][32 * q:32 * q + 32, tl * 128 + 32 * q: tl * 128 + 32 * q + 32],
                                       Bb[:, ri, pr, :], CAb[:, ri, pr, :], ri == 0, ri == 1, [RB], [*PSR(b)], tp=(0, 32 * q), sync=False)
                        for q in range(4):
                            src = ps[b][32 * q:32 * q + 32, 0:384].rearrange("p (t c) -> p t c", t=3)[:, :, 32 * q:32 * q + 32]
                            dst = Kst[32 * q:32 * q + 32, tg * 3:tg * 3 + 3, 32 * q:32 * q + 32]
                            A_act(dst, src, AF.Copy, [*PSR(b)], [RK])
                    kb.dma("sp", km_v[:, qt * 6:(qt + 1) * 6, jj, :], Kst, reads=[RK], writes=[R("kmat_d")], chan="prepst")
                    cmul(Cf[:, 0], Cf[:, 1], bc(ar), bc(ai), Cf[:, 0], Cf[:, 1], T0, T1, r03, [RX])
        n1 = 3 * 128
        l1v = s5_l1[j].rearrange("p (a n) -> p a n", a=3)
        b1v = s5_bl1[j].rearrange("p (a n) -> p a n", a=2)
        zm_v = zmat_d.rearrange("t p (k c) -> p t k c", k=16)
        Wr = arF[:, PSCR:PSCR + n1]
        Wi = arF[:, PSCR + n1:PSCR + 2 * n1]
        U0 = arF[:, PSCR + 2 * n1:PSCR + 3 * n1]
        U1 = arF[:, PSCR + 3 * n1:PSCR + 4 * n1]
        Zst = carve(arB, 4096, [3, 16, 128])
        for ch in range(8):
            kb.dma("sp", pin[:, 0:3 * n1].rearrange("p (a n) -> p a n", a=3), l1v[:, :, ch * n1:(ch + 1) * n1],
                   writes=[R("prepin")], chan="misc")
            ar, ai, cre, cim = cplx_disc(n1, pin[:, 0:n1], pin[:, n1:2 * n1], pin[:, 2 * n1:3 * n1])
            kb.dma("sp", arF[:, PSCR:PSCR + 2 * n1].rearrange("p (a n) -> p a n", a=2), b1v[:, :, ch * n1:(ch + 1) * n1],
                   writes=[RX], chan="misc")
            cmul(Wr, Wi, cre, cim, Wr, Wi, U0, U1, r03, [RX])
            for k in range(7, -1, -1):
                V_copy(Zst[:, :, 2 * k, :], Wr.rearrange("p (t c) -> p t c", t=3), [RX], [RB])
                V_copy(Zst[:, :, 2 * k + 1, :], Wi.rearrange("p (t c) -> p t c", t=3), [RX], [RB])
                if k > 0:
                    cmul(Wr, Wi, ar, ai, Wr, Wi, U0, U1, r03, [RX])
            kb.dma("sp", zm_v[:, ch * 3:(ch + 1) * 3], Zst, reads=[RB], writes=[R("zmat_d")], chan="prepst")
        kb.dma("sp", dg[:], s5_dg[j].rearrange("p (a t) -> p a t", a=2), writes=[R("dg")], chan="misc")
        V_ts(hbias[:], dg[:, 1, :], 0.5, None, ALU.mult, None, [R("dg")], [R("hb")])
        V_memset(S[:], 0.0, [R("S")], [R("S")])

    def s5_block(j, blk):
        RZ = R("Zsb")
        RS = R("S")
        RH = R("Shist")
        RT = R("sct")
        for t in range(24):
            i = t % 2
            par = t % 2
            kb.dma("sp", zmat[i], zmat_d[t].rearrange("p (k c) -> p k c", k=16), reads=[R("zmat_d")], writes=[R("zmat", i)],
                   chan="zm%d" % i)
            for q in range(4):
                A_act(uq[32 * q:32 * q + 32, par, q, :], u[32 * q:32 * q + 32, t, 3:3 + TB], AF.Copy, [R("u", t)], [R("uq", par)])
            b = z_bank()
            for q in range(4):
                uv = uq[:, par, q, :].rearrange("p (n k) -> p k n", k=T8)
                for ri in range(2):
                    for k in range(T8):
                        MM(ps[b][:, (ri * 4 + q) * NCH:(ri * 4 + q + 1) * NCH], zmat[i][:, 2 * k + ri, :],
                           uv[:, k, :], k == 0, k == T8 - 1, [R("zmat", i), R("uq", par)], [*PSR(b)])
            dst = Zsb.rearrange("p n (r g) -> p n r g", r=2)[:, :, :, 4 * t:4 * t + 4]
            src = ps[b][:, 0:8 * NCH].rearrange("p (r q n) -> p n r q", r=2, q=4)
            A_act(dst, src, AF.Copy, [*PSR(b)], [RZ])
        for n in range(NCH):
            A_act(Shist[:, n, :], S[:], AF.Copy, [RS], [RH])
            V_tt(sct[:, 0, :], AAB[:, 0, :], S[:], ALU.mult, [RS, R("AAB")], [RT])
            V_tt(sct[:, 1, 0:96], AAB[:, 1, 0:96], S[:, 96:192], ALU.mult, [RS, R("AAB")], [RT])
            V_tt(sct[:, 1, 96:192], AAB[:, 1, 96:192], S[:, 0:96], ALU.mult, [RS, R("AAB")], [RT])
            V_tt(sct[:, 0, :], sct[:, 0, :], sct[:, 1, :], ALU.add, [RT], [RT])
            V_tt(S[:], sct[:, 0, :], Zsb[:, n, :], ALU.add, [RT, RZ], [RS])
        for t in range(24):
            i = t % 2
            kb.dma("sp", camat[i].rearrange("p r k q c -> p (r k) (q c)"),
                   camat_d.rearrange("p (rk pr c) -> p rk pr c", rk=16, pr=96)[:, :, 4 * t:4 * t + 4, :].rearrange("p a q c -> p a (q c)"),
                   reads=[R("camat_d")], writes=[R("camat", i)], chan="cm%d" % i)
            kb.dma("sp", kmat[i], kmat_d[t].rearrange("p (l c) -> p l c", l=8), reads=[R("kmat_d")], writes=[R("kmat", i)],
                   chan="km%d" % i)
            b = y_bank()
            uv = u[:, t, 3:3 + TB].rearrange("p (n k) -> p k n", k=T8)
            for k in range(T8):
                yo = ps[b][:, k * NCH:(k + 1) * NCH]
                for l in range(k + 1):
                    MM(yo, kmat[i][:, l, :], uv[:, k - l, :], l == 0, False, [R("kmat", i), R("u", t)], [*PSR(b)])
                for ri in range(2):
                    for q in range(4):
                        MM(ps[b][32 * q:32 * q + 32, k * NCH:(k + 1) * NCH], camat[i][:, ri, k, q, :],
                           Shist[:, :, ri * 96 + 4 * t + q], False, (ri == 1), [R("camat", i), RH], [*PSR(b)], tp=(0, 32 * q),
                           sync=(YSYNC == 1 or (YSYNC == 2 and q == 0)))
            yt, ry = tmp()
            V_stt(yt[:].rearrange("p (n k) -> p n k", k=T8), u[:, t, 3:3 + TB].rearrange("p (n k) -> p n k", k=T8), dg[:, 0, t:t + 1],
                  ps[b][:, 0:TB].rearrange("p (k n) -> p n k", k=T8), ALU.mult, ALU.add, [R("u", t), R("dg"), *PSR(b)], [ry])
            t2, r2 = tmp()
            V_tt(t2[:], yt[:], yt[:], ALU.mult, [ry], [r2])
            V_ts(t2[:], t2[:], 0.044715, 1.0, ALU.mult, ALU.add, [r2], [r2])
            V_tt(t2[:], t2[:], yt[:], ALU.mult, [r2, ry], [r2])
            A_act(t2[:], t2[:], AF.Tanh, [r2], [r2], scale=math.sqrt(2.0 / PI))
            V_ts(t2[:], t2[:], 0.5, 0.5, ALU.mult, ALU.add, [r2], [r2])
            V_tt(gT[:, t, :], t2[:], yt[:], ALU.mult, [r2, ry], [R("gT", t)])
        for c in range(24):
            wt, rw = load_w("glu", 2 * j, c, blk == 0)
            b = mm_bank()
            for kt in range(24):
                MM(ps[b][:, 0:TB], wt[:, kt, :], gT[:, kt, :], kt == 0, kt == 23, [rw, R("gT", kt)], [*PSR(b)])
            t2, r2 = tmp()
            A_act(t2[:], ps[b][:, 0:TB], AF.Tanh, [*PSR(b), R("hb")], [r2], scale=0.5, bias=hbias[:, c:c + 1])
            V_ts(t2[:], t2[:], 0.5, 0.5, ALU.mult, ALU.add, [r2], [r2])
            V_tt(t2[:], t2[:], gT[:, c, :], ALU.mult, [r2, R("gT", c)], [r2])
            V_tt(mixed[:, c, :], t2[:], gsil[:, c, :], ALU.mult, [r2, R("gsil", c)], [R("mixed", c)])

    def lru_prep(j):
        kb.dma("sp", lv[:], lru_v[j].rearrange("p (t k) -> p t k", k=8), writes=[R("lv")], chan="misc")
        A_act(negc[:], lv[:, :, 7], AF.Exp, [R("lv")], [R("negc")], scale=-1.0)
        A_act(negc[:], negc[:], AF.Ln, [R("negc")], [R("negc")], bias=1.0)
        V_ts(negc[:], negc[:], -8.0, None, ALU.mult, None, [R("negc")], [R("negc")])
        V_ts(lv[:, :, 5:7], lv[:, :, 5:7], 0.5, None, ALU.mult, None, [R("lv")], [R("lv")])
        V_memset(hst[:], 0.0, [R("hst")], [R("hst")])
        V_memset(u[:, :, 0:3], 0.0, [R("utail")], [R("utail")])

    def lru_block(j, blk):
        for t in range(24):
            ru = [R("u", t), R("utail"), R("lv")]
            A_act(xc[:, t, :], u[:, t, 3:3 + TB], AF.Identity, ru, [R("xc", t)], bias=lv[:, t, 4:5], scale=lv[:, t, 3:4])
            for jj in range(3):
                V_stt(xc[:, t, :], u[:, t, jj:jj + TB], lv[:, t, jj:jj + 1], xc[:, t, :], ALU.mult, ALU.add, ru + [R("xc", t)], [R("xc", t)])
            A_act(xcb[:, t, :], xc[:, t, :], AF.Copy, [R("xc", t)], [R("xcb", t)])
        V_copy(u[:, :, 0:3], u[:, :, TB:TB + 3], [R("u", t) for t in range(24)] + [R("xc", t) for t in range(24)], [R("utail")])
        for h2 in range(6):
            R4 = R("lru4")
            for hh in range(2):
                h = 2 * h2 + hh
                i = h % 2
                kb.dma("pool", gw[i], lru_w[j, h].rearrange("p (a i j) -> p a i j", a=2, i=2), writes=[R("gw", i)], chan="gw%d" % i)
                for j2 in range(2):
                    t = 2 * h + j2
                    k4 = 2 * hh + j2
                    b = z_bank()
                    for ax in range(2):
                        for i2 in range(2):
                            MM(ps[b][:, ax * TB:(ax + 1) * TB], gw[i][:, ax, i2, j2 * 128:(j2 + 1) * 128], xcb[:, 2 * h + i2, :],
                               i2 == 0, i2 == 1, [R("gw", i), R("xcb", 2 * h + i2)], [*PSR(b)])
                    r_, rr_ = tmp()
                    A_act(r_[:], ps[b][:, 0:TB], AF.Tanh, [*PSR(b), R("lv")], [rr_], scale=0.5, bias=lv[:, t, 5:6])
                    A_act(g4[:, k4, :], ps[b][:, TB:2 * TB], AF.Tanh, [*PSR(b), R("lv")], [R4], scale=0.5, bias=lv[:, t, 6:7])
                    V_ts(r_[:], r_[:], 0.5, 0.5, ALU.mult, ALU.add, [rr_], [rr_])
                    V_ts(g4[:, k4, :], g4[:, k4, :], 0.5, 0.5, ALU.mult, ALU.add, [R4], [R4])
                    A_act(a4[:, k4, :], r_[:], AF.Exp, [rr_, R("negc")], [R4], scale=negc[:, t:t + 1])
                    V_tt(v4[:, k4, :], a4[:, k4, :], a4[:, k4, :], ALU.mult, [R4], [R4])
                    V_ts(v4[:, k4, :], v4[:, k4, :], -1.0, 1.0, ALU.mult, ALU.add, [R4], [R4])
                    V_tt(g4[:, k4, :], g4[:, k4, :], xc[:, t, :], ALU.mult, [R4, R("xc", t)], [R4])
            A_act(v4, v4, AF.Sqrt, [R4], [R4])
            if blk == 0:
                V_memset(v4[:, :, 0:1], 1.0, [R4], [R4])
            V_tt(g4, g4, v4, ALU.mult, [R4], [R4])
            for k4 in range(4):
                t = 4 * h2 + k4
                kb.op("dve", lambda e, k4=k4, t=t: e.tensor_tensor_scan(out=v4[:, k4, :], data0=a4[:, k4, :], data1=g4[:, k4, :],
                                                                         initial=hst[:, t:t + 1], op0=ALU.mult, op1=ALU.add),
                      [R4, R("hst")], [R4])
                V_copy(hst[:, t:t + 1], v4[:, k4, TB - 1:TB], [R4], [R("hst")])
                V_tt(mixed[:, t, :], v4[:, k4, :], gsil[:, t, :], ALU.mult, [R4, R("gsil", t)], [R("mixed", t)])

    def phase_A(l, li, src, blk):
        tok = slice(blk * TB, (blk + 1) * TB)
        for kt in range(32):
            rsrc = [] if li == 0 else [R("outT", blk, kt)]
            si, ht, rh = hs_slot()
            kb.dma("sp", ht[:], src[kt * 128:(kt + 1) * 128, tok], reads=rsrc, writes=[rh], chan="hs%d" % si)
            i = kt % 2
            A_act(sqb[i][:], ht[:], AF.Square, [rh], [R("sqb", i)])
            MM(ps[PS_SS][:, 0:TB], ones_bf[:], sqb[i][:], kt == 0, kt == 31, [R("ones"), R("sqb", i)], [*PSR(PS_SS)])
        rsqrt_mean(rstdA[:], R("rstdA"), ps[PS_SS][:, 0:TB], PSR(PS_SS))
        for kt in range(32):
            rsrc = [] if li == 0 else [R("outT", blk, kt)]
            si, ht, rh = hs_slot()
            kb.dma("sp", ht[:], src[kt * 128:(kt + 1) * 128, tok], reads=rsrc, writes=[rh], chan="hs%d" % si)
            V_stt(hnT[:, kt, :], ht[:], gains[:, 0, l, kt:kt + 1], rstdA[:], ALU.mult, ALU.mult,
                  [rh, R("gains"), R("rstdA")], [R("hnT", kt)])

    last_stores = []
    V_memset(u[:, :, 0:3], 0.0, (), [R("utail")])
    for li, l in enumerate(layers):
        is_s5 = (l % 2 == 0)
        j = l // 2
        src = xT if li == 0 else outT
        kb.barrier()
        phase_kv(l)
        if is_s5:
            s5_prep(j)
        else:
            lru_prep(j)
        kb.barrier()
        for blk in range(NB):
            tok = slice(blk * TB, (blk + 1) * TB)
            if blk == 0:
                phase_A(l, li, src, blk)
            for c in range(64):
                wt, rw = load_w("in", l, c, blk == 0)
                b = mm_bank()
                for kt in range(32):
                    MM(ps[b][:, 0:TB], wt[:, kt, :], hnT[:, kt, :], kt == 0, kt == 31, [rw, R("hnT", kt)], [*PSR(b)])
                if c < 24:
                    A_act(u[:, c, 3:3 + TB], ps[b][:, 0:TB], AF.Copy, [*PSR(b)], [R("u", c)])
                elif c < 48:
                    silu_from_psum(gsil[:, c - 24, :], ps[b][:, 0:TB], PSR(b), R("gsil", c - 24))
                elif c < 56:
                    A_act(qT[:, c - 48, :], ps[b][:, 0:TB], AF.Copy, [*PSR(b)], [R("qT", c - 48)])
                else:
                    silu_from_psum(gqsil[:, c - 56, :], ps[b][:, 0:TB], PSR(b), R("gqsil", c - 56))
            if is_s5:
                s5_block(j, blk)
            else:
                lru_block(j, blk)
            for hd in range(4):
                for jn in range(2):
                    b = z_bank()
                    for dt_ in range(2):
                        MM(ps[b][:, 0:TB], KT[:, 2 * hd + dt_, jn * 128:(jn + 1) * 128], qT[:, 2 * hd + dt_, :], dt_ == 0, dt_ == 1,
                           [R("KT"), R("qT", 2 * hd + dt_)], [*PSR(b)])
                    A_act(expT[:, jn, :], ps[b][:, 0:TB], AF.Exp, [*PSR(b)], [R("expT", jn)], scale=1.0 / 16.0)
                for jn in range(2):
                    MM(ps[PS_SS][:, 0:TB], ones_bf[:], expT[:, jn, :], jn == 0, jn == 1, [R("ones"), R("expT", jn)], [*PSR(PS_SS)])
                V_recip(rden[:], ps[PS_SS][:, 0:TB], [*PSR(PS_SS)], [R("rden")])
                for dt_ in range(2):
                    b = y_bank()
                    c = 2 * hd + dt_
                    for jn in range(2):
                        MM(ps[b][:, 0:TB], Vt[:, jn, c * 128:(c + 1) * 128], expT[:, jn, :], jn == 0, jn == 1,
                           [R("Vt"), R("expT", jn)], [*PSR(b)])
                    t2, r2 = tmp()
                    V_tt(t2[:], ps[b][:, 0:TB], rden[:], ALU.mult, [*PSR(b), R("rden")], [r2])
                    V_tt(mixed[:, 24 + c, :], t2[:], gqsil[:, c, :], ALU.mult, [r2, R("gqsil", c)], [R("mixed", 24 + c)])
            if blk + 1 < NB:
                phase_A(l, li, src, blk + 1)
            for c in range(32):
                wt, rw = load_w("out", l, c, blk == 0)
                b = mm_bank()
                for kt in range(32):
                    MM(ps[b][:, 0:TB], wt[:, kt, :], mixed[:, kt, :], kt == 0, kt == 31, [rw, R("mixed", kt)], [*PSR(b)])
                i = c % 2
                A_act(osb[i][:], ps[b][:, 0:TB], AF.Copy, [*PSR(b)], [R("osb", i)])
                A_act(sqb[i][:], ps[b][:, 0:TB], AF.Square, [*PSR(b)], [R("sqb", i)])
                MM(ps[PS_SS][:, 0:TB], ones_bf[:], sqb[i][:], c == 0, c == 31, [R("ones"), R("sqb", i)], [*PSR(PS_SS)])
                kb.dma("sp", o_scr[c * 128:(c + 1) * 128, :], osb[i][:], reads=[R("osb", i)], writes=[R("o_scr", c)], chan="ost%d" % i)
            rsqrt_mean(rstd[:], R("rstd"), ps[PS_SS][:, 0:TB], PSR(PS_SS))
            for c in range(32):
                i = c % 2
                rsrc = [] if li == 0 else [R("outT", blk, c)]
                si, ht, rh = hs_slot()
                kb.dma("sp", ht[:], src[c * 128:(c + 1) * 128, tok], reads=rsrc, writes=[rh], chan="hs%d" % si)
                kb.dma("sp", osb[i][:], o_scr[c * 128:(c + 1) * 128, :], reads=[R("o_scr", c)], writes=[R("osb", i)], chan="old%d" % i)
                V_stt(osb[i][:], osb[i][:], gains[:, 1, l, c:c + 1], rstd[:], ALU.mult, ALU.mult, [R("osb", i), R("gains"), R("rstd")],
                      [R("osb", i)])
                V_tt(ht[:], ht[:], osb[i][:], ALU.add, [rh, R("osb", i)], [rh])
                st = kb.dma("sp", outT[c * 128:(c + 1) * 128, tok], ht[:], reads=[rh], writes=[R("outT", blk, c)], chan="hst")
                if li == len(layers) - 1:
                    last_stores.append(st)
    kb.emit(final_waits=last_stores)
    return nc


def _panel(w, ktn):
    K, C = w.shape
    a = w.reshape(ktn, 128, C // 128, 128)
    return np.ascontiguousarray(a.transpose(2, 1, 0, 3)).reshape(C // 128, 128, ktn * 128)


def _chan_major(v):
    n = v.shape[-1] // 128
    a = v.reshape(v.shape[:-1] + (n, 128))
    return np.ascontiguousarray(np.moveaxis(a, -1, 0))


def prep_shared(inp):
    f = np.float32
    sh = {}
    sh["w_in"] = np.stack([_panel(np.asarray(inp["w_in"][l], f), 32) for l in range(4)])
    sh["w_out"] = np.stack([_panel(np.asarray(inp["w_out"][l], f), 32) for l in range(4)])
    sh["w_kv"] = np.stack([_panel(np.asarray(inp["w_kv"][l], f), 32) for l in range(4)])
    sh["w_glu"] = np.stack([_panel(np.asarray(inp["s5_w_glu"][j], f), 24) for j in range(2)])
    g = np.stack([_chan_major(np.asarray(inp[k], f)) for k in ("pre_norm", "post_norm", "mem_norm")], axis=1)
    sh["gains"] = np.ascontiguousarray(g).reshape(128, 3 * 4 * 32)
    lre = np.asarray(inp["s5_lam_re"], f)
    lim = np.asarray(inp["s5_lam_im"], f)
    ls = np.asarray(inp["s5_log_step"], f)
    bre = np.asarray(inp["s5_b_re"], f)
    bim = np.asarray(inp["s5_b_im"], f)
    cre = np.asarray(inp["s5_c_re"], f)
    cim = np.asarray(inp["s5_c_im"], f)

    def l2(v):
        return v.reshape(96, 2, 64).transpose(1, 2, 0).reshape(128, 96)

    def l1(v):
        a = v.reshape(24, 4, 2, 64)
        a = np.broadcast_to(a[:, :, None, :, :], (24, 4, 32, 2, 64))
        return np.ascontiguousarray(a.transpose(1, 2, 0, 3, 4)).reshape(128, 24 * 128)

    s5_l2, s5_bl2, s5_cl2, s5_l1, s5_bl1, s5_dg = [], [], [], [], [], []
    for j in range(2):
        lsb = np.broadcast_to(ls[j][:, None], (192, 64))
        s5_l2.append(np.stack([l2(lre[j]), l2(lim[j]), l2(lsb)], axis=1).reshape(128, 3 * 96))
        s5_l1.append(np.stack([l1(lre[j]), l1(lim[j]), l1(lsb)], axis=1).reshape(128, 3 * 24 * 128))

        def bl2(b):
            o = np.zeros((2, 64, 96, 2, 16), f)
            bb = b.reshape(96, 2, 64, 16)
            for g2 in range(2):
                o[g2, :, :, g2, :] = bb[:, g2].transpose(1, 0, 2)
            return o.reshape(128, 96 * 32)

        def cl2(c):
            return bl2(np.ascontiguousarray(c.transpose(0, 2, 1)))

        def bl1(b):
            o = np.zeros((4, 2, 16, 24, 2, 64), f)
            bb = b.reshape(24, 4, 2, 64, 16)
            for g2 in range(2):
                o[:, g2, :, :, g2, :] = bb[:, :, g2].transpose(1, 3, 0, 2)
            return o.reshape(128, 24 * 128)

        s5_bl2.append(np.stack([bl2(bre[j]), bl2(bim[j])], axis=1).reshape(128, -1))
        s5_cl2.append(np.stack([cl2(cre[j]), cl2(cim[j])], axis=1).reshape(128, -1))
        s5_bl1.append(np.stack([bl1(bre[j]), bl1(bim[j])], axis=1).reshape(128, -1))
        s5_dg.append(np.stack([_chan_major(np.asarray(inp["s5_d"][j], f)), _chan_major(np.asarray(inp["s5_b_glu"][j], f))],
                              axis=1).reshape(128, 48))
    sh["s5_l2"] = np.stack(s5_l2)
    sh["s5_bl2"] = np.stack(s5_bl2)
    sh["s5_cl2"] = np.stack(s5_cl2)
    sh["s5_l1"] = np.stack(s5_l1)
    sh["s5_bl1"] = np.stack(s5_bl1)
    sh["s5_dg"] = np.stack(s5_dg)
    lv, lw = [], []
    for j in range(2):
        cw = np.asarray(inp["lru_conv_w"][j], f)
        cols = [cw[0], cw[1], cw[2], cw[3], np.asarray(inp["lru_conv_b"][j], f),
                np.asarray(inp["lru_b_a"][j], f).reshape(-1), np.asarray(inp["lru_b_x"][j], f).reshape(-1),
                np.asarray(inp["lru_lam"][j], f)]
        v = np.stack([_chan_major(c) for c in cols], axis=2)
        lv.append(v.reshape(128, 24 * 8))
        wa = np.asarray(inp["lru_w_a"][j], f).reshape(12, 2, 128, 256)
        wx = np.asarray(inp["lru_w_x"][j], f).reshape(12, 2, 128, 256)
        w = np.stack([wa, wx], axis=1)
        lw.append(np.ascontiguousarray(w.transpose(0, 3, 1, 2, 4)).reshape(12, 128, 2 * 2 * 256))
    sh["lru_v"] = np.stack(lv)
    sh["lru_w"] = np.stack(lw)
    return {k: np.ascontiguousarray(v, dtype=f) for k, v in sh.items()}


def kernel(**inputs):
    x = np.asarray(inputs["x"], np.float32)
    mem = np.asarray(inputs["mem"], np.float32)
    B, L, _ = x.shape
    sh = prep_shared(inputs)
    nc = bass.Bass("TRN2", target_bir_lowering=False)
    build(nc, L)
    in_maps = []
    for b in range(B):
        m = dict(sh)
        m["xT"] = np.ascontiguousarray(x[b].T)
        m["memT"] = np.ascontiguousarray(mem[b].T)
        in_maps.append(m)
    res = run_bass_kernel_spmd(nc, in_maps, core_ids=list(range(B)))
    out = np.stack([np.ascontiguousarray(res.results[b]["outT"].T) for b in range(B)])
    return out.astype(np.float32)
```
